# Optimizing a Trainium2 kernel written in Bass

```python
import math
import jax, jax.numpy as jnp
from jax import lax
import numpy as np

D_MODEL = 1024
BATCH = 16
SEQ = 2048
DEPTH = 1

HEAD_DIM = 64
N_HEADS_FOX = 8
N_HEADS_DSA = 8
N_IDX_HEADS = 8
IDX_DIM = 64
TOPK_MAX = 256
Q_BLOCK = 128
N_BUCKETS = 32
MAX_DISTANCE = 128
D_FF = 2816
EPS = 1e-6
NEG_INF = -1e30

FOX_W = N_HEADS_FOX * HEAD_DIM
DSA_W = N_HEADS_DSA * HEAD_DIM
IDX_Q_W = N_IDX_HEADS * IDX_DIM
IN_SPLITS = (FOX_W, FOX_W, FOX_W, N_HEADS_FOX,
             DSA_W, HEAD_DIM, HEAD_DIM,
             IDX_Q_W, IDX_DIM, N_IDX_HEADS,
             D_MODEL, D_MODEL)
IN_WIDTH = 3 * FOX_W + N_HEADS_FOX + DSA_W + 2 * HEAD_DIM + IDX_Q_W + IDX_DIM + N_IDX_HEADS + 2 * D_MODEL

kernel_name = "hybrid_fox_dsa_macaron_gated"


def rms_norm(x, g):
    xf = x.astype(jnp.float32)
    y = xf * lax.rsqrt(jnp.mean(xf * xf, axis=-1, keepdims=True) + EPS)
    return (y * g.astype(jnp.float32)).astype(x.dtype)


def swiglu_ffn(x, g, w_gate, w_up, w_down):
    h = rms_norm(x, g)
    return (jax.nn.silu(h @ w_gate) * (h @ w_up)) @ w_down


def t5_causal_bucket(rel):
    max_exact = N_BUCKETS // 2
    relf = jnp.maximum(rel, 1).astype(jnp.float32)
    large = max_exact + (jnp.log(relf / max_exact) / math.log(MAX_DISTANCE / max_exact)
                         * (N_BUCKETS - max_exact)).astype(jnp.int32)
    large = jnp.minimum(large, N_BUCKETS - 1)
    return jnp.where(rel < max_exact, rel, large)


def split_columns(z):
    outs, off = [], 0
    for w in IN_SPLITS:
        outs.append(z[..., off:off + w])
        off += w
    return outs


def fox_attention(q, k, v, log_f):
    S = q.shape[1]
    cum = jnp.cumsum(log_f, axis=1).transpose(0, 2, 1)
    scale = HEAD_DIM ** -0.5
    outs = []
    for start in range(0, S, Q_BLOCK):
        end = start + Q_BLOCK
        logits = jnp.einsum('bqhd,bkhd->bhqk', q[:, start:end], k[:, :end]).astype(jnp.float32) * scale
        logits = logits + cum[:, :, start:end, None] - cum[:, :, None, :end]
        qpos = jnp.arange(start, end)[:, None]
        kpos = jnp.arange(end)[None, :]
        logits = jnp.where(kpos <= qpos, logits, NEG_INF)
        p = jax.nn.softmax(logits, axis=-1).astype(v.dtype)
        outs.append(jnp.einsum('bhqk,bkhd->bqhd', p, v[:, :end]))
    return jnp.concatenate(outs, axis=1)


def dsa_attention(q, k, v, q_idx, k_idx, w_idx, rel_bias):
    S = q.shape[1]
    topk = min(TOPK_MAX, S // 4)
    scale = HEAD_DIM ** -0.5
    idx_scale = IDX_DIM ** -0.5
    w = w_idx.astype(jnp.float32) * (N_IDX_HEADS ** -0.5)
    gather = jax.vmap(lambda arr, ind: arr[ind])
    outs = []
    for start in range(0, S, Q_BLOCK):
        end = start + Q_BLOCK
        n_keys = min(S, max(end, topk))
        qpos = jnp.arange(start, end)
        kpos = jnp.arange(n_keys)
        causal = kpos[None, :] <= qpos[:, None]
        s_idx = jax.nn.relu(jnp.einsum('bqhd,bkd->bqhk', q_idx[:, start:end],
                                       k_idx[:, :n_keys]).astype(jnp.float32) * idx_scale)
        score = jnp.einsum('bqhk,bqh->bqk', s_idx, w[:, start:end])
        score = jnp.where(causal[None], score, NEG_INF)
        _, sel = lax.top_k(score, topk)
        valid = sel <= qpos[None, :, None]
        k_sel = gather(k, sel)
        v_sel = gather(v, sel)
        logits = jnp.einsum('bqhd,bqkd->bhqk', q[:, start:end], k_sel).astype(jnp.float32) * scale
        bucket = t5_causal_bucket(jnp.maximum(qpos[None, :, None] - sel, 0))
        bias = rel_bias.astype(jnp.float32)[bucket]
        logits = logits + bias.transpose(0, 3, 1, 2)
        logits = jnp.where(valid[:, None], logits, NEG_INF)
        p = jax.nn.softmax(logits, axis=-1).astype(v.dtype)
        outs.append(jnp.einsum('bhqk,bqkd->bqhd', p, v_sel))
    return jnp.concatenate(outs, axis=1)


def setup_inputs(seed: int = 0) -> dict:
    key = jax.random.key(seed)
    ks = jax.random.split(key, 24)
    f32 = jnp.float32

    def dense(k, fan_in, fan_out):
        return jax.random.normal(k, (fan_in, fan_out), f32) * fan_in ** -0.5

    def gain(k, n):
        return 1.0 + 0.02 * jax.random.normal(k, (n,), f32)

    return {
        "x": jax.random.normal(ks[0], (BATCH, SEQ, D_MODEL), f32),
        "ffn1_norm": gain(ks[1], D_MODEL),
        "ffn1_w_gate": dense(ks[2], D_MODEL, D_FF),
        "ffn1_w_up": dense(ks[3], D_MODEL, D_FF),
        "ffn1_w_down": dense(ks[4], D_FF, D_MODEL),
        "mix_norm": gain(ks[5], D_MODEL),
        "w_in": dense(ks[6], D_MODEL, IN_WIDTH),
        "b_forget": 1.0 + 0.1 * jax.random.normal(ks[7], (N_HEADS_FOX,), f32),
        "fox_q_norm": gain(ks[8], HEAD_DIM),
        "fox_k_norm": gain(ks[9], HEAD_DIM),
        "dsa_q_norm": gain(ks[10], HEAD_DIM),
        "dsa_k_norm": gain(ks[11], HEAD_DIM),
        "rel_bias": 0.5 * jax.random.normal(ks[12], (N_BUCKETS, N_HEADS_DSA), f32),
        "w_branch_a": dense(ks[13], FOX_W, D_MODEL),
        "w_branch_b": dense(ks[14], DSA_W, D_MODEL),
        "w_out": dense(ks[15], D_MODEL, D_MODEL),
        "ffn2_norm": gain(ks[16], D_MODEL),
        "ffn2_w_gate": dense(ks[17], D_MODEL, D_FF),
        "ffn2_w_up": dense(ks[18], D_MODEL, D_FF),
        "ffn2_w_down": dense(ks[19], D_FF, D_MODEL),
    }


def reference(x, ffn1_norm, ffn1_w_gate, ffn1_w_up, ffn1_w_down, mix_norm, w_in, b_forget,
              fox_q_norm, fox_k_norm, dsa_q_norm, dsa_k_norm, rel_bias,
              w_branch_a, w_branch_b, w_out, ffn2_norm, ffn2_w_gate, ffn2_w_up, ffn2_w_down):
    B, S, _ = x.shape
    for _layer in range(DEPTH):
        x = x + 0.5 * swiglu_ffn(x, ffn1_norm, ffn1_w_gate, ffn1_w_up, ffn1_w_down)

        h = rms_norm(x, mix_norm)
        (fq, fk, fv, ff, dq, dk, dv, iq, ik, iw, ga, gb) = split_columns(h @ w_in)

        fq = rms_norm(fq.reshape(B, S, N_HEADS_FOX, HEAD_DIM), fox_q_norm)
        fk = rms_norm(fk.reshape(B, S, N_HEADS_FOX, HEAD_DIM), fox_k_norm)
        fv = fv.reshape(B, S, N_HEADS_FOX, HEAD_DIM)
        log_f = jax.nn.log_sigmoid((ff + b_forget).astype(jnp.float32))
        o_a = fox_attention(fq, fk, fv, log_f).reshape(B, S, FOX_W)

        dq = rms_norm(dq.reshape(B, S, N_HEADS_DSA, HEAD_DIM), dsa_q_norm)
        dk = rms_norm(dk, dsa_k_norm)
        iq = iq.reshape(B, S, N_IDX_HEADS, IDX_DIM)
        o_b = dsa_attention(dq, dk, dv, iq, ik, iw, rel_bias).reshape(B, S, DSA_W)

        merged = jax.nn.sigmoid(ga) * (o_a @ w_branch_a) + jax.nn.sigmoid(gb) * (o_b @ w_branch_b)
        x = x + merged @ w_out

        x = x + 0.5 * swiglu_ffn(x, ffn2_norm, ffn2_w_gate, ffn2_w_up, ffn2_w_down)
    return x
```

```python
import contextlib
import os
PARTS = os.environ.get('DBG_PARTS', 'qk,fv,small,small2,gates,iw,iwd,dvd').split(',')
LVL = int(os.environ.get('DBG_LVL', '9'))
LVLB = int(os.environ.get('DBG_LVLB', '9'))
import numpy as np
import ml_dtypes
import concourse.bass as bass
import concourse.mybir as mybir
from concourse.bass_utils import run_bass_kernel_spmd

F32 = mybir.dt.float32
BF16 = mybir.dt.bfloat16
F32R = mybir.dt.float32r
AF = mybir.ActivationFunctionType
ALU = mybir.AluOpType
AX = mybir.AxisListType

D = 1024
S = 2048
NSEQ = 2
DFF = 2816
NFC = DFF // 128
TB = 1024
NITER = 16
TOPK = 256
EPS = 1e-6
NEG = -1e30

C_FQ, C_FK, C_DQ, C_IQ, C_FV = 0, 512, 1024, 1536, 2048
C_SM = 2560
C_GA = C_SM + 208
C_GB = C_GA + 1024
NIN = C_GB + 1024

CF_G1, CF_GM, CF_G2 = 0, 8, 16
CF_GQ, CF_GK, CF_GDQ, CF_GDK, CF_BF = 24, 25, 26, 27, 28
CF_RB31 = 29
CF_POW = 37
CF_MNEG = 64
CF_MNEGT = 192
CF_GT = 320
CF_IDF = CF_GT + 8 * 256
NCF = CF_IDF + 128
CB_ID, CB_BLK, CB_ONE, CB_CAU, CB_ONE2 = 0, 128, 256, 320, 448
NCB = 576


class Buf:
    __slots__ = ("name", "w", "rs", "excl")

    def __init__(self, name, excl=False):
        self.name = name
        self.excl = excl
        self.w = None
        self.rs = {}


class Trk:
    def __init__(self, nc, es):
        self.nc = nc
        self.es = es
        self.eng = {"pe": nc.tensor, "act": nc.scalar, "dve": nc.vector, "pool": nc.gpsimd, "sp": nc.sync}
        self.sem = {k: es.enter_context(nc.semaphore("s_" + k)) for k in ("pe", "act", "dve", "pool")}
        self.cnt = {k: 0 for k in self.sem}
        self.waited = {k: {} for k in self.eng}
        self.dsem = {}
        self.dcnt = {}
        self.nwaits = 0

    def _wait(self, e, tok):
        sem, val, key, src, isdma = tok
        if src == e and e == "pe" and not isdma:
            return
        w = self.waited[e]
        if w.get(key, 0) >= val:
            return
        w[key] = val
        self.nwaits += 1
        self.eng[e].wait_ge(sem, val)

    def op(self, e, fn, r=(), w=(), dk=None):
        w = list(w) + [b for b in r if b.excl]
        r = [b for b in r if not b.excl]
        for b in r:
            if b.w is not None:
                self._wait(e, b.w)
        for b in w:
            if b.w is not None:
                self._wait(e, b.w)
            for t in b.rs.values():
                self._wait(e, t)
        ins = fn(self.eng[e])
        if dk is not None:
            if dk not in self.dsem:
                self.dsem[dk] = self.es.enter_context(self.nc.semaphore("d_" + dk))
                self.dcnt[dk] = 0
            self.dcnt[dk] += 16
            ins.then_inc(self.dsem[dk], 16)
            tok = (self.dsem[dk], self.dcnt[dk], "d_" + dk, e, True)
        else:
            self.cnt[e] += 1
            ins.then_inc(self.sem[e], 1)
            tok = (self.sem[e], self.cnt[e], "s_" + e, e, False)
        for b in r:
            b.rs[tok[2]] = tok
        for b in w:
            b.w = tok
            b.rs = {}
        return tok

    def barrier(self):
        toks = [(self.sem[k], self.cnt[k], "s_" + k, k, False) for k in self.sem if self.cnt[k] > 0]
        toks += [(self.dsem[k], self.dcnt[k], "d_" + k, None, True) for k in self.dsem]
        for e in self.eng:
            for t in toks:
                self._wait(e, t)


def build(nc, dbg=False, stage=99):
    es = contextlib.ExitStack()
    tr = Trk(nc, es)
    OUTK = "ExternalOutput" if dbg else "Internal"

    def din(name, shape, dt=F32):
        return nc.dram_tensor(name, shape, dt, kind="ExternalInput").ap()

    def dscr(name, shape, dt, kind=None):
        return nc.dram_tensor(name, shape, dt, kind=kind or "Internal").ap()

    x = din("x", [NSEQ * S, D])
    wsrc = {
        "f1g": din("f1g", [D, DFF]), "f1u": din("f1u", [D, DFF]), "f1d": din("f1d", [DFF, D]),
        "win": din("win", [D, NIN]), "wa": din("wa", [512, D]), "wb": din("wb", [512, D]),
        "wo": din("wo", [D, D]),
        "f2g": din("f2g", [D, DFF]), "f2u": din("f2u", [D, DFF]), "f2d": din("f2d", [DFF, D]),
    }
    cf_d = din("cf", [128, NCF])
    cb_d = din("cb", [128, NCB], BF16)
    out = nc.dram_tensor("out", [NSEQ * S, D], F32, kind="ExternalOutput").ap()

    wb16 = {k: dscr(k + "_b", list(v.shape), BF16) for k, v in wsrc.items()}
    wbuf = {k: [] for k in wsrc}
    castslot = [Buf("cast%d" % i) for i in range(6)]
    ncast = [0]
    x1_d = dscr("x1_d", [NSEQ * S, D], F32, OUTK)
    qT_d = dscr("qT_d", [NSEQ, 512, S], BF16, OUTK)
    kT_d = dscr("kT_d", [NSEQ, 512, S], BF16, OUTK)
    dqT_d = dscr("dqT_d", [NSEQ, 512, S], BF16, OUTK)
    iqT_d = dscr("iqT_d", [NSEQ, 512, S], BF16, OUTK)
    dkT_d = dscr("dkT_d", [NSEQ, 64, S], BF16, OUTK)
    ikT_d = dscr("ikT_d", [NSEQ, 64, S], BF16, OUTK)
    vf_d = dscr("vf_d", [NSEQ, S, 512], BF16, OUTK)
    vd_d = dscr("vd_d", [NSEQ, 128, 16, 64], BF16, OUTK)
    ff_d = dscr("ff_d", [NSEQ, 8, S], F32, OUTK)
    iw_d = dscr("iw_d", [NSEQ, 128, 16, 8], F32, OUTK)
    gT_d = dscr("gT_d", [NSEQ, 2048, S], BF16, OUTK)
    cum_d = dscr("cum_d", [NSEQ, 8, 6, S], BF16, OUTK)
    OK_O = "ExternalInput" if (dbg and stage == 3) else OUTK
    oaT_d = dscr("oaT_d", [NSEQ, 512, S], BF16, OK_O)
    obT_d = dscr("obT_d", [NSEQ, 512, S], BF16, OK_O)

    uid = [0]

    def sb(stack, name, shape, dt):
        uid[0] += 1
        return stack.enter_context(nc.sbuf_tensor("%s_s%d" % (name, uid[0]), shape, dt))

    cf = sb(es, "cf", [128, NCF], F32)
    cb = sb(es, "cb", [128, NCB], BF16)
    nrb = sb(es, "nrb", [128, 8], F32)
    Rt = sb(es, "Rt", [128, 8, 256], BF16)
    b_Rt = Buf("Rt")
    banks = [es.enter_context(nc.psum_tensor("bank%d" % i, [128, 512], F32)) for i in range(8)]
    bbuf = [Buf("bank%d" % i, excl=True) for i in range(8)]
    bstate = {"i": 0}

    def nbank():
        i = bstate["i"]
        bstate["i"] = (i + 1) % 8
        return banks[i], bbuf[i]

    b_cf, b_cb, b_nrb = Buf("cf"), Buf("cb"), Buf("nrb")
    ident = cb[:, CB_ID:CB_ID + 128]
    blkones = cb[:, CB_BLK:CB_BLK + 128]
    ones64 = cb[:, CB_ONE:CB_ONE + 64]
    ones128 = cb[:, CB_ONE2:CB_ONE2 + 128]

    tr.op("sp", lambda q: q.dma_start(out=cf[:], in_=cf_d[:, :]), w=[b_cf], dk="cf")
    tr.op("sp", lambda q: q.dma_start(out=cb[:], in_=cb_d[:, :]), w=[b_cb], dk="cb")
    order = ["f1g", "f1u", "f1d", "win", "wa", "wb", "wo", "f2g", "f2u", "f2d"]
    for k in order:
        src = wsrc[k]
        n = src.shape[0] * src.shape[1]
        rows = n // 2048
        s2 = src.rearrange("a (b c) -> (a b) c", c=2048) if src.shape[1] % 2048 == 0 else None
        if s2 is None:
            s2 = src.rearrange("a b -> (a b)").rearrange("(r c) -> r c", c=2048)
            d2 = wb16[k].rearrange("a b -> (a b)").rearrange("(r c) -> r c", c=2048)
        else:
            d2 = wb16[k].rearrange("a (b c) -> (a b) c", c=2048)
        step = 704 if rows % 704 == 0 else (rows if rows <= 704 else 602)
        assert rows % step == 0, (k, rows, step)
        for r0 in range(0, rows, step):
            cbuf = Buf("wc_%s_%d" % (k, r0))
            wbuf[k].append(cbuf)
            ci = ncast[0] % 6
            ncast[0] += 1
            tr.op("pool", lambda q, r0=r0, s2=s2, d2=d2, step=step: q.dma_start(out=d2[r0:r0 + step, :], in_=s2[r0:r0 + step, :]),
                  w=[cbuf, castslot[ci]], dk="cw%d" % ci)
    tr.op("dve", lambda v: v.tensor_scalar(out=nrb[:], in0=cf[:, CF_RB31:CF_RB31 + 8], scalar1=-1.0, scalar2=None, op0=ALU.mult), r=[b_cf], w=[b_nrb])
    for h in range(8):
        tr.op("act", lambda a, h=h: a.activation(out=Rt[:, h, :], in_=cf[:, CF_GT + h * 256:CF_GT + (h + 1) * 256], func=AF.Exp, bias=nrb[:, h:h + 1]), r=[b_cf, b_nrb], w=[b_Rt])

    class Dense:
        pass

    def alloc_dense(stack):
        d = Dense()
        d.xt = sb(stack, "xt", [128, 8, D], F32)
        d.xnT = sb(stack, "xnT", [128, 8, TB], BF16)
        d.aT = sb(stack, "aT", [128, NFC, TB], BF16)
        d.ringA = [sb(stack, "rA%d" % i, [128, 8, 512], BF16) for i in range(3)]
        d.ringB = [sb(stack, "rB%d" % i, [128, NFC, 512], BF16) for i in range(2)]
        d.sg = [sb(stack, "sg%d" % i, [128, 512], F32) for i in range(3)]
        d.xs = [sb(stack, "xs%d" % i, [128, D], BF16) for i in range(2)]
        d.junk = sb(stack, "junk", [128, D], BF16)
        d.ssq = sb(stack, "ssq", [128, 8], F32)
        d.rstd = sb(stack, "rstd", [128, 8], F32)
        d.b_xtl, d.b_xnT = [Buf("xt%d" % i) for i in range(8)], Buf("xnT")
        d.b_aT = [Buf("aT%d" % i) for i in range(NFC)]
        d.b_rA = [Buf("rA%d" % i) for i in range(3)]
        d.b_rB = [Buf("rB%d" % i) for i in range(2)]
        d.b_sg = [Buf("sg%d" % i) for i in range(3)]
        d.b_xs = [Buf("xs%d" % i) for i in range(2)]
        d.b_junk, d.b_ssq, d.b_rstd = Buf("junk"), Buf("ssq"), Buf("rstd")
        d.iA = 0
        d.iB = 0
        d.isg = 0
        return d

    def ringA_load(d, wkey, parts):
        i = d.iA % 3
        d.iA += 1
        t, b = d.ringA[i], d.b_rA[i]
        for (dc, sc, ncol, kch) in parts:
            srcv = wb16[wkey][0:kch * 128, sc:sc + ncol].rearrange("(k p) c -> p k c", p=128)
            tr.op("sp", lambda q, t=t, dc=dc, ncol=ncol, kch=kch, srcv=srcv: q.dma_start(out=t[:, 0:kch, dc:dc + ncol], in_=srcv),
                  r=wbuf[wkey], w=[b], dk="rA%d" % i)
        return t, b

    def ringB_load(d, wkey, c0):
        i = d.iB % 2
        d.iB += 1
        t, b = d.ringB[i], d.b_rB[i]
        srcv = wb16[wkey][:, c0:c0 + 512].rearrange("(k p) c -> p k c", p=128)
        tr.op("sp", lambda q: q.dma_start(out=t[:, :, :], in_=srcv), r=wbuf[wkey], w=[b], dk="rB%d" % i)
        return t, b

    def mm(outap, pairs, r, w):
        def fn(pe):
            ins = None
            n = len(pairs)
            for i, (l, rh) in enumerate(pairs):
                ins = pe.matmul(outap, l, rh, start=(i == 0), stop=(i == n - 1))
            return ins
        return tr.op("pe", fn, r=r, w=w)

    def pipeline(units, stages, skew=1):
        n = len(units)
        for s_ in range(n + (len(stages) - 1) * skew):
            for i, fn in enumerate(stages):
                u = s_ - i * skew
                if 0 <= u < n:
                    fn(units[u])

    def norm_transpose(d, gj):
        for i in range(8):
            tr.op("act", lambda a, i=i: a.activation(out=d.junk[:], in_=d.xt[:, i, :], func=AF.Square, accum_out=d.ssq[:, i:i + 1]),
                  r=[d.b_xtl[i]], w=[d.b_junk, d.b_ssq])
        tr.op("act", lambda a: a.activation(out=d.rstd[:], in_=d.ssq[:], func=AF.Sqrt, bias=EPS, scale=1.0 / D), r=[d.b_ssq], w=[d.b_rstd])
        tr.op("dve", lambda v: v.reciprocal(out=d.rstd[:], in_=d.rstd[:]), r=[d.b_rstd], w=[d.b_rstd])
        for i in range(8):
            xs, bxs = d.xs[i % 2], d.b_xs[i % 2]
            tr.op("act", lambda a, i=i, xs=xs: a.activation(out=xs[:], in_=d.xt[:, i, :], func=AF.Copy, scale=d.rstd[:, i:i + 1]),
                  r=[d.b_xtl[i], d.b_rstd], w=[bxs])
            bk, bb = nbank()
            pt = bk[:].bitcast(BF16)

            def fn(pe, xs=xs, pt=pt):
                ins = None
                for kc in range(8):
                    ins = pe.transpose(out=pt[:, kc * 128:(kc + 1) * 128], in_=xs[:, kc * 128:(kc + 1) * 128], identity=ident)
                return ins
            tr.op("pe", fn, r=[bxs, b_cb], w=[bb])
            def fev(v, i=i, pt=pt):
                ins = None
                for kc in range(8):
                    ins = v.tensor_scalar(out=d.xnT[:, kc, i * 128:(i + 1) * 128], in0=pt[:, kc * 128:(kc + 1) * 128], scalar1=cf[:, gj * 8 + kc:gj * 8 + kc + 1], scalar2=None, op0=ALU.mult)
                return ins
            tr.op("dve", fev, r=[bb, b_cf], w=[d.b_xnT])

    def ffn(d, wg, wu, wd):
        for s in range(NFC // 2):
            t, b = ringA_load(d, wg, [(0, s * 256, 256, 8)])
            i = (d.iA - 1) % 3
            srcv = wb16[wu][:, s * 256:(s + 1) * 256].rearrange("(k p) c -> p k c", p=128)
            tr.op("sp", lambda q, t=t, srcv=srcv: q.dma_start(out=t[:, :, 256:512], in_=srcv), r=wbuf[wu], w=[b], dk="rA%d" % i)
            for f in range(2):
                fc = s * 2 + f
                for sub in range(2):
                    tok = slice(sub * 512, (sub + 1) * 512)
                    bg, bbg = nbank()
                    bu, bbu = nbank()
                    mm(bg[:, :], [(t[:, kc, f * 128:(f + 1) * 128], d.xnT[:, kc, tok]) for kc in range(8)], r=[b, d.b_xnT], w=[bbg])
                    mm(bu[:, :], [(t[:, kc, 256 + f * 128:256 + (f + 1) * 128], d.xnT[:, kc, tok]) for kc in range(8)], r=[b, d.b_xnT], w=[bbu])
                    j = d.isg % 3
                    d.isg += 1
                    sg, bsg = d.sg[j], d.b_sg[j]
                    tr.op("act", lambda a, sg=sg, bg=bg: a.activation(out=sg[:], in_=bg[:, :], func=AF.Silu), r=[bbg], w=[bsg])
                    tr.op("dve", lambda v, sg=sg, bu=bu, fc=fc, tok=tok: v.tensor_tensor(out=d.aT[:, fc, tok], in0=sg[:], in1=bu[:, :], op=ALU.mult),
                          r=[bsg, bbu], w=[d.b_aT[fc]])
        for half in range(2):
            t, b = ringB_load(d, wd, half * 512)
            for i in range(8):
                bk, bb = nbank()
                mm(bk[:, :], [(d.aT[:, kc, i * 128:(i + 1) * 128], t[:, kc, :]) for kc in range(NFC)], r=[b] + d.b_aT, w=[bb])
                tr.op("dve", lambda v, i=i, bk=bk, half=half: v.scalar_tensor_tensor(out=d.xt[:, i, half * 512:(half + 1) * 512], in0=bk[:, :], scalar=0.5, in1=d.xt[:, i, half * 512:(half + 1) * 512], op0=ALU.mult, op1=ALU.add),
                      r=[bb, d.b_xtl[i]], w=[d.b_xtl[i]])

    def phase_A(d, st, blk):
        seq = blk // 2
        t0 = (blk % 2) * TB
        g0 = blk * TB
        for i in range(8):
            tr.op("sp", lambda q, i=i: q.dma_start(out=d.xt[:, i, :], in_=x[g0 + i * 128:g0 + (i + 1) * 128, :]), w=[d.b_xtl[i]], dk="xt%d" % i)
        norm_transpose(d, 0)
        ffn(d, "f1g", "f1u", "f1d")
        for i in range(8):
            tr.op("sp", lambda q, i=i: q.dma_start(out=x1_d[g0 + i * 128:g0 + (i + 1) * 128, :], in_=d.xt[:, i, :]), r=[d.b_xtl[i]], dk="x1st%d" % i)
        if stage < 2:
            return
        slot_parts = [[(0, C_FQ, 512, 8)], [(0, C_FK, 512, 8)], [(0, C_DQ, 512, 8)], [(0, C_IQ, 512, 8)], [(0, C_FV, 512, 8)],
                      [(0, C_SM, 136, 8), (256, C_SM + 136, 72, 8)]] + [[(0, C_GA + gs * 512, 512, 8)] for gs in range(4)]
        loaded = {}

        def get_slot(i):
            for k in range(i, min(i + 3, len(slot_parts))):
                if k not in loaded:
                    loaded[k] = ringA_load(d, "win", slot_parts[k])
            return loaded[i]
        get_slot(0)
        norm_transpose(d, 1)
        ist = [0]

        def stage_buf():
            j = ist[0] % 3
            ist[0] += 1
            return st.stg[j], st.b_stg[j], j

        groups = (
            (C_FQ, qT_d, CF_GQ, 1.0, 64.0 * EPS, True),
            (C_FK, kT_d, CF_GK, 1.0 / 64, EPS, True),
            (C_DQ, dqT_d, CF_GDQ, 1.0, 64.0 * EPS, True),
            (C_IQ, iqT_d, None, None, None, False))
        units = []
        for gi, grp in enumerate(groups):
            for c in range(4):
                for sub in range(2):
                    units.append(dict(gi=gi, grp=grp, c=c, sub=sub))
        gstate = {}

        def p1(u):
            (c0, dst, gcol, sq_scale, sq_bias, donorm) = u["grp"]
            c, sub = u["c"], u["sub"]
            if c == 0 and sub == 0:
                gstate[u["gi"]] = get_slot(u["gi"])
            t, b = gstate[u["gi"]]
            if sub == 0:
                gstate[(u["gi"], c)] = stage_buf()
            sg_t, sg_b, j = gstate[(u["gi"], c)]
            tok = slice(sub * 512, (sub + 1) * 512)
            bk, bb = nbank()
            u["bk"], u["bb"] = bk, bb
            mm(bk[:, :], [(t[:, kc, c * 128:(c + 1) * 128], d.xnT[:, kc, tok]) for kc in range(8)], r=[b, d.b_xnT], w=[bb])
            if donorm:
                jj = d.isg % 3
                d.isg += 1
                u["jj"] = jj
                tr.op("act", lambda a: a.activation(out=st.sq[jj][:], in_=bk[:, :], func=AF.Square), r=[bb], w=[st.b_sq[jj]])
            else:
                tr.op("act", lambda a: a.activation(out=sg_t[:, tok], in_=bk[:, :], func=AF.Copy), r=[bb], w=[sg_b])
                if sub == 1:
                    tr.op("sp", lambda q: q.dma_start(out=dst[seq, c * 128:(c + 1) * 128, t0:t0 + TB], in_=sg_t[:, :]), r=[sg_b], dk="stg%d" % j)

        def p2(u):
            (c0, dst, gcol, sq_scale, sq_bias, donorm) = u["grp"]
            if not donorm:
                return
            jj = u["jj"]
            b2, bb2 = nbank()
            mm(b2[:, :], [(blkones, st.sq[jj][:])], r=[st.b_sq[jj], b_cb], w=[bb2])
            rs, brs = d.sg[jj], d.b_sg[jj]
            tr.op("act", lambda a: a.activation(out=rs[:], in_=b2[:, :], func=AF.Ln, bias=sq_bias, scale=sq_scale), r=[bb2], w=[brs])
            tr.op("act", lambda a: a.activation(out=rs[:], in_=rs[:], func=AF.Exp, scale=-0.5), w=[brs])

        def p3(u):
            (c0, dst, gcol, sq_scale, sq_bias, donorm) = u["grp"]
            if not donorm:
                return
            c, sub, jj, bk, bb = u["c"], u["sub"], u["jj"], u["bk"], u["bb"]
            sg_t, sg_b, j = gstate[(u["gi"], c)]
            tok = slice(sub * 512, (sub + 1) * 512)
            rs, brs = d.sg[jj], d.b_sg[jj]
            tr.op("dve", lambda v: v.scalar_tensor_tensor(out=sg_t[:, tok], in0=bk[:, :], scalar=cf[:, gcol:gcol + 1], in1=rs[:], op0=ALU.mult, op1=ALU.mult),
                  r=[bb, brs, b_cf], w=[sg_b])
            if sub == 1:
                tr.op("sp", lambda q: q.dma_start(out=dst[seq, c * 128:(c + 1) * 128, t0:t0 + TB], in_=sg_t[:, :]), r=[sg_b], dk="stg%d" % j)

        pipeline(units, [p1, p2, p3], skew=1)
        if LVL < 2:
            return
        t, b = get_slot(4)
        for i in range(8 if 'fv' in PARTS else 0):
            bk, bb = nbank()
            mm(bk[:, :], [(d.xnT[:, kc, i * 128:(i + 1) * 128], t[:, kc, :]) for kc in range(8)], r=[b, d.b_xnT], w=[bb])
            tr.op("act", lambda a, bk=bk, i=i: a.activation(out=st.vst[:, i, :], in_=bk[:, :], func=AF.Copy), r=[bb], w=[st.b_vst])
        if 'fv' in PARTS:
          tr.op("sp", lambda q: q.dma_start(out=vf_d[seq, t0:t0 + TB, :].rearrange("(i p) c -> p i c", p=128), in_=st.vst[:, :, :]), r=[st.b_vst], dk="vst")
        if LVL < 3:
            return
        t, b = get_slot(5)
        sg_dk, b_dk, jdk = stage_buf()
        sg_ik, b_ik, jik = stage_buf()
        for sub in range(2):
            tok = slice(sub * 512, (sub + 1) * 512)
            bk, bb = nbank()
            mm(bk[0:64, :], [(t[:, kc, 0:64], d.xnT[:, kc, tok]) for kc in range(8)], r=[b, d.b_xnT], w=[bb])
            jj = d.isg % 3
            d.isg += 1
            sq, bsq = st.sq[jj], st.b_sq[jj]
            tr.op("act", lambda a, sq=sq, bk=bk: a.activation(out=sq[0:64, :], in_=bk[0:64, :], func=AF.Square), r=[bb], w=[bsq])
            b2, bb2 = nbank()
            mm(b2[0:64, :], [(blkones[0:64, 0:64], sq[0:64, :])], r=[bsq, b_cb], w=[bb2])
            rs, brs = d.sg[jj], d.b_sg[jj]
            tr.op("act", lambda a, rs=rs, b2=b2: a.activation(out=rs[0:64, :], in_=b2[0:64, :], func=AF.Ln, bias=EPS, scale=1.0 / 64), r=[bb2], w=[brs])
            tr.op("act", lambda a, rs=rs: a.activation(out=rs[0:64, :], in_=rs[0:64, :], func=AF.Exp, scale=-0.5), w=[brs])
            tr.op("dve", lambda v, rs=rs, bk=bk, tok=tok: v.scalar_tensor_tensor(out=sg_dk[0:64, tok], in0=bk[0:64, :], scalar=cf[0:64, CF_GDK:CF_GDK + 1], in1=rs[0:64, :], op0=ALU.mult, op1=ALU.mult),
                  r=[bb, brs, b_cf], w=[b_dk])
            bk, bb = nbank()
            mm(bk[0:64, :], [(t[:, kc, 64:128], d.xnT[:, kc, tok]) for kc in range(8)], r=[b, d.b_xnT], w=[bb])
            tr.op("act", lambda a, bk=bk, tok=tok: a.activation(out=sg_ik[0:64, tok], in_=bk[0:64, :], func=AF.Copy), r=[bb], w=[b_ik])
            bk, bb = nbank()
            mm(bk[0:8, :], [(t[:, kc, 128:136], d.xnT[:, kc, tok]) for kc in range(8)], r=[b, d.b_xnT], w=[bb])
            jf = d.isg % 3
            d.isg += 1
            fst, bfst = d.sg[jf], d.b_sg[jf]
            tr.op("act", lambda a, bk=bk, fst=fst: a.activation(out=fst[0:8, :], in_=bk[0:8, :], func=AF.Copy), r=[bb], w=[bfst])
            tr.op("sp", lambda q, fst=fst, sub=sub: q.dma_start(out=ff_d[seq, :, t0 + sub * 512:t0 + (sub + 1) * 512], in_=fst[0:8, :]), r=[bfst], dk="sgst%d" % jf)
        tr.op("sp", lambda q: q.dma_start(out=dkT_d[seq, :, t0:t0 + TB], in_=sg_dk[0:64, :]), r=[b_dk], dk="stg%d" % jdk)
        tr.op("sp", lambda q: q.dma_start(out=ikT_d[seq, :, t0:t0 + TB], in_=sg_ik[0:64, :]), r=[b_ik], dk="stg%d" % jik)
        if LVL < 4:
            return
        for i in range(8):
            bk, bb = nbank()
            mm(bk[:, 0:128], [(d.xnT[:, kc, i * 128:(i + 1) * 128], t[:, kc, 256:384]) for kc in range(8)], r=[b, d.b_xnT], w=[bb])
            tr.op("act", lambda a, bk=bk, i=i: a.activation(out=st.dvst[:, i, :], in_=bk[:, 0:64], func=AF.Copy), r=[bb], w=[st.b_dvst])
            if 'iw' in PARTS:
                tr.op("dve", lambda v, bk=bk, i=i: v.tensor_copy(out=st.iwst[:, i, :], in_=bk[:, 64:72]), r=[bb], w=[st.b_iwst])
        if 'dvd' in PARTS:
          tr.op("sp", lambda q: q.dma_start(out=vd_d[seq, :, t0 // 128:t0 // 128 + 8, :], in_=st.dvst[:, :, :]), r=[st.b_dvst], dk="dvst")
        if 'iwd' in PARTS:
          tr.op("sp", lambda q: q.dma_start(out=iw_d[seq, :, t0 // 128:t0 // 128 + 8, :], in_=st.iwst[:, :, :]), r=[st.b_iwst], dk="iwst")
        if LVL < 5:
            return
        for gs in range(4):
            t, b = get_slot(6 + gs)
            for c in range(4):
                sg_t, sg_b, j = stage_buf()
                for sub in range(2):
                    tok = slice(sub * 512, (sub + 1) * 512)
                    bk, bb = nbank()
                    mm(bk[:, :], [(t[:, kc, c * 128:(c + 1) * 128], d.xnT[:, kc, tok]) for kc in range(8)], r=[b, d.b_xnT], w=[bb])
                    tr.op("act", lambda a, bk=bk, sg_t=sg_t, tok=tok: a.activation(out=sg_t[:, tok], in_=bk[:, :], func=AF.Sigmoid), r=[bb], w=[sg_b])
                row = gs * 512 + c * 128
                tr.op("sp", lambda q, row=row, sg_t=sg_t: q.dma_start(out=gT_d[seq, row:row + 128, t0:t0 + TB], in_=sg_t[:, :]), r=[sg_b], dk="stg%d" % j)

    class Stg:
        pass

    def alloc_stage(stack):
        st = Stg()
        st.stg = [sb(stack, "stg%d" % i, [128, TB], BF16) for i in range(3)]
        st.b_stg = [Buf("stg%d" % i) for i in range(3)]
        st.sq = [sb(stack, "sq%d" % i, [128, 512], BF16) for i in range(3)]
        st.b_sq = [Buf("sq%d" % i) for i in range(3)]
        st.vst = sb(stack, "vst", [128, 8, 512], BF16)
        st.b_vst = Buf("vst")
        st.dvst = sb(stack, "dvst", [128, 8, 64], BF16)
        st.b_dvst = Buf("dvst")
        st.iwst = sb(stack, "iwst", [128, 8, 8], F32)
        st.b_iwst = Buf("iwst")
        return st

    def phase_C(d, st, blk):
        seq = blk // 2
        t0 = (blk % 2) * TB
        g0 = blk * TB
        for i in range(8):
            tr.op("sp", lambda q, i=i: q.dma_start(out=d.xt[:, i, :], in_=x1_d[g0 + i * 128:g0 + (i + 1) * 128, :]), w=[d.b_xtl[i]], dk="xt%d" % i)
        tr.op("sp", lambda q: q.dma_start(out=st.oa[:, :, :], in_=oaT_d[seq, :, t0:t0 + TB].rearrange("(k p) t -> p k t", p=128)), w=[st.b_oa], dk="oa")
        tr.op("sp", lambda q: q.dma_start(out=st.ob[:, :, :], in_=obT_d[seq, :, t0:t0 + TB].rearrange("(k p) t -> p k t", p=128)), w=[st.b_ob], dk="ob")
        for hb in range(2):
            tr.op("sp", lambda q, hb=hb: q.dma_start(out=d.aT[:, hb * 8:(hb + 1) * 8, :], in_=gT_d[seq, hb * 1024:(hb + 1) * 1024, t0:t0 + TB].rearrange("(k p) t -> p k t", p=128)), w=d.b_aT[hb * 8:(hb + 1) * 8], dk="gt%d" % hb)
        units = [dict(half=half, c=c, sub=sub) for half in range(2) for c in range(4) for sub in range(2)]
        wst = {}

        def m1(u):
            half, c, sub = u["half"], u["c"], u["sub"]
            if c == 0 and sub == 0:
                wst[half] = (ringA_load(d, "wa", [(0, half * 512, 512, 4)]), ringA_load(d, "wb", [(0, half * 512, 512, 4)]))
            (ta, ba), (tb_, bb_) = wst[half]
            tok = slice(sub * 512, (sub + 1) * 512)
            bk1, bb1 = nbank()
            bk2, bb2 = nbank()
            u["b"] = (bk1, bb1, bk2, bb2)
            mm(bk1[:, :], [(ta[:, kc, c * 128:(c + 1) * 128], st.oa[:, kc, tok]) for kc in range(4)], r=[ba, st.b_oa], w=[bb1])
            mm(bk2[:, :], [(tb_[:, kc, c * 128:(c + 1) * 128], st.ob[:, kc, tok]) for kc in range(4)], r=[bb_, st.b_ob], w=[bb2])

        def m2(u):
            half, c, sub = u["half"], u["c"], u["sub"]
            ch = half * 4 + c
            tok = slice(sub * 512, (sub + 1) * 512)
            bk1, bb1, bk2, bb2 = u["b"]
            jj = d.isg % 3
            d.isg += 1
            u["jj"] = jj
            tmp, btmp = d.sg[jj], d.b_sg[jj]
            tr.op("dve", lambda v: v.tensor_tensor(out=tmp[:], in0=bk1[:, :], in1=d.aT[:, ch, tok], op=ALU.mult), r=[bb1, d.b_aT[ch]], w=[btmp])
            tr.op("dve", lambda v: v.tensor_tensor(out=bk2[:, :], in0=bk2[:, :], in1=d.aT[:, 8 + ch, tok], op=ALU.mult), r=[d.b_aT[8 + ch]], w=[bb2])

        def m3(u):
            half, c, sub = u["half"], u["c"], u["sub"]
            ch = half * 4 + c
            tok = slice(sub * 512, (sub + 1) * 512)
            bk1, bb1, bk2, bb2 = u["b"]
            tmp, btmp = d.sg[u["jj"]], d.b_sg[u["jj"]]
            tr.op("dve", lambda v: v.tensor_tensor(out=d.xnT[:, ch, tok], in0=bk2[:, :], in1=tmp[:], op=ALU.add), r=[bb2, btmp], w=[d.b_xnT])

        pipeline(units, [m1, m2, m3], skew=1)
        for half in range(2):
            t, b = ringA_load(d, "wo", [(0, half * 512, 512, 8)])
            for i in range(8):
                bk, bb = nbank()
                mm(bk[:, :], [(d.xnT[:, kc, i * 128:(i + 1) * 128], t[:, kc, :]) for kc in range(8)], r=[b, d.b_xnT], w=[bb])
                tr.op("dve", lambda v, i=i, bk=bk, half=half: v.tensor_tensor(out=d.xt[:, i, half * 512:(half + 1) * 512], in0=bk[:, :], in1=d.xt[:, i, half * 512:(half + 1) * 512], op=ALU.add),
                      r=[bb, d.b_xtl[i]], w=[d.b_xtl[i]])
        norm_transpose(d, 2)
        ffn(d, "f2g", "f2u", "f2d")
        for i in range(8):
            tr.op("sp", lambda q, i=i: q.dma_start(out=out[g0 + i * 128:g0 + (i + 1) * 128, :], in_=d.xt[:, i, :]), r=[d.b_xtl[i]], dk="outst%d" % i)

    class StgC:
        pass

    def alloc_stage_C(stack):
        st = StgC()
        st.oa = sb(stack, "oa", [128, 4, TB], BF16)
        st.ob = sb(stack, "ob", [128, 4, TB], BF16)
        st.b_oa, st.b_ob = Buf("oa"), Buf("ob")
        return st

    def run_fn(e, fn, r, w):
        return tr.op(e, fn, r=r, w=w)

    def phase_B_fox(seq):
        with contextlib.ExitStack() as ph:
            with contextlib.ExitStack() as p0:
                ffl = sb(p0, "ffl", [8, S], F32)
                t1 = sb(p0, "fft1", [8, S], F32)
                t2 = sb(p0, "fft2", [8, S], F32)
                onesr = sb(p0, "onesr", [8, S], F32)
                parts = [sb(p0, "cpart%d" % i, [8, S], BF16) for i in range(6)]
                b_ffl, b_t1, b_t2, b_on = Buf("ffl"), Buf("t1"), Buf("t2"), Buf("onesr")
                b_parts = [Buf("cpart%d" % i) for i in range(6)]
                tr.op("sp", lambda q: q.dma_start(out=ffl[:], in_=ff_d[seq, :, :]), w=[b_ffl], dk="ffl")
                tr.op("dve", lambda v: v.memset(onesr[:], 1.0), w=[b_on])
                tr.op("dve", lambda v: v.tensor_scalar(out=ffl[:], in0=ffl[:], scalar1=cf[0:8, CF_BF:CF_BF + 1], scalar2=None, op0=ALU.add), r=[b_cf], w=[b_ffl])
                tr.op("dve", lambda v: v.tensor_scalar(out=t1[:], in0=ffl[:], scalar1=-1.0, scalar2=None, op0=ALU.mult), r=[b_ffl], w=[b_t1])
                tr.op("dve", lambda v: v.tensor_tensor(out=t1[:], in0=ffl[:], in1=t1[:], op=ALU.min), r=[b_ffl], w=[b_t1])
                tr.op("act", lambda a: a.activation(out=t1[:], in_=t1[:], func=AF.Exp), w=[b_t1])
                tr.op("act", lambda a: a.activation(out=t1[:], in_=t1[:], func=AF.Ln, bias=1.0), w=[b_t1])
                tr.op("dve", lambda v: v.scalar_tensor_tensor(out=t2[:], in0=ffl[:], scalar=0.0, in1=t1[:], op0=ALU.min, op1=ALU.subtract), r=[b_ffl, b_t1], w=[b_t2])
                tr.op("dve", lambda v: v.tensor_tensor_scan(out=ffl[:], data0=onesr[:], data1=t2[:], initial=0.0, op0=ALU.mult, op1=ALU.add), r=[b_on, b_t2], w=[b_ffl])
                tr.op("dve", lambda v: v.tensor_copy(out=parts[0][:], in_=ffl[:]), r=[b_ffl], w=[b_parts[0]])
                tr.op("dve", lambda v: v.tensor_tensor(out=t1[:], in0=ffl[:], in1=parts[0][:], op=ALU.subtract), r=[b_ffl, b_parts[0]], w=[b_t1])
                tr.op("dve", lambda v: v.tensor_copy(out=parts[1][:], in_=t1[:]), r=[b_t1], w=[b_parts[1]])
                tr.op("dve", lambda v: v.tensor_tensor(out=t2[:], in0=t1[:], in1=parts[1][:], op=ALU.subtract), r=[b_t1, b_parts[1]], w=[b_t2])
                tr.op("dve", lambda v: v.tensor_copy(out=parts[2][:], in_=t2[:]), r=[b_t2], w=[b_parts[2]])
                for i in range(3):
                    tr.op("dve", lambda v, i=i: v.tensor_scalar(out=parts[3 + i][:], in0=parts[i][:], scalar1=-1.0, scalar2=None, op0=ALU.mult), r=[b_parts[i]], w=[b_parts[3 + i]])
                b_cumd = Buf("cumd")
                for i in range(6):
                    tr.op("sp", lambda q, i=i: q.dma_start(out=cum_d[seq, :, i, :], in_=parts[i][:]), r=[b_parts[i]], w=[b_cumd], dk="cumst")
                tr.barrier()
            qA = sb(ph, "qA", [128, 8, S], BF16)
            kA = sb(ph, "kA", [128, 8, S], BF16)
            Vf = sb(ph, "Vf", [128, 16, 512], BF16)
            NPT = 4
            PT = [sb(ph, "PT%d" % i, [128, 512], BF16) for i in range(NPT)]
            dtmp = [sb(ph, "dtmp%d" % i, [128, 128], F32) for i in range(2)]
            rec = [sb(ph, "rec%d" % i, [128, 512], F32) for i in range(2)]
            ost = [sb(ph, "ost%d" % i, [128, 512], BF16) for i in range(2)]
            b_qA, b_kA, b_Vf = Buf("qA"), Buf("kA"), Buf("Vf")
            b_PT = [Buf("PT%d" % i) for i in range(NPT)]
            b_dtmp = [Buf("dtmp%d" % i) for i in range(2)]
            b_rec = [Buf("rec%d" % i) for i in range(2)]
            b_ost = [Buf("ost%d" % i) for i in range(2)]
            tr.op("pool", lambda g: g.memset(qA[64:70, :, :], 1.0), w=[b_qA])
            tr.op("pool", lambda g: g.memset(kA[64:70, :, :], 1.0), w=[b_kA])
            tr.op("sp", lambda q: q.dma_start(out=qA[0:64, :, :], in_=qT_d[seq, :, :].rearrange("(h d) s -> d h s", d=64)), w=[b_qA], dk="qA")
            tr.op("sp", lambda q: q.dma_start(out=kA[0:64, :, :], in_=kT_d[seq, :, :].rearrange("(h d) s -> d h s", d=64)), w=[b_kA], dk="kA")
            tr.op("sp", lambda q: q.dma_start(out=qA[64:67, :, :], in_=cum_d[seq, :, 0:3, :].rearrange("h j s -> j h s")), r=[b_cumd], w=[b_qA], dk="qA")
            tr.op("sp", lambda q: q.dma_start(out=kA[67:70, :, :], in_=cum_d[seq, :, 3:6, :].rearrange("h j s -> j h s")), r=[b_cumd], w=[b_kA], dk="kA")
            tr.op("sp", lambda q: q.dma_start(out=Vf[:, :, :], in_=vf_d[seq, :, :].rearrange("(i p) c -> p i c", p=128)), w=[b_Vf], dk="Vf")
            units = []
            npair = 0
            for h in range(8):
                for qb in range(4):
                    nkt = 4 * qb + 4
                    for kt in range(nkt):
                        units.append(dict(h=h, qb=qb, kt=kt, nkt=nkt, pair=npair))
                    npair += 1
            cnt = {"u": 0}

            def stA(u):
                h, qb, kt = u["h"], u["qb"], u["kt"]
                j = kt - 4 * qb
                c0 = max(j, 0) * 128
                i = cnt["u"]
                cnt["u"] += 1
                si = 4 + (i % 4)
                Sb, Sbb = banks[si], bbuf[si]
                pt, bpt = PT[i % NPT], b_PT[i % NPT]
                u["pt"], u["bpt"], u["c0"] = pt, bpt, c0
                mm(Sb[:, c0:512], [(kA[0:70, h, kt * 128:(kt + 1) * 128], qA[0:70, h, qb * 512 + c0:(qb + 1) * 512])], r=[b_kA, b_qA], w=[Sbb])
                if j >= 0:
                    dt_, bdt = dtmp[kt % 2], b_dtmp[kt % 2]
                    tr.op("dve", lambda v: v.tensor_tensor(out=dt_[:], in0=Sb[:, c0:c0 + 128], in1=cf[:, CF_MNEG:CF_MNEG + 128], op=ALU.add), r=[Sbb, b_cf], w=[bdt])
                    tr.op("act", lambda a: a.activation(out=pt[:, c0:c0 + 128], in_=dt_[:], func=AF.Exp), r=[bdt], w=[bpt])
                    if c0 + 128 < 512:
                        tr.op("act", lambda a: a.activation(out=pt[:, c0 + 128:512], in_=Sb[:, c0 + 128:512], func=AF.Exp), r=[Sbb], w=[bpt])
                else:
                    tr.op("act", lambda a: a.activation(out=pt[:, :], in_=Sb[:, :], func=AF.Exp), r=[Sbb], w=[bpt])

            def stB(u):
                h, qb, kt, nkt = u["h"], u["qb"], u["kt"], u["nkt"]
                so = (u["pair"] % 2) * 2
                io = u["pair"] % 2
                Ob, Obb, Db, Dbb = banks[so], bbuf[so], banks[so + 1], bbuf[so + 1]
                pt, bpt, c0 = u["pt"], u["bpt"], u["c0"]

                hp = h // 2
                r0 = (h % 2) * 64

                def fpv(pe):
                    pe.matmul(Ob[:, c0:512], Vf[:, kt, hp * 128:(hp + 1) * 128], pt[:, c0:512], start=(kt == 0), stop=(kt == nkt - 1))
                    return pe.matmul(Db[:, c0:512], ones128, pt[:, c0:512], start=(kt == 0), stop=(kt == nkt - 1))
                tr.op("pe", fpv, r=[bpt, b_Vf, b_cb], w=[Obb, Dbb])
                if kt == nkt - 1:
                    rc, brc = rec[io], b_rec[io]
                    os_, bos = ost[io], b_ost[io]
                    tr.op("dve", lambda v: v.reciprocal(out=rc[r0:r0 + 64, :], in_=Db[r0:r0 + 64, :]), r=[Dbb], w=[brc])
                    tr.op("dve", lambda v: v.tensor_tensor(out=os_[r0:r0 + 64, :], in0=Ob[r0:r0 + 64, :], in1=rc[r0:r0 + 64, :], op=ALU.mult), r=[Obb, brc], w=[bos])
                    tr.op("sp", lambda q: q.dma_start(out=oaT_d[seq, h * 64:(h + 1) * 64, qb * 512:(qb + 1) * 512], in_=os_[r0:r0 + 64, :]), r=[bos], dk="ost%d" % io)

            pipeline(units, [stA, stB], skew=2)
            tr.barrier()

    def phase_B_dsa(seq):
        with contextlib.ExitStack() as ph:
            QQ = sb(ph, "QQ", [128, 8, S], BF16)
            Kz = sb(ph, "Kz", [128, S], BF16)
            Kzi = sb(ph, "Kzi", [128, S], BF16)
            dg = [sb(ph, "dg%d" % i, [128, 8, 128], F32R) for i in range(2)]
            b_dg = [Buf("dg%d" % i) for i in range(2)]
            Vd = sb(ph, "Vd", [128, 16, 128], BF16)
            iwt = sb(ph, "iwt", [128, 16, 8], F32)
            NSC = 7
            NRL = 3
            sc = [sb(ph, "sc%d" % i, [128, S], F32) for i in range(NSC)]
            Mtok = [sb(ph, "Mtok%d" % i, [128, S], BF16) for i in range(4)]
            rl = [sb(ph, "rl%d" % i, [128, 512], F32R) for i in range(NRL)]
            MT = [sb(ph, "MT%d" % i, [128, 16, 512], BF16) for i in range(2)]
            E = [sb(ph, "E%d" % i, [128, 512], BF16) for i in range(3)]
            PT = [sb(ph, "PTd%d" % i, [128, 512], BF16) for i in range(3)]
            MR = [sb(ph, "MR%d" % i, [128, 128], BF16) for i in range(4)]
            osb = [sb(ph, "osb%d" % i, [64, 512], F32) for i in range(2)]
            rec = [sb(ph, "recd%d" % i, [64, 512], F32) for i in range(2)]
            ost = [sb(ph, "ostd%d" % i, [64, 512], BF16) for i in range(2)]
            sm = [sb(ph, "sm%d" % i, [128, 8], F32) for i in range(4)]
            steps = [sb(ph, "steps%d" % i, [128, NITER + 1], F32) for i in range(4)]
            b_dqT, b_iqT, b_dkT, b_ikT, b_Vd, b_iwt = Buf("dqT"), Buf("iqT"), Buf("dkT"), Buf("ikT"), Buf("Vd"), Buf("iwt")
            b_sc = [[Buf("sc%d_%d" % (i, c)) for c in range(4)] for i in range(NSC)]
            b_Mtok = [Buf("Mtok%d" % i) for i in range(4)]
            b_rl = [Buf("rl%d" % i) for i in range(NRL)]
            b_MT = [[Buf("MT%d_%d" % (i, k)) for k in range(4)] for i in range(2)]
            b_E = [Buf("E%d" % i) for i in range(3)]
            b_PT = [Buf("PTd%d" % i) for i in range(3)]
            b_MR = [Buf("MR%d" % i) for i in range(4)]
            b_osb = [Buf("osb%d" % i) for i in range(2)]
            b_rec = [Buf("recd%d" % i) for i in range(2)]
            b_ost = [Buf("ostd%d" % i) for i in range(2)]
            b_sm = [Buf("sm%d" % i) for i in range(4)]
            b_steps = [Buf("steps%d" % i) for i in range(4)]
            tr.op("sp", lambda q: q.dma_start(out=QQ[0:64, :, :], in_=dqT_d[seq, :, :].rearrange("(h d) s -> d h s", d=64)), w=[b_dqT], dk="qA")
            tr.op("sp", lambda q: q.dma_start(out=QQ[64:128, :, :], in_=iqT_d[seq, :, :].rearrange("(h d) s -> d h s", d=64)), w=[b_iqT], dk="kA")
            tr.op("pool", lambda g: g.memset(Kz[:, :], 0.0), w=[b_dkT])
            tr.op("pool", lambda g: g.memset(Kzi[:, :], 0.0), w=[b_ikT])
            tr.op("pool", lambda g: g.memset(Vd[:, :, :], 0.0), w=[b_Vd])
            tr.op("sp", lambda q: q.dma_start(out=Kz[0:64, :], in_=dkT_d[seq, :, :]), w=[b_dkT], dk="dkT")
            tr.op("sp", lambda q: q.dma_start(out=Kzi[64:128, :], in_=ikT_d[seq, :, :]), w=[b_ikT], dk="ikT")
            tr.op("sp", lambda q: q.dma_start(out=Vd[:, :, 0:64], in_=vd_d[seq, :, :, :]), w=[b_Vd], dk="Vf")
            tr.op("sp", lambda q: q.dma_start(out=iwt[:, :, :], in_=iw_d[seq, :, :, :]), w=[b_iwt], dk="iwt")
            cau = cb[:, CB_CAU:CB_CAU + 128]
            state = {"sb": 0, "db": 0, "acc": 0, "dg": 0, "rl": 0, "pt": 0, "mr": 0, "pair": 0}

            def sbank():
                i = 2 + (state["sb"] % 2)
                state["sb"] += 1
                return banks[i], bbuf[i]

            def dbank():
                i = 4 + (state["db"] % 2)
                state["db"] += 1
                return banks[i], bbuf[i]

            tiles_of = {}

            def scores(qb):
                mt, bmt = MT[qb % 2], b_MT[qb % 2]
                tiles = []
                tiles_of[qb] = tiles
                for qq in range(4):
                    qt = 4 * qb + qq
                    cs = slice(qq * 128, (qq + 1) * 128)
                    if qt < 2:
                        if qt == 1:
                            tr.op("pool", lambda g, cs=cs: g.memset(mt[:, 0, cs], 1.0), w=[bmt[qq]])
                        tr.op("pool", lambda g, cs=cs, qt=qt: g.tensor_copy(out=mt[:, qt, cs], in_=cau), r=[b_cb], w=[bmt[qq]])
                        continue
                    n = (qt + 1) * 128
                    nch = (n + 511) // 512
                    tiles.append(dict(qt=qt, qq=qq, cs=cs, n=n, nch=nch, s_=sc[qt % NSC], bsl=b_sc[qt % NSC][0:nch],
                                      smt=sm[qq], bsm=b_sm[qq], stp=steps[qq], bstp=b_steps[qq], mk=Mtok[qq], bmk=b_Mtok[qq]))
                units = []
                for T in tiles:
                    for c4 in range(T["nch"]):
                        for h in range(8):
                            units.append(dict(T=T, h=h, c4=c4, k0=c4 * 512, wdt=min(512, T["n"] - c4 * 512)))

                def s1(u):
                    T, h, k0, wdt = u["T"], u["h"], u["k0"], u["wdt"]
                    qt = T["qt"]
                    if h == 0 and u["c4"] == 0:
                        di = state["dg"] % 2
                        state["dg"] += 1
                        T["di"] = di

                        def fdg(g):
                            ins = None
                            for hh in range(8):
                                ins = g.tensor_scalar(out=dg[di][:, hh, :], in0=cf[:, CF_IDF:CF_IDF + 128], scalar1=iwt[:, qt, hh:hh + 1], scalar2=0.0, op0=ALU.mult, op1=ALU.add)
                            return ins
                        tr.op("pool", fdg, r=[b_cf, b_iwt], w=[b_dg[di]])
                    bk, bb = dbank()
                    mm(bk[:, 0:wdt], [(QQ[:, h, qt * 128:(qt + 1) * 128], Kzi[:, k0:k0 + wdt])], r=[b_iqT, b_dqT, b_ikT], w=[bb])
                    j = state["rl"] % NRL
                    state["rl"] += 1
                    u["j"] = j
                    tr.op("act", lambda a: a.activation(out=rl[j][:, 0:wdt], in_=bk[:, 0:wdt], func=AF.Relu), r=[bb], w=[b_rl[j]])

                def s2(u):
                    T, h, k0, wdt, j = u["T"], u["h"], u["k0"], u["wdt"], u["j"]
                    if h == 0:
                        ai = 6 + (state["acc"] % 2)
                        state["acc"] += 1
                        T["ai"] = ai
                    ai = T["ai"]
                    di = T["di"]
                    tr.op("pe", lambda pe: pe.matmul(banks[ai][:, 0:wdt], dg[di][:, h, :], rl[j][:, 0:wdt], start=(h == 0), stop=(h == 7)), r=[b_rl[j], b_dg[di]], w=[bbuf[ai]])
                    if h == 7:
                        s_ = T["s_"]
                        tr.op("act", lambda a: a.activation(out=s_[:, k0:k0 + wdt], in_=banks[ai][:, 0:wdt], func=AF.Copy), r=[bbuf[ai]], w=[T["bsl"][u["c4"]]])

                pipeline(units, [s1, s2], skew=1)

            def bisect(qb):
                tiles = tiles_of[qb]
                for T in tiles:
                    tr.op("dve", lambda v, T=T: v.tensor_reduce(out=T["smt"][:, 0:1], in_=T["s_"][:, 0:T["n"]], axis=AX.X, op=ALU.min), r=T["bsl"], w=[T["bsm"]])
                for T in tiles:
                    tr.op("dve", lambda v, T=T: v.tensor_reduce(out=T["smt"][:, 1:2], in_=T["s_"][:, 0:T["n"]], axis=AX.X, op=ALU.max), r=T["bsl"], w=[T["bsm"]])
                for T in tiles:
                    tr.op("dve", lambda v, T=T: v.tensor_tensor(out=T["s_"][:, T["n"] - 128:T["n"]], in0=T["s_"][:, T["n"] - 128:T["n"]], in1=cf[:, CF_MNEGT:CF_MNEGT + 128], op=ALU.add), r=[b_cf], w=[T["bsl"][-1]])
                for T in tiles:
                    tr.op("dve", lambda v, T=T: v.tensor_scalar(out=T["smt"][:, 2:3], in0=T["smt"][:, 1:2], scalar1=T["smt"][:, 0:1], scalar2=1.00390625, op0=ALU.subtract, op1=ALU.mult), w=[T["bsm"]])
                for T in tiles:
                    tr.op("dve", lambda v, T=T: v.tensor_scalar(out=T["stp"][:, :], in0=cf[:, CF_POW:CF_POW + NITER + 1], scalar1=T["smt"][:, 2:3], scalar2=0.5, op0=ALU.mult, op1=ALU.mult), r=[T["bsm"], b_cf], w=[T["bstp"]])
                for T in tiles:
                    tr.op("dve", lambda v, T=T: v.tensor_tensor(out=T["smt"][:, 3:4], in0=T["smt"][:, 0:1], in1=T["stp"][:, 0:1], op=ALU.add), r=[T["bstp"]], w=[T["bsm"]])
                for k in range(NITER):
                    for T in tiles:
                        tr.op("dve", lambda v, T=T: v.tensor_scalar(out=T["mk"][:, 0:T["n"]], in0=T["s_"][:, 0:T["n"]], scalar1=T["smt"][:, 3:4], scalar2=None, op0=ALU.is_ge, op1=ALU.add, accum_out=T["smt"][:, 4:5]),
                              r=T["bsl"], w=[T["bsm"], T["bmk"]])
                    for T in tiles:
                        tr.op("dve", lambda v, T=T: v.tensor_scalar(out=T["smt"][:, 5:6], in0=T["smt"][:, 4:5], scalar1=TOPK - 0.5, scalar2=0.5, op0=ALU.is_ge, op1=ALU.subtract), w=[T["bsm"]])
                    for T in tiles:
                        tr.op("dve", lambda v, T=T, k=k: v.scalar_tensor_tensor(out=T["smt"][:, 3:4], in0=T["smt"][:, 5:6], scalar=T["stp"][:, k:k + 1], in1=T["smt"][:, 3:4], op0=ALU.mult, op1=ALU.add), r=[T["bstp"]], w=[T["bsm"]])
                for T in tiles:
                    tr.op("dve", lambda v, T=T: v.tensor_tensor(out=T["smt"][:, 6:7], in0=T["smt"][:, 3:4], in1=T["stp"][:, NITER:NITER + 1], op=ALU.subtract), r=[T["bstp"]], w=[T["bsm"]])
                for T in tiles:
                    tr.op("dve", lambda v, T=T: v.tensor_scalar(out=T["mk"][:, 0:T["n"]], in0=T["s_"][:, 0:T["n"]], scalar1=T["smt"][:, 6:7], scalar2=None, op0=ALU.is_ge), r=T["bsl"] + [T["bsm"]], w=[T["bmk"]])

            def transp(qb):
                mt, bmt = MT[qb % 2], b_MT[qb % 2]
                for T in tiles_of[qb]:
                    qt, mk, cs, qq = T["qt"], T["mk"], T["cs"], T["qq"]
                    for g0 in range(0, qt + 1, 8):
                        g1 = min(qt + 1, g0 + 8)
                        bk, bb = dbank()
                        ptb = bk[:].bitcast(BF16)

                        def ftr(pe, g0=g0, g1=g1, ptb=ptb, mk=mk):
                            ins = None
                            for kt in range(g0, g1):
                                ins = pe.transpose(out=ptb[:, (kt - g0) * 128:(kt - g0 + 1) * 128], in_=mk[:, kt * 128:(kt + 1) * 128], identity=ident)
                            return ins
                        tr.op("pe", ftr, r=[T["bmk"], b_cb], w=[bb])
                        tr.op("act", lambda a, g0=g0, g1=g1, ptb=ptb, cs=cs: a.activation(out=mt[:, g0:g1, cs], in_=ptb[:, 0:(g1 - g0) * 128].rearrange("p (g t) -> p g t", t=128), func=AF.Copy),
                              r=[bb], w=[bmt[qq]])

            def heads(qb):
                mt, bmt = MT[qb % 2], b_MT[qb % 2]
                nkt = 4 * qb + 4
                units = []
                for h in range(8):
                    for kt in range(nkt):
                        units.append(dict(h=h, kt=kt, pair=state["pair"]))
                    state["pair"] += 1

                def hA(u):
                    h, kt = u["h"], u["kt"]
                    j = kt - 4 * qb
                    c0 = max(j, 0) * 128
                    Sb, Sbb = sbank()
                    ip = state["pt"] % 3
                    state["pt"] += 1
                    u["ip"], u["c0"], u["j"] = ip, c0, j
                    e_, be = E[ip], b_E[ip]
                    mm(Sb[:, c0:512], [(Kz[:, kt * 128:(kt + 1) * 128], QQ[:, h, qb * 512 + c0:(qb + 1) * 512])], r=[b_dkT, b_dqT, b_iqT], w=[Sbb])
                    tr.op("act", lambda a: a.activation(out=e_[:, c0:512], in_=Sb[:, c0:512], func=AF.Exp, bias=cf[:, CF_RB31 + h:CF_RB31 + h + 1]), r=[Sbb, b_cf], w=[be])

                def hB(u):
                    h, kt, ip, c0, j = u["h"], u["kt"], u["ip"], u["c0"], u["j"]
                    e_, be = E[ip], b_E[ip]
                    pt, bpt = PT[ip], b_PT[ip]
                    near = []
                    if j >= 0:
                        near.append((c0, 0))
                        if c0 + 128 < 512:
                            near.append((c0 + 128, 128))
                    elif j == -1:
                        near.append((0, 128))
                    cfar = (near[-1][0] + 128) if near else 0
                    rds = [bmt[c // 128] for c in range(c0, 512, 128)]
                    for (cn, ro) in near:
                        im = state["mr"] % 4
                        state["mr"] += 1
                        tr.op("pool", lambda g, im=im, cn=cn, ro=ro: g.tensor_tensor(out=MR[im][:], in0=mt[:, kt, cn:cn + 128], in1=Rt[:, h, ro:ro + 128], op=ALU.mult), r=[bmt[cn // 128], b_Rt], w=[b_MR[im]])
                        tr.op("pool", lambda g, im=im, cn=cn: g.tensor_tensor(out=pt[:, cn:cn + 128], in0=e_[:, cn:cn + 128], in1=MR[im][:], op=ALU.mult), r=[be, b_MR[im]], w=[bpt])
                    if cfar < 512:
                        tr.op("dve" if qb == 3 else "pool", lambda g: g.tensor_tensor(out=pt[:, cfar:512], in0=e_[:, cfar:512], in1=mt[:, kt, cfar:512], op=ALU.mult), r=[be] + rds, w=[bpt])

                def hC(u):
                    h, kt, ip, c0 = u["h"], u["kt"], u["ip"], u["c0"]
                    pt, bpt = PT[ip], b_PT[ip]
                    so = (u["pair"] % 2) * 4
                    io = u["pair"] % 2
                    Ob, Obb, Db, Dbb = banks[so], bbuf[so], banks[so + 1], bbuf[so + 1]

                    def fpv(pe):
                        pe.matmul(Ob[:, c0:512], Vd[:, kt, :], pt[:, c0:512], start=(kt == 0), stop=(kt == nkt - 1))
                        return pe.matmul(Db[:, c0:512], ones128, pt[:, c0:512], start=(kt == 0), stop=(kt == nkt - 1))
                    tr.op("pe", fpv, r=[bpt, b_Vd, b_cb], w=[Obb, Dbb])
                    if kt == nkt - 1:
                        tr.op("act", lambda a: a.activation(out=rec[io][:], in_=Db[0:64, :], func=AF.Ln), r=[Dbb], w=[b_rec[io]])
                        tr.op("act", lambda a: a.activation(out=osb[io][:], in_=Ob[0:64, :], func=AF.Copy), r=[Obb], w=[b_osb[io]])
                        tr.op("act", lambda a: a.activation(out=rec[io][:], in_=rec[io][:], func=AF.Exp, scale=-1.0), w=[b_rec[io]])
                        tr.op("pool", lambda g: g.tensor_tensor(out=ost[io][:], in0=osb[io][:], in1=rec[io][:], op=ALU.mult), r=[b_osb[io], b_rec[io]], w=[b_ost[io]])
                        tr.op("sp", lambda q: q.dma_start(out=obT_d[seq, h * 64:(h + 1) * 64, qb * 512:(qb + 1) * 512], in_=ost[io][:]), r=[b_ost[io]], dk="ostd%d" % io)

                pipeline(units, [hA, hB, hC], skew=1)

            scores(0)
            bisect(0)
            scores(1)
            transp(0)
            bisect(1)
            heads(0)
            scores(2)
            transp(1)
            bisect(2)
            heads(1)
            scores(3)
            transp(2)
            bisect(3)
            heads(2)
            transp(3)
            heads(3)
            tr.barrier()

    for seq in range(NSEQ):
        with contextlib.ExitStack() as ph:
            d = alloc_dense(ph)
            st = alloc_stage(ph)
            for blk in (2 * seq, 2 * seq + 1):
                phase_A(d, st, blk)
            tr.barrier()
        if stage < 3:
            continue
        if stage >= 4:
            phase_B_fox(seq)
        if stage >= 5:
            phase_B_dsa(seq)
        if stage in (4, 5):
            continue
        with contextlib.ExitStack() as ph:
            d = alloc_dense(ph)
            st = alloc_stage_C(ph)
            for blk in (2 * seq, 2 * seq + 1):
                phase_C(d, st, blk)
            tr.barrier()
    tr.barrier()
    es.close()
    return nc


def _t5_bucket_table(n):
    rel = np.arange(n, dtype=np.int32)
    relf = np.maximum(rel, 1).astype(np.float32)
    large = 16 + (np.log(relf / np.float32(16)) / np.float32(np.log(128 / 16)) * np.float32(16)).astype(np.int32)
    large = np.minimum(large, 31)
    return np.where(rel < 16, rel, large)


def _consts(inp):
    cf = np.zeros((128, NCF), np.float32)
    for c0, k in ((CF_G1, "ffn1_norm"), (CF_GM, "mix_norm"), (CF_G2, "ffn2_norm")):
        cf[:, c0:c0 + 8] = np.asarray(inp[k], np.float32).reshape(8, 128).T
    cf[:, CF_GQ] = np.tile(np.asarray(inp["fox_q_norm"], np.float32), 2)
    cf[:, CF_GK] = np.tile(np.asarray(inp["fox_k_norm"], np.float32), 2)
    cf[:, CF_GDQ] = np.tile(np.asarray(inp["dsa_q_norm"], np.float32), 2)
    cf[:, CF_GDK] = np.tile(np.asarray(inp["dsa_k_norm"], np.float32), 2)
    cf[0:8, CF_BF] = np.asarray(inp["b_forget"], np.float32)
    rb = np.asarray(inp["rel_bias"], np.float32)
    cf[:, CF_RB31:CF_RB31 + 8] = rb[31][None, :]
    cf[:, CF_POW:CF_POW + NITER + 1] = (0.5 ** np.arange(NITER + 1, dtype=np.float64))[None, :].astype(np.float32)
    s = np.arange(128)
    cf[:, CF_MNEG:CF_MNEG + 128] = np.where(s[:, None] <= s[None, :], 0.0, NEG)
    cf[:, CF_MNEGT:CF_MNEGT + 128] = np.where(s[None, :] <= s[:, None], 0.0, NEG)
    tp = np.arange(256)
    delta = np.maximum(tp[None, :] - s[:, None], 0)
    bt = _t5_bucket_table(512)[delta]
    g = rb[bt]
    cf[:, CF_GT:CF_GT + 2048] = np.transpose(g, (0, 2, 1)).reshape(128, 2048)
    cf[:, CF_IDF:CF_IDF + 128] = np.eye(128)
    cb = np.zeros((128, NCB), np.float32)
    cb[:, CB_ID:CB_ID + 128] = np.eye(128)
    cb[:, CB_BLK:CB_BLK + 128] = (s[:, None] // 64 == s[None, :] // 64)
    cb[:, CB_ONE:CB_ONE + 64] = 1.0
    cb[:, CB_ONE2:CB_ONE2 + 128] = 1.0
    cb[:, CB_CAU:CB_CAU + 128] = (s[:, None] <= s[None, :])
    return cf, cb.astype(ml_dtypes.bfloat16)


def _perm_win(w_in):
    w = np.asarray(w_in, np.float32)
    o = {}
    off = 0
    for name, wd in (("fq", 512), ("fk", 512), ("fv", 512), ("ff", 8), ("dq", 512), ("dk", 64), ("dv", 64),
                     ("iq", 512), ("ik", 64), ("iw", 8), ("ga", 1024), ("gb", 1024)):
        o[name] = w[:, off:off + wd]
        off += wd
    return np.ascontiguousarray(np.concatenate([o[k] for k in ("fq", "fk", "dq", "iq", "fv", "dk", "ik", "ff", "dv", "iw", "ga", "gb")], axis=1))


def make_in_maps(inp, n_cores=8):
    f = lambda k: np.ascontiguousarray(np.asarray(inp[k], np.float32))
    cf, cb = _consts(inp)
    shared = {
        "f1g": f("ffn1_w_gate"), "f1u": f("ffn1_w_up"), "f1d": f("ffn1_w_down"),
        "win": _perm_win(inp["w_in"]), "wa": f("w_branch_a"), "wb": f("w_branch_b"), "wo": f("w_out"),
        "f2g": f("ffn2_w_gate"), "f2u": f("ffn2_w_up"), "f2d": f("ffn2_w_down"),
        "cf": cf, "cb": cb,
    }
    x = np.asarray(inp["x"], np.float32)
    maps = []
    for c in range(n_cores):
        m = dict(shared)
        m["x"] = np.ascontiguousarray(x[NSEQ * c:NSEQ * (c + 1)].reshape(NSEQ * S, D))
        maps.append(m)
    return maps


def kernel(**inputs):
    nc = bass.Bass("TRN2", target_bir_lowering=False)
    build(nc)
    maps = make_in_maps(inputs, 8)
    res = run_bass_kernel_spmd(nc, maps, core_ids=list(range(8)))
    outs = [np.asarray(r["out"], np.float32).reshape(NSEQ, S, D) for r in res.results]
    return np.concatenate(outs, axis=0)
```

```python
import contextlib
import os
PARTS = os.environ.get('DBG_PARTS', 'qk,fv,small,small2,gates,iw,iwd,dvd').split(',')
LVL = int(os.environ.get('DBG_LVL', '9'))
LVLB = int(os.environ.get('DBG_LVLB', '9'))
import numpy as np
import ml_dtypes
import concourse.bass as bass
import concourse.mybir as mybir
from concourse.bass_utils import run_bass_kernel_spmd

F32 = mybir.dt.float32
BF16 = mybir.dt.bfloat16
F32R = mybir.dt.float32r
AF = mybir.ActivationFunctionType
ALU = mybir.AluOpType
AX = mybir.AxisListType

D = 1024
S = 2048
NSEQ = 2
DFF = 2816
NFC = DFF // 128
TB = 1024
NITER = 16
TOPK = 256
EPS = 1e-6
NEG = -1e30

C_FQ, C_FK, C_DQ, C_IQ, C_FV = 0, 512, 1024, 1536, 2048
C_SM = 2560
C_GA = C_SM + 208
C_GB = C_GA + 1024
NIN = C_GB + 1024

CF_G1, CF_GM, CF_G2 = 0, 8, 16
CF_GQ, CF_GK, CF_GDQ, CF_GDK, CF_BF = 24, 25, 26, 27, 28
CF_RB31 = 29
CF_POW = 37
CF_MNEG = 64
CF_MNEGT = 192
CF_GT = 320
CF_IDF = CF_GT + 8 * 256
NCF = CF_IDF + 128
CB_ID, CB_BLK, CB_ONE, CB_CAU, CB_ONE2 = 0, 128, 256, 320, 448
NCB = 576


class Buf:
    __slots__ = ("name", "w", "rs", "excl")

    def __init__(self, name, excl=False):
        self.name = name
        self.excl = excl
        self.w = None
        self.rs = {}


class Trk:
    def __init__(self, nc, es):
        self.nc = nc
        self.es = es
        self.eng = {"pe": nc.tensor, "act": nc.scalar, "dve": nc.vector, "pool": nc.gpsimd, "sp": nc.sync}
        self.sem = {k: es.enter_context(nc.semaphore("s_" + k)) for k in ("pe", "act", "dve", "pool")}
        self.cnt = {k: 0 for k in self.sem}
        self.waited = {k: {} for k in self.eng}
        self.dsem = {}
        self.dcnt = {}
        self.nwaits = 0

    def _wait(self, e, tok):
        sem, val, key, src, isdma = tok
        if src == e and e == "pe" and not isdma:
            return
        w = self.waited[e]
        if w.get(key, 0) >= val:
            return
        w[key] = val
        self.nwaits += 1
        self.eng[e].wait_ge(sem, val)

    def op(self, e, fn, r=(), w=(), dk=None):
        w = list(w) + [b for b in r if b.excl]
        r = [b for b in r if not b.excl]
        for b in r:
            if b.w is not None:
                self._wait(e, b.w)
        for b in w:
            if b.w is not None:
                self._wait(e, b.w)
            for t in b.rs.values():
                self._wait(e, t)
        ins = fn(self.eng[e])
        if dk is not None:
            if dk not in self.dsem:
                self.dsem[dk] = self.es.enter_context(self.nc.semaphore("d_" + dk))
                self.dcnt[dk] = 0
            self.dcnt[dk] += 16
            ins.then_inc(self.dsem[dk], 16)
            tok = (self.dsem[dk], self.dcnt[dk], "d_" + dk, e, True)
        else:
            self.cnt[e] += 1
            ins.then_inc(self.sem[e], 1)
            tok = (self.sem[e], self.cnt[e], "s_" + e, e, False)
        for b in r:
            b.rs[tok[2]] = tok
        for b in w:
            b.w = tok
            b.rs = {}
        return tok

    def barrier(self):
        toks = [(self.sem[k], self.cnt[k], "s_" + k, k, False) for k in self.sem if self.cnt[k] > 0]
        toks += [(self.dsem[k], self.dcnt[k], "d_" + k, None, True) for k in self.dsem]
        for e in self.eng:
            for t in toks:
                self._wait(e, t)


def build(nc, dbg=False, stage=99):
    es = contextlib.ExitStack()
    tr = Trk(nc, es)
    OUTK = "ExternalOutput" if dbg else "Internal"

    def din(name, shape, dt=F32):
        return nc.dram_tensor(name, shape, dt, kind="ExternalInput").ap()

    def dscr(name, shape, dt, kind=None):
        return nc.dram_tensor(name, shape, dt, kind=kind or "Internal").ap()

    x = din("x", [NSEQ * S, D])
    wsrc = {
        "f1g": din("f1g", [NFC // 2, D, 256]), "f1u": din("f1u", [NFC // 2, D, 256]), "f1d": din("f1d", [DFF, D]),
        "win": din("win", [D, NIN]), "wa": din("wa", [512, D]), "wb": din("wb", [512, D]),
        "wo": din("wo", [D, D]),
        "f2g": din("f2g", [NFC // 2, D, 256]), "f2u": din("f2u", [NFC // 2, D, 256]), "f2d": din("f2d", [DFF, D]),
    }
    cf_d = din("cf", [128, NCF])
    cb_d = din("cb", [128, NCB], BF16)
    out = nc.dram_tensor("out", [NSEQ * S, D], F32, kind="ExternalOutput").ap()

    wb16 = {k: dscr(k + "_b", list(v.shape), BF16) for k, v in wsrc.items()}
    wbuf = {k: [] for k in wsrc}
    NCAST = 2
    castslot = [Buf("cast%d" % i) for i in range(NCAST)]
    ncast = [0]
    x1_d = dscr("x1_d", [NSEQ * S, D], F32, OUTK)
    qT_d = dscr("qT_d", [NSEQ, 512, S], BF16, OUTK)
    kT_d = dscr("kT_d", [NSEQ, 512, S], BF16, OUTK)
    dqT_d = dscr("dqT_d", [NSEQ, 512, S], BF16, OUTK)
    iqT_d = dscr("iqT_d", [NSEQ, 512, S], BF16, OUTK)
    dkT_d = dscr("dkT_d", [NSEQ, 64, S], BF16, OUTK)
    ikT_d = dscr("ikT_d", [NSEQ, 64, S], BF16, OUTK)
    vf_d = dscr("vf_d", [NSEQ, S, 512], BF16, OUTK)
    vd_d = dscr("vd_d", [NSEQ, 128, 16, 64], BF16, OUTK)
    ff_d = dscr("ff_d", [NSEQ, 8, S], F32, OUTK)
    iw_d = dscr("iw_d", [NSEQ, 128, 16, 8], F32, OUTK)
    gT_d = dscr("gT_d", [NSEQ, 2048, S], BF16, OUTK)
    cum_d = dscr("cum_d", [NSEQ, 8, 6, S], BF16, OUTK)
    OK_O = "ExternalInput" if (dbg and stage == 3) else OUTK
    oaT_d = dscr("oaT_d", [NSEQ, 512, S], BF16, OK_O)
    obT_d = dscr("obT_d", [NSEQ, 512, S], BF16, OK_O)

    uid = [0]

    def sb(stack, name, shape, dt):
        uid[0] += 1
        return stack.enter_context(nc.sbuf_tensor("%s_s%d" % (name, uid[0]), shape, dt))

    cf = sb(es, "cf", [128, NCF], F32)
    cb = sb(es, "cb", [128, NCB], BF16)
    nrb = sb(es, "nrb", [128, 8], F32)
    Rt = sb(es, "Rt", [128, 8, 256], BF16)
    b_Rt = Buf("Rt")
    banks = [es.enter_context(nc.psum_tensor("bank%d" % i, [128, 512], F32)) for i in range(8)]
    bbuf = [Buf("bank%d" % i, excl=True) for i in range(8)]
    bstate = {"i": 0}

    def nbank():
        i = bstate["i"]
        bstate["i"] = (i + 1) % 8
        return banks[i], bbuf[i]

    b_cf, b_cb, b_nrb = Buf("cf"), Buf("cb"), Buf("nrb")
    ident = cb[:, CB_ID:CB_ID + 128]
    blkones = cb[:, CB_BLK:CB_BLK + 128]
    ones64 = cb[:, CB_ONE:CB_ONE + 64]
    ones128 = cb[:, CB_ONE2:CB_ONE2 + 128]

    tr.op("sp", lambda q: q.dma_start(out=cf[:], in_=cf_d[:, :]), w=[b_cf], dk="cf")
    tr.op("sp", lambda q: q.dma_start(out=cb[:], in_=cb_d[:, :]), w=[b_cb], dk="cb")
    def cast_dma(k, srcv, dstv, tag):
        cbuf = Buf("wc_%s_%s" % (k, tag))
        wbuf[k].append(cbuf)
        ci = ncast[0] % NCAST
        ncast[0] += 1
        tr.op("pool", lambda q: q.dma_start(out=dstv, in_=srcv), w=[cbuf, castslot[ci]], dk="cw%d" % ci)

    def cast_slots(kg, ku):
        for s_ in range(NFC // 2):
            for k in (kg, ku):
                cast_dma(k, wsrc[k][s_].rearrange("(a b) c -> a (b c)", b=8), wb16[k][s_].rearrange("(a b) c -> a (b c)", b=8), "s%d" % s_)

    def cast_flat(k):
        src = wsrc[k]
        n = src.shape[0] * src.shape[1]
        rows = n // 2048
        s2 = src.rearrange("a b -> (a b)").rearrange("(r c) -> r c", c=2048)
        d2 = wb16[k].rearrange("a b -> (a b)").rearrange("(r c) -> r c", c=2048)
        step = 704 if rows % 704 == 0 else (rows if rows <= 704 else 602)
        assert rows % step == 0, (k, rows, step)
        for r0 in range(0, rows, step):
            cast_dma(k, s2[r0:r0 + step, :], d2[r0:r0 + step, :], "r%d" % r0)

    cast_slots("f1g", "f1u")
    for k in ("f1d", "win", "wa", "wb", "wo"):
        cast_flat(k)
    cast_slots("f2g", "f2u")
    cast_flat("f2d")
    tr.op("dve", lambda v: v.tensor_scalar(out=nrb[:], in0=cf[:, CF_RB31:CF_RB31 + 8], scalar1=-1.0, scalar2=None, op0=ALU.mult), r=[b_cf], w=[b_nrb])
    for h in range(8):
        tr.op("act", lambda a, h=h: a.activation(out=Rt[:, h, :], in_=cf[:, CF_GT + h * 256:CF_GT + (h + 1) * 256], func=AF.Exp, bias=nrb[:, h:h + 1]), r=[b_cf, b_nrb], w=[b_Rt])

    class Dense:
        pass

    def alloc_dense(stack):
        d = Dense()
        d.xt = sb(stack, "xt", [128, 8, D], F32)
        d.xnT = sb(stack, "xnT", [128, 8, TB], BF16)
        d.aT = sb(stack, "aT", [128, NFC, TB], BF16)
        d.ringA = [sb(stack, "rA%d" % i, [128, 8, 512], BF16) for i in range(3)]
        d.ringB = [sb(stack, "rB%d" % i, [128, NFC, 512], BF16) for i in range(2)]
        d.sg = [sb(stack, "sg%d" % i, [128, 512], F32) for i in range(3)]
        d.xs = [sb(stack, "xs%d" % i, [128, D], BF16) for i in range(2)]
        d.junk = sb(stack, "junk", [128, D], BF16)
        d.ssq = sb(stack, "ssq", [128, 8], F32)
        d.rstd = sb(stack, "rstd", [128, 8], F32)
        d.b_xtl, d.b_xnT = [Buf("xt%d" % i) for i in range(8)], Buf("xnT")
        d.b_aT = [Buf("aT%d" % i) for i in range(NFC)]
        d.b_rA = [Buf("rA%d" % i) for i in range(3)]
        d.b_rB = [Buf("rB%d" % i) for i in range(2)]
        d.b_sg = [Buf("sg%d" % i) for i in range(3)]
        d.b_xs = [Buf("xs%d" % i) for i in range(2)]
        d.b_junk, d.b_ssq, d.b_rstd = Buf("junk"), Buf("ssq"), Buf("rstd")
        d.iA = 0
        d.iB = 0
        d.isg = 0
        return d

    def ringA_load(d, wkey, parts):
        i = d.iA % 3
        d.iA += 1
        t, b = d.ringA[i], d.b_rA[i]
        for (dc, sc, ncol, kch) in parts:
            srcv = wb16[wkey][0:kch * 128, sc:sc + ncol].rearrange("(k p) c -> p k c", p=128)
            tr.op("sp", lambda q, t=t, dc=dc, ncol=ncol, kch=kch, srcv=srcv: q.dma_start(out=t[:, 0:kch, dc:dc + ncol], in_=srcv),
                  r=wbuf[wkey], w=[b], dk="rA%d" % i)
        return t, b

    def ringB_load(d, wkey, c0):
        i = d.iB % 2
        d.iB += 1
        t, b = d.ringB[i], d.b_rB[i]
        srcv = wb16[wkey][:, c0:c0 + 512].rearrange("(k p) c -> p k c", p=128)
        tr.op("sp", lambda q: q.dma_start(out=t[:, :, :], in_=srcv), r=wbuf[wkey], w=[b], dk="rB%d" % i)
        return t, b

    def mm(outap, pairs, r, w):
        def fn(pe):
            ins = None
            n = len(pairs)
            for i, (l, rh) in enumerate(pairs):
                ins = pe.matmul(outap, l, rh, start=(i == 0), stop=(i == n - 1))
            return ins
        return tr.op("pe", fn, r=r, w=w)

    def pipeline(units, stages, skew=1):
        n = len(units)
        for s_ in range(n + (len(stages) - 1) * skew):
            for i, fn in enumerate(stages):
                u = s_ - i * skew
                if 0 <= u < n:
                    fn(units[u])

    def norm_transpose(d, gj):
        for i in range(8):
            tr.op("act", lambda a, i=i: a.activation(out=d.junk[:], in_=d.xt[:, i, :], func=AF.Square, accum_out=d.ssq[:, i:i + 1]),
                  r=[d.b_xtl[i]], w=[d.b_junk, d.b_ssq])
        tr.op("act", lambda a: a.activation(out=d.rstd[:], in_=d.ssq[:], func=AF.Sqrt, bias=EPS, scale=1.0 / D), r=[d.b_ssq], w=[d.b_rstd])
        tr.op("dve", lambda v: v.reciprocal(out=d.rstd[:], in_=d.rstd[:]), r=[d.b_rstd], w=[d.b_rstd])
        for i in range(8):
            xs, bxs = d.xs[i % 2], d.b_xs[i % 2]
            tr.op("act", lambda a, i=i, xs=xs: a.activation(out=xs[:], in_=d.xt[:, i, :], func=AF.Copy, scale=d.rstd[:, i:i + 1]),
                  r=[d.b_xtl[i], d.b_rstd], w=[bxs])
            bk, bb = nbank()
            pt = bk[:].bitcast(BF16)

            def fn(pe, xs=xs, pt=pt):
                ins = None
                for kc in range(8):
                    ins = pe.transpose(out=pt[:, kc * 128:(kc + 1) * 128], in_=xs[:, kc * 128:(kc + 1) * 128], identity=ident)
                return ins
            tr.op("pe", fn, r=[bxs, b_cb], w=[bb])
            def fev(v, i=i, pt=pt):
                ins = None
                for kc in range(8):
                    ins = v.tensor_scalar(out=d.xnT[:, kc, i * 128:(i + 1) * 128], in0=pt[:, kc * 128:(kc + 1) * 128], scalar1=cf[:, gj * 8 + kc:gj * 8 + kc + 1], scalar2=None, op0=ALU.mult)
                return ins
            tr.op("dve", fev, r=[bb, b_cf], w=[d.b_xnT])

    def ffn(d, wg, wu, wd):
        for s in range(NFC // 2):
            i = d.iA % 3
            d.iA += 1
            t, b = d.ringA[i], d.b_rA[i]
            for (wk, c0_) in ((wg, 0), (wu, 256)):
                srcv = wb16[wk][s].rearrange("(k p) c -> p k c", p=128)
                tr.op("sp", lambda q, t=t, srcv=srcv, c0_=c0_: q.dma_start(out=t[:, :, c0_:c0_ + 256], in_=srcv), r=[wbuf[wk][s]], w=[b], dk="rA%d" % i)
            for f in range(2):
                fc = s * 2 + f
                for sub in range(2):
                    tok = slice(sub * 512, (sub + 1) * 512)
                    bg, bbg = nbank()
                    bu, bbu = nbank()
                    mm(bg[:, :], [(t[:, kc, f * 128:(f + 1) * 128], d.xnT[:, kc, tok]) for kc in range(8)], r=[b, d.b_xnT], w=[bbg])
                    mm(bu[:, :], [(t[:, kc, 256 + f * 128:256 + (f + 1) * 128], d.xnT[:, kc, tok]) for kc in range(8)], r=[b, d.b_xnT], w=[bbu])
                    j = d.isg % 3
                    d.isg += 1
                    sg, bsg = d.sg[j], d.b_sg[j]
                    tr.op("act", lambda a, sg=sg, bg=bg: a.activation(out=sg[:], in_=bg[:, :], func=AF.Silu), r=[bbg], w=[bsg])
                    tr.op("dve", lambda v, sg=sg, bu=bu, fc=fc, tok=tok: v.tensor_tensor(out=d.aT[:, fc, tok], in0=sg[:], in1=bu[:, :], op=ALU.mult),
                          r=[bsg, bbu], w=[d.b_aT[fc]])
        for half in range(2):
            t, b = ringB_load(d, wd, half * 512)
            for i in range(8):
                bk, bb = nbank()
                mm(bk[:, :], [(d.aT[:, kc, i * 128:(i + 1) * 128], t[:, kc, :]) for kc in range(NFC)], r=[b] + d.b_aT, w=[bb])
                tr.op("dve", lambda v, i=i, bk=bk, half=half: v.scalar_tensor_tensor(out=d.xt[:, i, half * 512:(half + 1) * 512], in0=bk[:, :], scalar=0.5, in1=d.xt[:, i, half * 512:(half + 1) * 512], op0=ALU.mult, op1=ALU.add),
                      r=[bb, d.b_xtl[i]], w=[d.b_xtl[i]])

    def phase_A(d, st, blk):
        seq = blk // 2
        t0 = (blk % 2) * TB
        g0 = blk * TB
        for i in range(8):
            tr.op("sp", lambda q, i=i: q.dma_start(out=d.xt[:, i, :], in_=x[g0 + i * 128:g0 + (i + 1) * 128, :]), w=[d.b_xtl[i]], dk="xt%d" % i)
        norm_transpose(d, 0)
        ffn(d, "f1g", "f1u", "f1d")
        for i in range(8):
            tr.op("sp", lambda q, i=i: q.dma_start(out=x1_d[g0 + i * 128:g0 + (i + 1) * 128, :], in_=d.xt[:, i, :]), r=[d.b_xtl[i]], dk="x1st%d" % i)
        if stage < 2:
            return
        slot_parts = [[(0, C_FQ, 512, 8)], [(0, C_FK, 512, 8)], [(0, C_DQ, 512, 8)], [(0, C_IQ, 512, 8)], [(0, C_FV, 512, 8)],
                      [(0, C_SM, 136, 8), (256, C_SM + 136, 72, 8)]] + [[(0, C_GA + gs * 512, 512, 8)] for gs in range(4)]
        loaded = {}

        def get_slot(i):
            for k in range(i, min(i + 3, len(slot_parts))):
                if k not in loaded:
                    loaded[k] = ringA_load(d, "win", slot_parts[k])
            return loaded[i]
        get_slot(0)
        norm_transpose(d, 1)
        ist = [0]

        def stage_buf():
            j = ist[0] % 3
            ist[0] += 1
            return st.stg[j], st.b_stg[j], j

        groups = (
            (C_FQ, qT_d, CF_GQ, 1.0, 64.0 * EPS, True),
            (C_FK, kT_d, CF_GK, 1.0 / 64, EPS, True),
            (C_DQ, dqT_d, CF_GDQ, 1.0, 64.0 * EPS, True),
            (C_IQ, iqT_d, None, None, None, False))
        units = []
        for gi, grp in enumerate(groups):
            for c in range(4):
                for sub in range(2):
                    units.append(dict(gi=gi, grp=grp, c=c, sub=sub))
        gstate = {}

        def p1(u):
            (c0, dst, gcol, sq_scale, sq_bias, donorm) = u["grp"]
            c, sub = u["c"], u["sub"]
            if c == 0 and sub == 0:
                gstate[u["gi"]] = get_slot(u["gi"])
            t, b = gstate[u["gi"]]
            if sub == 0:
                gstate[(u["gi"], c)] = stage_buf()
            sg_t, sg_b, j = gstate[(u["gi"], c)]
            tok = slice(sub * 512, (sub + 1) * 512)
            bk, bb = nbank()
            u["bk"], u["bb"] = bk, bb
            mm(bk[:, :], [(t[:, kc, c * 128:(c + 1) * 128], d.xnT[:, kc, tok]) for kc in range(8)], r=[b, d.b_xnT], w=[bb])
            if donorm:
                jj = d.isg % 3
                d.isg += 1
                u["jj"] = jj
                tr.op("act", lambda a: a.activation(out=st.sq[jj][:], in_=bk[:, :], func=AF.Square), r=[bb], w=[st.b_sq[jj]])
            else:
                tr.op("act", lambda a: a.activation(out=sg_t[:, tok], in_=bk[:, :], func=AF.Copy), r=[bb], w=[sg_b])
                if sub == 1:
                    tr.op("sp", lambda q: q.dma_start(out=dst[seq, c * 128:(c + 1) * 128, t0:t0 + TB], in_=sg_t[:, :]), r=[sg_b], dk="stg%d" % j)

        def p2(u):
            (c0, dst, gcol, sq_scale, sq_bias, donorm) = u["grp"]
            if not donorm:
                return
            jj = u["jj"]
            b2, bb2 = nbank()
            mm(b2[:, :], [(blkones, st.sq[jj][:])], r=[st.b_sq[jj], b_cb], w=[bb2])
            rs, brs = d.sg[jj], d.b_sg[jj]
            tr.op("act", lambda a: a.activation(out=rs[:], in_=b2[:, :], func=AF.Ln, bias=sq_bias, scale=sq_scale), r=[bb2], w=[brs])
            tr.op("act", lambda a: a.activation(out=rs[:], in_=rs[:], func=AF.Exp, scale=-0.5), w=[brs])

        def p3(u):
            (c0, dst, gcol, sq_scale, sq_bias, donorm) = u["grp"]
            if not donorm:
                return
            c, sub, jj, bk, bb = u["c"], u["sub"], u["jj"], u["bk"], u["bb"]
            sg_t, sg_b, j = gstate[(u["gi"], c)]
            tok = slice(sub * 512, (sub + 1) * 512)
            rs, brs = d.sg[jj], d.b_sg[jj]
            tr.op("dve", lambda v: v.scalar_tensor_tensor(out=sg_t[:, tok], in0=bk[:, :], scalar=cf[:, gcol:gcol + 1], in1=rs[:], op0=ALU.mult, op1=ALU.mult),
                  r=[bb, brs, b_cf], w=[sg_b])
            if sub == 1:
                tr.op("sp", lambda q: q.dma_start(out=dst[seq, c * 128:(c + 1) * 128, t0:t0 + TB], in_=sg_t[:, :]), r=[sg_b], dk="stg%d" % j)

        pipeline(units, [p1, p2, p3], skew=1)
        if LVL < 2:
            return
        t, b = get_slot(4)
        for i in range(8 if 'fv' in PARTS else 0):
            bk, bb = nbank()
            mm(bk[:, :], [(d.xnT[:, kc, i * 128:(i + 1) * 128], t[:, kc, :]) for kc in range(8)], r=[b, d.b_xnT], w=[bb])
            tr.op("act", lambda a, bk=bk, i=i: a.activation(out=st.vst[:, i, :], in_=bk[:, :], func=AF.Copy), r=[bb], w=[st.b_vst])
        if 'fv' in PARTS:
          tr.op("sp", lambda q: q.dma_start(out=vf_d[seq, t0:t0 + TB, :].rearrange("(i p) c -> p i c", p=128), in_=st.vst[:, :, :]), r=[st.b_vst], dk="vst")
        if LVL < 3:
            return
        t, b = get_slot(5)
        sg_dk, b_dk, jdk = stage_buf()
        sg_ik, b_ik, jik = stage_buf()
        for sub in range(2):
            tok = slice(sub * 512, (sub + 1) * 512)
            bk, bb = nbank()
            mm(bk[0:64, :], [(t[:, kc, 0:64], d.xnT[:, kc, tok]) for kc in range(8)], r=[b, d.b_xnT], w=[bb])
            jj = d.isg % 3
            d.isg += 1
            sq, bsq = st.sq[jj], st.b_sq[jj]
            tr.op("act", lambda a, sq=sq, bk=bk: a.activation(out=sq[0:64, :], in_=bk[0:64, :], func=AF.Square), r=[bb], w=[bsq])
            b2, bb2 = nbank()
            mm(b2[0:64, :], [(blkones[0:64, 0:64], sq[0:64, :])], r=[bsq, b_cb], w=[bb2])
            rs, brs = d.sg[jj], d.b_sg[jj]
            tr.op("act", lambda a, rs=rs, b2=b2: a.activation(out=rs[0:64, :], in_=b2[0:64, :], func=AF.Ln, bias=EPS, scale=1.0 / 64), r=[bb2], w=[brs])
            tr.op("act", lambda a, rs=rs: a.activation(out=rs[0:64, :], in_=rs[0:64, :], func=AF.Exp, scale=-0.5), w=[brs])
            tr.op("dve", lambda v, rs=rs, bk=bk, tok=tok: v.scalar_tensor_tensor(out=sg_dk[0:64, tok], in0=bk[0:64, :], scalar=cf[0:64, CF_GDK:CF_GDK + 1], in1=rs[0:64, :], op0=ALU.mult, op1=ALU.mult),
                  r=[bb, brs, b_cf], w=[b_dk])
            bk, bb = nbank()
            mm(bk[0:64, :], [(t[:, kc, 64:128], d.xnT[:, kc, tok]) for kc in range(8)], r=[b, d.b_xnT], w=[bb])
            tr.op("act", lambda a, bk=bk, tok=tok: a.activation(out=sg_ik[0:64, tok], in_=bk[0:64, :], func=AF.Copy), r=[bb], w=[b_ik])
            bk, bb = nbank()
            mm(bk[0:8, :], [(t[:, kc, 128:136], d.xnT[:, kc, tok]) for kc in range(8)], r=[b, d.b_xnT], w=[bb])
            jf = d.isg % 3
            d.isg += 1
            fst, bfst = d.sg[jf], d.b_sg[jf]
            tr.op("act", lambda a, bk=bk, fst=fst: a.activation(out=fst[0:8, :], in_=bk[0:8, :], func=AF.Copy), r=[bb], w=[bfst])
            tr.op("sp", lambda q, fst=fst, sub=sub: q.dma_start(out=ff_d[seq, :, t0 + sub * 512:t0 + (sub + 1) * 512], in_=fst[0:8, :]), r=[bfst], dk="sgst%d" % jf)
        tr.op("sp", lambda q: q.dma_start(out=dkT_d[seq, :, t0:t0 + TB], in_=sg_dk[0:64, :]), r=[b_dk], dk="stg%d" % jdk)
        tr.op("sp", lambda q: q.dma_start(out=ikT_d[seq, :, t0:t0 + TB], in_=sg_ik[0:64, :]), r=[b_ik], dk="stg%d" % jik)
        if LVL < 4:
            return
        for i in range(8):
            bk, bb = nbank()
            mm(bk[:, 0:128], [(d.xnT[:, kc, i * 128:(i + 1) * 128], t[:, kc, 256:384]) for kc in range(8)], r=[b, d.b_xnT], w=[bb])
            tr.op("act", lambda a, bk=bk, i=i: a.activation(out=st.dvst[:, i, :], in_=bk[:, 0:64], func=AF.Copy), r=[bb], w=[st.b_dvst])
            if 'iw' in PARTS:
                tr.op("dve", lambda v, bk=bk, i=i: v.tensor_copy(out=st.iwst[:, i, :], in_=bk[:, 64:72]), r=[bb], w=[st.b_iwst])
        if 'dvd' in PARTS:
          tr.op("sp", lambda q: q.dma_start(out=vd_d[seq, :, t0 // 128:t0 // 128 + 8, :], in_=st.dvst[:, :, :]), r=[st.b_dvst], dk="dvst")
        if 'iwd' in PARTS:
          tr.op("sp", lambda q: q.dma_start(out=iw_d[seq, :, t0 // 128:t0 // 128 + 8, :], in_=st.iwst[:, :, :]), r=[st.b_iwst], dk="iwst")
        if LVL < 5:
            return
        for gs in range(4):
            t, b = get_slot(6 + gs)
            for c in range(4):
                sg_t, sg_b, j = stage_buf()
                for sub in range(2):
                    tok = slice(sub * 512, (sub + 1) * 512)
                    bk, bb = nbank()
                    mm(bk[:, :], [(t[:, kc, c * 128:(c + 1) * 128], d.xnT[:, kc, tok]) for kc in range(8)], r=[b, d.b_xnT], w=[bb])
                    tr.op("act", lambda a, bk=bk, sg_t=sg_t, tok=tok: a.activation(out=sg_t[:, tok], in_=bk[:, :], func=AF.Sigmoid), r=[bb], w=[sg_b])
                row = gs * 512 + c * 128
                tr.op("sp", lambda q, row=row, sg_t=sg_t: q.dma_start(out=gT_d[seq, row:row + 128, t0:t0 + TB], in_=sg_t[:, :]), r=[sg_b], dk="stg%d" % j)

    class Stg:
        pass

    def alloc_stage(stack):
        st = Stg()
        st.stg = [sb(stack, "stg%d" % i, [128, TB], BF16) for i in range(3)]
        st.b_stg = [Buf("stg%d" % i) for i in range(3)]
        st.sq = [sb(stack, "sq%d" % i, [128, 512], BF16) for i in range(3)]
        st.b_sq = [Buf("sq%d" % i) for i in range(3)]
        st.vst = sb(stack, "vst", [128, 8, 512], BF16)
        st.b_vst = Buf("vst")
        st.dvst = sb(stack, "dvst", [128, 8, 64], BF16)
        st.b_dvst = Buf("dvst")
        st.iwst = sb(stack, "iwst", [128, 8, 8], F32)
        st.b_iwst = Buf("iwst")
        return st

    def phase_C(d, st, blk):
        seq = blk // 2
        t0 = (blk % 2) * TB
        g0 = blk * TB
        for i in range(8):
            tr.op("sp", lambda q, i=i: q.dma_start(out=d.xt[:, i, :], in_=x1_d[g0 + i * 128:g0 + (i + 1) * 128, :]), w=[d.b_xtl[i]], dk="xt%d" % i)
        tr.op("sp", lambda q: q.dma_start(out=st.oa[:, :, :], in_=oaT_d[seq, :, t0:t0 + TB].rearrange("(k p) t -> p k t", p=128)), w=[st.b_oa], dk="oa")
        tr.op("sp", lambda q: q.dma_start(out=st.ob[:, :, :], in_=obT_d[seq, :, t0:t0 + TB].rearrange("(k p) t -> p k t", p=128)), w=[st.b_ob], dk="ob")
        for hb in range(2):
            tr.op("sp", lambda q, hb=hb: q.dma_start(out=d.aT[:, hb * 8:(hb + 1) * 8, :], in_=gT_d[seq, hb * 1024:(hb + 1) * 1024, t0:t0 + TB].rearrange("(k p) t -> p k t", p=128)), w=d.b_aT[hb * 8:(hb + 1) * 8], dk="gt%d" % hb)
        units = [dict(half=half, c=c, sub=sub) for half in range(2) for c in range(4) for sub in range(2)]
        wst = {}

        def m1(u):
            half, c, sub = u["half"], u["c"], u["sub"]
            if c == 0 and sub == 0:
                wst[half] = (ringA_load(d, "wa", [(0, half * 512, 512, 4)]), ringA_load(d, "wb", [(0, half * 512, 512, 4)]))
            (ta, ba), (tb_, bb_) = wst[half]
            tok = slice(sub * 512, (sub + 1) * 512)
            bk1, bb1 = nbank()
            bk2, bb2 = nbank()
            u["b"] = (bk1, bb1, bk2, bb2)
            mm(bk1[:, :], [(ta[:, kc, c * 128:(c + 1) * 128], st.oa[:, kc, tok]) for kc in range(4)], r=[ba, st.b_oa], w=[bb1])
            mm(bk2[:, :], [(tb_[:, kc, c * 128:(c + 1) * 128], st.ob[:, kc, tok]) for kc in range(4)], r=[bb_, st.b_ob], w=[bb2])

        def m2(u):
            half, c, sub = u["half"], u["c"], u["sub"]
            ch = half * 4 + c
            tok = slice(sub * 512, (sub + 1) * 512)
            bk1, bb1, bk2, bb2 = u["b"]
            jj = d.isg % 3
            d.isg += 1
            u["jj"] = jj
            tmp, btmp = d.sg[jj], d.b_sg[jj]
            tr.op("dve", lambda v: v.tensor_tensor(out=tmp[:], in0=bk1[:, :], in1=d.aT[:, ch, tok], op=ALU.mult), r=[bb1, d.b_aT[ch]], w=[btmp])
            tr.op("dve", lambda v: v.tensor_tensor(out=bk2[:, :], in0=bk2[:, :], in1=d.aT[:, 8 + ch, tok], op=ALU.mult), r=[d.b_aT[8 + ch]], w=[bb2])

        def m3(u):
            half, c, sub = u["half"], u["c"], u["sub"]
            ch = half * 4 + c
            tok = slice(sub * 512, (sub + 1) * 512)
            bk1, bb1, bk2, bb2 = u["b"]
            tmp, btmp = d.sg[u["jj"]], d.b_sg[u["jj"]]
            tr.op("dve", lambda v: v.tensor_tensor(out=d.xnT[:, ch, tok], in0=bk2[:, :], in1=tmp[:], op=ALU.add), r=[bb2, btmp], w=[d.b_xnT])

        pipeline(units, [m1, m2, m3], skew=1)
        for half in range(2):
            t, b = ringA_load(d, "wo", [(0, half * 512, 512, 8)])
            for i in range(8):
                bk, bb = nbank()
                mm(bk[:, :], [(d.xnT[:, kc, i * 128:(i + 1) * 128], t[:, kc, :]) for kc in range(8)], r=[b, d.b_xnT], w=[bb])
                tr.op("dve", lambda v, i=i, bk=bk, half=half: v.tensor_tensor(out=d.xt[:, i, half * 512:(half + 1) * 512], in0=bk[:, :], in1=d.xt[:, i, half * 512:(half + 1) * 512], op=ALU.add),
                      r=[bb, d.b_xtl[i]], w=[d.b_xtl[i]])
        norm_transpose(d, 2)
        ffn(d, "f2g", "f2u", "f2d")
        for i in range(8):
            tr.op("sp", lambda q, i=i: q.dma_start(out=out[g0 + i * 128:g0 + (i + 1) * 128, :], in_=d.xt[:, i, :]), r=[d.b_xtl[i]], dk="outst%d" % i)

    class StgC:
        pass

    def alloc_stage_C(stack):
        st = StgC()
        st.oa = sb(stack, "oa", [128, 4, TB], BF16)
        st.ob = sb(stack, "ob", [128, 4, TB], BF16)
        st.b_oa, st.b_ob = Buf("oa"), Buf("ob")
        return st

    def run_fn(e, fn, r, w):
        return tr.op(e, fn, r=r, w=w)

    def phase_B_fox(seq):
        with contextlib.ExitStack() as ph:
            with contextlib.ExitStack() as p0:
                ffl = sb(p0, "ffl", [8, S], F32)
                t1 = sb(p0, "fft1", [8, S], F32)
                t2 = sb(p0, "fft2", [8, S], F32)
                onesr = sb(p0, "onesr", [8, S], F32)
                parts = [sb(p0, "cpart%d" % i, [8, S], BF16) for i in range(6)]
                b_ffl, b_t1, b_t2, b_on = Buf("ffl"), Buf("t1"), Buf("t2"), Buf("onesr")
                b_parts = [Buf("cpart%d" % i) for i in range(6)]
                tr.op("sp", lambda q: q.dma_start(out=ffl[:], in_=ff_d[seq, :, :]), w=[b_ffl], dk="ffl")
                tr.op("dve", lambda v: v.memset(onesr[:], 1.0), w=[b_on])
                tr.op("dve", lambda v: v.tensor_scalar(out=ffl[:], in0=ffl[:], scalar1=cf[0:8, CF_BF:CF_BF + 1], scalar2=None, op0=ALU.add), r=[b_cf], w=[b_ffl])
                tr.op("dve", lambda v: v.tensor_scalar(out=t1[:], in0=ffl[:], scalar1=-1.0, scalar2=None, op0=ALU.mult), r=[b_ffl], w=[b_t1])
                tr.op("dve", lambda v: v.tensor_tensor(out=t1[:], in0=ffl[:], in1=t1[:], op=ALU.min), r=[b_ffl], w=[b_t1])
                tr.op("act", lambda a: a.activation(out=t1[:], in_=t1[:], func=AF.Exp), w=[b_t1])
                tr.op("act", lambda a: a.activation(out=t1[:], in_=t1[:], func=AF.Ln, bias=1.0), w=[b_t1])
                tr.op("dve", lambda v: v.scalar_tensor_tensor(out=t2[:], in0=ffl[:], scalar=0.0, in1=t1[:], op0=ALU.min, op1=ALU.subtract), r=[b_ffl, b_t1], w=[b_t2])
                tr.op("dve", lambda v: v.tensor_tensor_scan(out=ffl[:], data0=onesr[:], data1=t2[:], initial=0.0, op0=ALU.mult, op1=ALU.add), r=[b_on, b_t2], w=[b_ffl])
                tr.op("dve", lambda v: v.tensor_copy(out=parts[0][:], in_=ffl[:]), r=[b_ffl], w=[b_parts[0]])
                tr.op("dve", lambda v: v.tensor_tensor(out=t1[:], in0=ffl[:], in1=parts[0][:], op=ALU.subtract), r=[b_ffl, b_parts[0]], w=[b_t1])
                tr.op("dve", lambda v: v.tensor_copy(out=parts[1][:], in_=t1[:]), r=[b_t1], w=[b_parts[1]])
                tr.op("dve", lambda v: v.tensor_tensor(out=t2[:], in0=t1[:], in1=parts[1][:], op=ALU.subtract), r=[b_t1, b_parts[1]], w=[b_t2])
                tr.op("dve", lambda v: v.tensor_copy(out=parts[2][:], in_=t2[:]), r=[b_t2], w=[b_parts[2]])
                for i in range(3):
                    tr.op("dve", lambda v, i=i: v.tensor_scalar(out=parts[3 + i][:], in0=parts[i][:], scalar1=-1.0, scalar2=None, op0=ALU.mult), r=[b_parts[i]], w=[b_parts[3 + i]])
                b_cumd = Buf("cumd")
                for i in range(6):
                    tr.op("sp", lambda q, i=i: q.dma_start(out=cum_d[seq, :, i, :], in_=parts[i][:]), r=[b_parts[i]], w=[b_cumd], dk="cumst")
                tr.barrier()
            qA = sb(ph, "qA", [128, 8, S], BF16)
            kA = sb(ph, "kA", [128, 8, S], BF16)
            Vf = sb(ph, "Vf", [128, 16, 512], BF16)
            NPT = 4
            PT = [sb(ph, "PT%d" % i, [128, 512], BF16) for i in range(NPT)]
            dtmp = [sb(ph, "dtmp%d" % i, [128, 128], F32) for i in range(2)]
            rec = [sb(ph, "rec%d" % i, [128, 512], F32) for i in range(2)]
            ost = [sb(ph, "ost%d" % i, [128, 512], BF16) for i in range(2)]
            b_qA, b_kA, b_Vf = Buf("qA"), Buf("kA"), Buf("Vf")
            b_PT = [Buf("PT%d" % i) for i in range(NPT)]
            b_dtmp = [Buf("dtmp%d" % i) for i in range(2)]
            b_rec = [Buf("rec%d" % i) for i in range(2)]
            b_ost = [Buf("ost%d" % i) for i in range(2)]
            tr.op("pool", lambda g: g.memset(qA[64:70, :, :], 1.0), w=[b_qA])
            tr.op("pool", lambda g: g.memset(kA[64:70, :, :], 1.0), w=[b_kA])
            tr.op("sp", lambda q: q.dma_start(out=qA[0:64, :, :], in_=qT_d[seq, :, :].rearrange("(h d) s -> d h s", d=64)), w=[b_qA], dk="qA")
            tr.op("sp", lambda q: q.dma_start(out=kA[0:64, :, :], in_=kT_d[seq, :, :].rearrange("(h d) s -> d h s", d=64)), w=[b_kA], dk="kA")
            tr.op("sp", lambda q: q.dma_start(out=qA[64:67, :, :], in_=cum_d[seq, :, 0:3, :].rearrange("h j s -> j h s")), r=[b_cumd], w=[b_qA], dk="qA")
            tr.op("sp", lambda q: q.dma_start(out=kA[67:70, :, :], in_=cum_d[seq, :, 3:6, :].rearrange("h j s -> j h s")), r=[b_cumd], w=[b_kA], dk="kA")
            tr.op("sp", lambda q: q.dma_start(out=Vf[:, :, :], in_=vf_d[seq, :, :].rearrange("(i p) c -> p i c", p=128)), w=[b_Vf], dk="Vf")
            units = []
            npair = 0
            for h in range(8):
                for qb in range(4):
                    nkt = 4 * qb + 4
                    for kt in range(nkt):
                        units.append(dict(h=h, qb=qb, kt=kt, nkt=nkt, pair=npair))
                    npair += 1
            cnt = {"u": 0}

            def stA(u):
                h, qb, kt = u["h"], u["qb"], u["kt"]
                j = kt - 4 * qb
                c0 = max(j, 0) * 128
                i = cnt["u"]
                cnt["u"] += 1
                si = 4 + (i % 4)
                Sb, Sbb = banks[si], bbuf[si]
                pt, bpt = PT[i % NPT], b_PT[i % NPT]
                u["pt"], u["bpt"], u["c0"] = pt, bpt, c0
                mm(Sb[:, c0:512], [(kA[0:70, h, kt * 128:(kt + 1) * 128], qA[0:70, h, qb * 512 + c0:(qb + 1) * 512])], r=[b_kA, b_qA], w=[Sbb])
                if j >= 0:
                    dt_, bdt = dtmp[kt % 2], b_dtmp[kt % 2]
                    tr.op("dve", lambda v: v.tensor_tensor(out=dt_[:], in0=Sb[:, c0:c0 + 128], in1=cf[:, CF_MNEG:CF_MNEG + 128], op=ALU.add), r=[Sbb, b_cf], w=[bdt])
                    tr.op("act", lambda a: a.activation(out=pt[:, c0:c0 + 128], in_=dt_[:], func=AF.Exp), r=[bdt], w=[bpt])
                    if c0 + 128 < 512:
                        tr.op("act", lambda a: a.activation(out=pt[:, c0 + 128:512], in_=Sb[:, c0 + 128:512], func=AF.Exp), r=[Sbb], w=[bpt])
                else:
                    tr.op("act", lambda a: a.activation(out=pt[:, :], in_=Sb[:, :], func=AF.Exp), r=[Sbb], w=[bpt])

            def stB(u):
                h, qb, kt, nkt = u["h"], u["qb"], u["kt"], u["nkt"]
                so = (u["pair"] % 2) * 2
                io = u["pair"] % 2
                Ob, Obb, Db, Dbb = banks[so], bbuf[so], banks[so + 1], bbuf[so + 1]
                pt, bpt, c0 = u["pt"], u["bpt"], u["c0"]

                hp = h // 2
                r0 = (h % 2) * 64

                def fpv(pe):
                    pe.matmul(Ob[:, c0:512], Vf[:, kt, hp * 128:(hp + 1) * 128], pt[:, c0:512], start=(kt == 0), stop=(kt == nkt - 1))
                    return pe.matmul(Db[:, c0:512], ones128, pt[:, c0:512], start=(kt == 0), stop=(kt == nkt - 1))
                tr.op("pe", fpv, r=[bpt, b_Vf, b_cb], w=[Obb, Dbb])
                if kt == nkt - 1:
                    rc, brc = rec[io], b_rec[io]
                    os_, bos = ost[io], b_ost[io]
                    tr.op("dve", lambda v: v.reciprocal(out=rc[r0:r0 + 64, :], in_=Db[r0:r0 + 64, :]), r=[Dbb], w=[brc])
                    tr.op("dve", lambda v: v.tensor_tensor(out=os_[r0:r0 + 64, :], in0=Ob[r0:r0 + 64, :], in1=rc[r0:r0 + 64, :], op=ALU.mult), r=[Obb, brc], w=[bos])
                    tr.op("sp", lambda q: q.dma_start(out=oaT_d[seq, h * 64:(h + 1) * 64, qb * 512:(qb + 1) * 512], in_=os_[r0:r0 + 64, :]), r=[bos], dk="ost%d" % io)

            pipeline(units, [stA, stB], skew=2)
            tr.barrier()

    def phase_B_dsa(seq):
        with contextlib.ExitStack() as ph:
            QQ = sb(ph, "QQ", [128, 8, S], BF16)
            Kz = sb(ph, "Kz", [128, S], BF16)
            Kzi = sb(ph, "Kzi", [128, S], BF16)
            dg = [sb(ph, "dg%d" % i, [128, 8, 128], F32R) for i in range(2)]
            b_dg = [Buf("dg%d" % i) for i in range(2)]
            Vd = sb(ph, "Vd", [128, 16, 128], BF16)
            iwt = sb(ph, "iwt", [128, 16, 8], F32)
            NSC = 7
            NRL = 3
            sc = [sb(ph, "sc%d" % i, [128, S], F32) for i in range(NSC)]
            Mtok = [sb(ph, "Mtok%d" % i, [128, S], BF16) for i in range(4)]
            rl = [sb(ph, "rl%d" % i, [128, 512], F32R) for i in range(NRL)]
            MT = [sb(ph, "MT%d" % i, [128, 16, 512], BF16) for i in range(2)]
            E = [sb(ph, "E%d" % i, [128, 512], BF16) for i in range(3)]
            PT = [sb(ph, "PTd%d" % i, [128, 512], BF16) for i in range(3)]
            MR = [sb(ph, "MR%d" % i, [128, 128], BF16) for i in range(4)]
            osb = [sb(ph, "osb%d" % i, [64, 512], F32) for i in range(2)]
            rec = [sb(ph, "recd%d" % i, [64, 512], F32) for i in range(2)]
            ost = [sb(ph, "ostd%d" % i, [64, 512], BF16) for i in range(2)]
            sm = [sb(ph, "sm%d" % i, [128, 8], F32) for i in range(4)]
            steps = [sb(ph, "steps%d" % i, [128, NITER + 1], F32) for i in range(4)]
            b_dqT, b_iqT, b_dkT, b_ikT, b_Vd, b_iwt = Buf("dqT"), Buf("iqT"), Buf("dkT"), Buf("ikT"), Buf("Vd"), Buf("iwt")
            b_sc = [[Buf("sc%d_%d" % (i, c)) for c in range(4)] for i in range(NSC)]
            b_Mtok = [Buf("Mtok%d" % i) for i in range(4)]
            b_rl = [Buf("rl%d" % i) for i in range(NRL)]
            b_MT = [[Buf("MT%d_%d" % (i, k)) for k in range(4)] for i in range(2)]
            b_E = [Buf("E%d" % i) for i in range(3)]
            b_PT = [Buf("PTd%d" % i) for i in range(3)]
            b_MR = [Buf("MR%d" % i) for i in range(4)]
            b_osb = [Buf("osb%d" % i) for i in range(2)]
            b_rec = [Buf("recd%d" % i) for i in range(2)]
            b_ost = [Buf("ostd%d" % i) for i in range(2)]
            b_sm = [Buf("sm%d" % i) for i in range(4)]
            b_steps = [Buf("steps%d" % i) for i in range(4)]
            tr.op("sp", lambda q: q.dma_start(out=QQ[0:64, :, :], in_=dqT_d[seq, :, :].rearrange("(h d) s -> d h s", d=64)), w=[b_dqT], dk="qA")
            tr.op("sp", lambda q: q.dma_start(out=QQ[64:128, :, :], in_=iqT_d[seq, :, :].rearrange("(h d) s -> d h s", d=64)), w=[b_iqT], dk="kA")
            tr.op("pool", lambda g: g.memset(Kz[:, :], 0.0), w=[b_dkT])
            tr.op("pool", lambda g: g.memset(Kzi[:, :], 0.0), w=[b_ikT])
            tr.op("pool", lambda g: g.memset(Vd[:, :, :], 0.0), w=[b_Vd])
            tr.op("sp", lambda q: q.dma_start(out=Kz[0:64, :], in_=dkT_d[seq, :, :]), w=[b_dkT], dk="dkT")
            tr.op("sp", lambda q: q.dma_start(out=Kzi[64:128, :], in_=ikT_d[seq, :, :]), w=[b_ikT], dk="ikT")
            tr.op("sp", lambda q: q.dma_start(out=Vd[:, :, 0:64], in_=vd_d[seq, :, :, :]), w=[b_Vd], dk="Vf")
            tr.op("sp", lambda q: q.dma_start(out=iwt[:, :, :], in_=iw_d[seq, :, :, :]), w=[b_iwt], dk="iwt")
            cau = cb[:, CB_CAU:CB_CAU + 128]
            state = {"sb": 0, "db": 0, "acc": 0, "dg": 0, "rl": 0, "pt": 0, "mr": 0, "pair": 0}

            def sbank():
                i = 2 + (state["sb"] % 2)
                state["sb"] += 1
                return banks[i], bbuf[i]

            def dbank():
                i = 4 + (state["db"] % 2)
                state["db"] += 1
                return banks[i], bbuf[i]

            tiles_of = {}

            def scores(qb):
                mt, bmt = MT[qb % 2], b_MT[qb % 2]
                tiles = []
                tiles_of[qb] = tiles
                for qq in range(4):
                    qt = 4 * qb + qq
                    cs = slice(qq * 128, (qq + 1) * 128)
                    if qt < 2:
                        if qt == 1:
                            tr.op("pool", lambda g, cs=cs: g.memset(mt[:, 0, cs], 1.0), w=[bmt[qq]])
                        tr.op("pool", lambda g, cs=cs, qt=qt: g.tensor_copy(out=mt[:, qt, cs], in_=cau), r=[b_cb], w=[bmt[qq]])
                        continue
                    n = (qt + 1) * 128
                    nch = (n + 511) // 512
                    tiles.append(dict(qt=qt, qq=qq, cs=cs, n=n, nch=nch, s_=sc[qt % NSC], bsl=b_sc[qt % NSC][0:nch],
                                      smt=sm[qq], bsm=b_sm[qq], stp=steps[qq], bstp=b_steps[qq], mk=Mtok[qq], bmk=b_Mtok[qq]))
                units = []
                for T in tiles:
                    for c4 in range(T["nch"]):
                        for h in range(8):
                            units.append(dict(T=T, h=h, c4=c4, k0=c4 * 512, wdt=min(512, T["n"] - c4 * 512)))

                def s1(u):
                    T, h, k0, wdt = u["T"], u["h"], u["k0"], u["wdt"]
                    qt = T["qt"]
                    if h == 0 and u["c4"] == 0:
                        di = state["dg"] % 2
                        state["dg"] += 1
                        T["di"] = di

                        def fdg(g):
                            ins = None
                            for hh in range(8):
                                ins = g.tensor_scalar(out=dg[di][:, hh, :], in0=cf[:, CF_IDF:CF_IDF + 128], scalar1=iwt[:, qt, hh:hh + 1], scalar2=0.0, op0=ALU.mult, op1=ALU.add)
                            return ins
                        tr.op("pool", fdg, r=[b_cf, b_iwt], w=[b_dg[di]])
                    bk, bb = dbank()
                    mm(bk[:, 0:wdt], [(QQ[:, h, qt * 128:(qt + 1) * 128], Kzi[:, k0:k0 + wdt])], r=[b_iqT, b_dqT, b_ikT], w=[bb])
                    j = state["rl"] % NRL
                    state["rl"] += 1
                    u["j"] = j
                    tr.op("act", lambda a: a.activation(out=rl[j][:, 0:wdt], in_=bk[:, 0:wdt], func=AF.Relu), r=[bb], w=[b_rl[j]])

                def s2(u):
                    T, h, k0, wdt, j = u["T"], u["h"], u["k0"], u["wdt"], u["j"]
                    if h == 0:
                        ai = 6 + (state["acc"] % 2)
                        state["acc"] += 1
                        T["ai"] = ai
                    ai = T["ai"]
                    di = T["di"]
                    tr.op("pe", lambda pe: pe.matmul(banks[ai][:, 0:wdt], dg[di][:, h, :], rl[j][:, 0:wdt], start=(h == 0), stop=(h == 7)), r=[b_rl[j], b_dg[di]], w=[bbuf[ai]])
                    if h == 7:
                        s_ = T["s_"]
                        tr.op("act", lambda a: a.activation(out=s_[:, k0:k0 + wdt], in_=banks[ai][:, 0:wdt], func=AF.Copy), r=[bbuf[ai]], w=[T["bsl"][u["c4"]]])

                pipeline(units, [s1, s2], skew=1)

            def bisect(qb):
                tiles = tiles_of[qb]
                for T in tiles:
                    tr.op("dve", lambda v, T=T: v.tensor_reduce(out=T["smt"][:, 0:1], in_=T["s_"][:, 0:T["n"]], axis=AX.X, op=ALU.min), r=T["bsl"], w=[T["bsm"]])
                for T in tiles:
                    tr.op("dve", lambda v, T=T: v.tensor_reduce(out=T["smt"][:, 1:2], in_=T["s_"][:, 0:T["n"]], axis=AX.X, op=ALU.max), r=T["bsl"], w=[T["bsm"]])
                for T in tiles:
                    tr.op("dve", lambda v, T=T: v.tensor_tensor(out=T["s_"][:, T["n"] - 128:T["n"]], in0=T["s_"][:, T["n"] - 128:T["n"]], in1=cf[:, CF_MNEGT:CF_MNEGT + 128], op=ALU.add), r=[b_cf], w=[T["bsl"][-1]])
                for T in tiles:
                    tr.op("dve", lambda v, T=T: v.tensor_scalar(out=T["smt"][:, 2:3], in0=T["smt"][:, 1:2], scalar1=T["smt"][:, 0:1], scalar2=1.00390625, op0=ALU.subtract, op1=ALU.mult), w=[T["bsm"]])
                for T in tiles:
                    tr.op("dve", lambda v, T=T: v.tensor_scalar(out=T["stp"][:, :], in0=cf[:, CF_POW:CF_POW + NITER + 1], scalar1=T["smt"][:, 2:3], scalar2=0.5, op0=ALU.mult, op1=ALU.mult), r=[T["bsm"], b_cf], w=[T["bstp"]])
                for T in tiles:
                    tr.op("dve", lambda v, T=T: v.tensor_tensor(out=T["smt"][:, 3:4], in0=T["smt"][:, 0:1], in1=T["stp"][:, 0:1], op=ALU.add), r=[T["bstp"]], w=[T["bsm"]])
                for k in range(NITER):
                    for T in tiles:
                        tr.op("dve", lambda v, T=T: v.tensor_scalar(out=T["mk"][:, 0:T["n"]], in0=T["s_"][:, 0:T["n"]], scalar1=T["smt"][:, 3:4], scalar2=None, op0=ALU.is_ge, op1=ALU.add, accum_out=T["smt"][:, 4:5]),
                              r=T["bsl"], w=[T["bsm"], T["bmk"]])
                    for T in tiles:
                        tr.op("dve", lambda v, T=T: v.tensor_scalar(out=T["smt"][:, 5:6], in0=T["smt"][:, 4:5], scalar1=TOPK - 0.5, scalar2=0.5, op0=ALU.is_ge, op1=ALU.subtract), w=[T["bsm"]])
                    for T in tiles:
                        tr.op("dve", lambda v, T=T, k=k: v.scalar_tensor_tensor(out=T["smt"][:, 3:4], in0=T["smt"][:, 5:6], scalar=T["stp"][:, k:k + 1], in1=T["smt"][:, 3:4], op0=ALU.mult, op1=ALU.add), r=[T["bstp"]], w=[T["bsm"]])
                for T in tiles:
                    tr.op("dve", lambda v, T=T: v.tensor_tensor(out=T["smt"][:, 6:7], in0=T["smt"][:, 3:4], in1=T["stp"][:, NITER:NITER + 1], op=ALU.subtract), r=[T["bstp"]], w=[T["bsm"]])
                for T in tiles:
                    tr.op("dve", lambda v, T=T: v.tensor_scalar(out=T["mk"][:, 0:T["n"]], in0=T["s_"][:, 0:T["n"]], scalar1=T["smt"][:, 6:7], scalar2=None, op0=ALU.is_ge), r=T["bsl"] + [T["bsm"]], w=[T["bmk"]])

            def transp(qb):
                mt, bmt = MT[qb % 2], b_MT[qb % 2]
                for T in tiles_of[qb]:
                    qt, mk, cs, qq = T["qt"], T["mk"], T["cs"], T["qq"]
                    for g0 in range(0, qt + 1, 8):
                        g1 = min(qt + 1, g0 + 8)
                        bk, bb = dbank()
                        ptb = bk[:].bitcast(BF16)

                        def ftr(pe, g0=g0, g1=g1, ptb=ptb, mk=mk):
                            ins = None
                            for kt in range(g0, g1):
                                ins = pe.transpose(out=ptb[:, (kt - g0) * 128:(kt - g0 + 1) * 128], in_=mk[:, kt * 128:(kt + 1) * 128], identity=ident)
                            return ins
                        tr.op("pe", ftr, r=[T["bmk"], b_cb], w=[bb])
                        tr.op("act", lambda a, g0=g0, g1=g1, ptb=ptb, cs=cs: a.activation(out=mt[:, g0:g1, cs], in_=ptb[:, 0:(g1 - g0) * 128].rearrange("p (g t) -> p g t", t=128), func=AF.Copy),
                              r=[bb], w=[bmt[qq]])

            def heads(qb):
                mt, bmt = MT[qb % 2], b_MT[qb % 2]
                nkt = 4 * qb + 4
                units = []
                for h in range(8):
                    for kt in range(nkt):
                        units.append(dict(h=h, kt=kt, pair=state["pair"]))
                    state["pair"] += 1

                def hA(u):
                    h, kt = u["h"], u["kt"]
                    j = kt - 4 * qb
                    c0 = max(j, 0) * 128
                    Sb, Sbb = sbank()
                    ip = state["pt"] % 3
                    state["pt"] += 1
                    u["ip"], u["c0"], u["j"] = ip, c0, j
                    e_, be = E[ip], b_E[ip]
                    mm(Sb[:, c0:512], [(Kz[:, kt * 128:(kt + 1) * 128], QQ[:, h, qb * 512 + c0:(qb + 1) * 512])], r=[b_dkT, b_dqT, b_iqT], w=[Sbb])
                    tr.op("act", lambda a: a.activation(out=e_[:, c0:512], in_=Sb[:, c0:512], func=AF.Exp, bias=cf[:, CF_RB31 + h:CF_RB31 + h + 1]), r=[Sbb, b_cf], w=[be])

                def hB(u):
                    h, kt, ip, c0, j = u["h"], u["kt"], u["ip"], u["c0"], u["j"]
                    e_, be = E[ip], b_E[ip]
                    pt, bpt = PT[ip], b_PT[ip]
                    near = []
                    if j >= 0:
                        near.append((c0, 0))
                        if c0 + 128 < 512:
                            near.append((c0 + 128, 128))
                    elif j == -1:
                        near.append((0, 128))
                    cfar = (near[-1][0] + 128) if near else 0
                    rds = [bmt[c // 128] for c in range(c0, 512, 128)]
                    for (cn, ro) in near:
                        im = state["mr"] % 4
                        state["mr"] += 1
                        tr.op("pool", lambda g, im=im, cn=cn, ro=ro: g.tensor_tensor(out=MR[im][:], in0=mt[:, kt, cn:cn + 128], in1=Rt[:, h, ro:ro + 128], op=ALU.mult), r=[bmt[cn // 128], b_Rt], w=[b_MR[im]])
                        tr.op("pool", lambda g, im=im, cn=cn: g.tensor_tensor(out=pt[:, cn:cn + 128], in0=e_[:, cn:cn + 128], in1=MR[im][:], op=ALU.mult), r=[be, b_MR[im]], w=[bpt])
                    if cfar < 512:
                        tr.op("dve" if qb == 3 else "pool", lambda g: g.tensor_tensor(out=pt[:, cfar:512], in0=e_[:, cfar:512], in1=mt[:, kt, cfar:512], op=ALU.mult), r=[be] + rds, w=[bpt])

                def hC(u):
                    h, kt, ip, c0 = u["h"], u["kt"], u["ip"], u["c0"]
                    pt, bpt = PT[ip], b_PT[ip]
                    so = (u["pair"] % 2) * 4
                    io = u["pair"] % 2
                    Ob, Obb, Db, Dbb = banks[so], bbuf[so], banks[so + 1], bbuf[so + 1]

                    def fpv(pe):
                        pe.matmul(Ob[:, c0:512], Vd[:, kt, :], pt[:, c0:512], start=(kt == 0), stop=(kt == nkt - 1))
                        return pe.matmul(Db[:, c0:512], ones128, pt[:, c0:512], start=(kt == 0), stop=(kt == nkt - 1))
                    tr.op("pe", fpv, r=[bpt, b_Vd, b_cb], w=[Obb, Dbb])
                    if kt == nkt - 1:
                        tr.op("act", lambda a: a.activation(out=rec[io][:], in_=Db[0:64, :], func=AF.Ln), r=[Dbb], w=[b_rec[io]])
                        tr.op("act", lambda a: a.activation(out=osb[io][:], in_=Ob[0:64, :], func=AF.Copy), r=[Obb], w=[b_osb[io]])
                        tr.op("act", lambda a: a.activation(out=rec[io][:], in_=rec[io][:], func=AF.Exp, scale=-1.0), w=[b_rec[io]])
                        tr.op("pool", lambda g: g.tensor_tensor(out=ost[io][:], in0=osb[io][:], in1=rec[io][:], op=ALU.mult), r=[b_osb[io], b_rec[io]], w=[b_ost[io]])
                        tr.op("sp", lambda q: q.dma_start(out=obT_d[seq, h * 64:(h + 1) * 64, qb * 512:(qb + 1) * 512], in_=ost[io][:]), r=[b_ost[io]], dk="ostd%d" % io)

                pipeline(units, [hA, hB, hC], skew=1)

            scores(0)
            bisect(0)
            scores(1)
            transp(0)
            bisect(1)
            heads(0)
            scores(2)
            transp(1)
            bisect(2)
            heads(1)
            scores(3)
            transp(2)
            bisect(3)
            heads(2)
            transp(3)
            heads(3)
            tr.barrier()

    for seq in range(NSEQ):
        with contextlib.ExitStack() as ph:
            d = alloc_dense(ph)
            st = alloc_stage(ph)
            for blk in (2 * seq, 2 * seq + 1):
                phase_A(d, st, blk)
            tr.barrier()
        if stage < 3:
            continue
        if stage >= 4:
            phase_B_fox(seq)
        if stage >= 5:
            phase_B_dsa(seq)
        if stage in (4, 5):
            continue
        with contextlib.ExitStack() as ph:
            d = alloc_dense(ph)
            st = alloc_stage_C(ph)
            for blk in (2 * seq, 2 * seq + 1):
                phase_C(d, st, blk)
            tr.barrier()
    tr.barrier()
    es.close()
    return nc


def _t5_bucket_table(n):
    rel = np.arange(n, dtype=np.int32)
    relf = np.maximum(rel, 1).astype(np.float32)
    large = 16 + (np.log(relf / np.float32(16)) / np.float32(np.log(128 / 16)) * np.float32(16)).astype(np.int32)
    large = np.minimum(large, 31)
    return np.where(rel < 16, rel, large)


def _consts(inp):
    cf = np.zeros((128, NCF), np.float32)
    for c0, k in ((CF_G1, "ffn1_norm"), (CF_GM, "mix_norm"), (CF_G2, "ffn2_norm")):
        cf[:, c0:c0 + 8] = np.asarray(inp[k], np.float32).reshape(8, 128).T
    cf[:, CF_GQ] = np.tile(np.asarray(inp["fox_q_norm"], np.float32), 2)
    cf[:, CF_GK] = np.tile(np.asarray(inp["fox_k_norm"], np.float32), 2)
    cf[:, CF_GDQ] = np.tile(np.asarray(inp["dsa_q_norm"], np.float32), 2)
    cf[:, CF_GDK] = np.tile(np.asarray(inp["dsa_k_norm"], np.float32), 2)
    cf[0:8, CF_BF] = np.asarray(inp["b_forget"], np.float32)
    rb = np.asarray(inp["rel_bias"], np.float32)
    cf[:, CF_RB31:CF_RB31 + 8] = rb[31][None, :]
    cf[:, CF_POW:CF_POW + NITER + 1] = (0.5 ** np.arange(NITER + 1, dtype=np.float64))[None, :].astype(np.float32)
    s = np.arange(128)
    cf[:, CF_MNEG:CF_MNEG + 128] = np.where(s[:, None] <= s[None, :], 0.0, NEG)
    cf[:, CF_MNEGT:CF_MNEGT + 128] = np.where(s[None, :] <= s[:, None], 0.0, NEG)
    tp = np.arange(256)
    delta = np.maximum(tp[None, :] - s[:, None], 0)
    bt = _t5_bucket_table(512)[delta]
    g = rb[bt]
    cf[:, CF_GT:CF_GT + 2048] = np.transpose(g, (0, 2, 1)).reshape(128, 2048)
    cf[:, CF_IDF:CF_IDF + 128] = np.eye(128)
    cb = np.zeros((128, NCB), np.float32)
    cb[:, CB_ID:CB_ID + 128] = np.eye(128)
    cb[:, CB_BLK:CB_BLK + 128] = (s[:, None] // 64 == s[None, :] // 64)
    cb[:, CB_ONE:CB_ONE + 64] = 1.0
    cb[:, CB_ONE2:CB_ONE2 + 128] = 1.0
    cb[:, CB_CAU:CB_CAU + 128] = (s[:, None] <= s[None, :])
    return cf, cb.astype(ml_dtypes.bfloat16)


def _perm_win(w_in):
    w = np.asarray(w_in, np.float32)
    o = {}
    off = 0
    for name, wd in (("fq", 512), ("fk", 512), ("fv", 512), ("ff", 8), ("dq", 512), ("dk", 64), ("dv", 64),
                     ("iq", 512), ("ik", 64), ("iw", 8), ("ga", 1024), ("gb", 1024)):
        o[name] = w[:, off:off + wd]
        off += wd
    return np.ascontiguousarray(np.concatenate([o[k] for k in ("fq", "fk", "dq", "iq", "fv", "dk", "ik", "ff", "dv", "iw", "ga", "gb")], axis=1))


def make_in_maps(inp, n_cores=8):
    f = lambda k: np.ascontiguousarray(np.asarray(inp[k], np.float32))
    cf, cb = _consts(inp)
    slotmajor = lambda k: np.ascontiguousarray(np.asarray(inp[k], np.float32).reshape(D, NFC // 2, 256).transpose(1, 0, 2))
    shared = {
        "f1g": slotmajor("ffn1_w_gate"), "f1u": slotmajor("ffn1_w_up"), "f1d": f("ffn1_w_down"),
        "win": _perm_win(inp["w_in"]), "wa": f("w_branch_a"), "wb": f("w_branch_b"), "wo": f("w_out"),
        "f2g": slotmajor("ffn2_w_gate"), "f2u": slotmajor("ffn2_w_up"), "f2d": f("ffn2_w_down"),
        "cf": cf, "cb": cb,
    }
    x = np.asarray(inp["x"], np.float32)
    maps = []
    for c in range(n_cores):
        m = dict(shared)
        m["x"] = np.ascontiguousarray(x[NSEQ * c:NSEQ * (c + 1)].reshape(NSEQ * S, D))
        maps.append(m)
    return maps


def kernel(**inputs):
    nc = bass.Bass("TRN2", target_bir_lowering=False)
    build(nc)
    maps = make_in_maps(inputs, 8)
    res = run_bass_kernel_spmd(nc, maps, core_ids=list(range(8)))
    outs = [np.asarray(r["out"], np.float32).reshape(NSEQ, S, D) for r in res.results]
    return np.concatenate(outs, axis=0)
```

```python
import contextlib
import os
PARTS = os.environ.get('DBG_PARTS', 'qk,fv,small,small2,gates,iw,iwd,dvd').split(',')
LVL = int(os.environ.get('DBG_LVL', '9'))
LVLB = int(os.environ.get('DBG_LVLB', '9'))
import numpy as np
import ml_dtypes
import concourse.bass as bass
import concourse.mybir as mybir
from concourse.bass_utils import run_bass_kernel_spmd

F32 = mybir.dt.float32
BF16 = mybir.dt.bfloat16
F32R = mybir.dt.float32r
AF = mybir.ActivationFunctionType
ALU = mybir.AluOpType
AX = mybir.AxisListType

D = 1024
S = 2048
NSEQ = 2
DFF = 2816
NFC = DFF // 128
TB = 1024
NITER = 16
TOPK = 256
EPS = 1e-6
NEG = -1e30

C_FQ, C_FK, C_DQ, C_IQ, C_FV = 0, 512, 1024, 1536, 2048
C_SM = 2560
C_GA = C_SM + 208
C_GB = C_GA + 1024
NIN = C_GB + 1024

CF_G1, CF_GM, CF_G2 = 0, 8, 16
CF_GQ, CF_GK, CF_GDQ, CF_GDK, CF_BF = 24, 25, 26, 27, 28
CF_RB31 = 29
CF_POW = 37
CF_MNEG = 64
CF_MNEGT = 192
CF_GT = 320
CF_IDF = CF_GT + 8 * 256
NCF = CF_IDF + 128
CB_ID, CB_BLK, CB_ONE, CB_CAU, CB_ONE2 = 0, 128, 256, 320, 448
NCB = 576


class Buf:
    __slots__ = ("name", "w", "rs", "excl")

    def __init__(self, name, excl=False):
        self.name = name
        self.excl = excl
        self.w = None
        self.rs = {}


class Trk:
    def __init__(self, nc, es):
        self.nc = nc
        self.es = es
        self.eng = {"pe": nc.tensor, "act": nc.scalar, "dve": nc.vector, "pool": nc.gpsimd, "sp": nc.sync}
        self.sem = {k: es.enter_context(nc.semaphore("s_" + k)) for k in ("pe", "act", "dve", "pool")}
        self.cnt = {k: 0 for k in self.sem}
        self.waited = {k: {} for k in self.eng}
        self.dsem = {}
        self.dcnt = {}
        self.nwaits = 0

    def _wait(self, e, tok):
        sem, val, key, src, isdma = tok
        if src == e and e == "pe" and not isdma:
            return
        w = self.waited[e]
        if w.get(key, 0) >= val:
            return
        w[key] = val
        self.nwaits += 1
        self.eng[e].wait_ge(sem, val)

    def op(self, e, fn, r=(), w=(), dk=None):
        w = list(w) + [b for b in r if b.excl]
        r = [b for b in r if not b.excl]
        for b in r:
            if b.w is not None:
                self._wait(e, b.w)
        for b in w:
            if b.w is not None:
                self._wait(e, b.w)
            for t in b.rs.values():
                self._wait(e, t)
        ins = fn(self.eng[e])
        if dk is not None:
            if dk not in self.dsem:
                self.dsem[dk] = self.es.enter_context(self.nc.semaphore("d_" + dk))
                self.dcnt[dk] = 0
            self.dcnt[dk] += 16
            ins.then_inc(self.dsem[dk], 16)
            tok = (self.dsem[dk], self.dcnt[dk], "d_" + dk, e, True)
        else:
            self.cnt[e] += 1
            ins.then_inc(self.sem[e], 1)
            tok = (self.sem[e], self.cnt[e], "s_" + e, e, False)
        for b in r:
            b.rs[tok[2]] = tok
        for b in w:
            b.w = tok
            b.rs = {}
        return tok

    def barrier(self):
        toks = [(self.sem[k], self.cnt[k], "s_" + k, k, False) for k in self.sem if self.cnt[k] > 0]
        toks += [(self.dsem[k], self.dcnt[k], "d_" + k, None, True) for k in self.dsem]
        for e in self.eng:
            for t in toks:
                self._wait(e, t)


def build(nc, dbg=False, stage=99):
    es = contextlib.ExitStack()
    tr = Trk(nc, es)
    OUTK = "ExternalOutput" if dbg else "Internal"

    def din(name, shape, dt=F32):
        return nc.dram_tensor(name, shape, dt, kind="ExternalInput").ap()

    def dscr(name, shape, dt, kind=None):
        return nc.dram_tensor(name, shape, dt, kind=kind or "Internal").ap()

    x = din("x", [NSEQ * S, D])
    wsrc = {
        "f1g": din("f1g", [NFC // 2, D, 256]), "f1u": din("f1u", [NFC // 2, D, 256]), "f1d": din("f1d", [DFF, D]),
        "win": din("win", [D, NIN]), "wa": din("wa", [512, D]), "wb": din("wb", [512, D]),
        "wo": din("wo", [D, D]),
        "f2g": din("f2g", [NFC // 2, D, 256]), "f2u": din("f2u", [NFC // 2, D, 256]), "f2d": din("f2d", [DFF, D]),
    }
    cf_d = din("cf", [128, NCF])
    cb_d = din("cb", [128, NCB], BF16)
    out = nc.dram_tensor("out", [NSEQ * S, D], F32, kind="ExternalOutput").ap()

    wb16 = {k: dscr(k + "_b", list(v.shape), BF16) for k, v in wsrc.items()}
    wbuf = {k: [] for k in wsrc}
    NCAST = 2
    castslot = [Buf("cast%d" % i) for i in range(NCAST)]
    ncast = [0]
    x1_d = dscr("x1_d", [NSEQ * S, D], F32, OUTK)
    qT_d = dscr("qT_d", [NSEQ, 512, S], BF16, OUTK)
    kT_d = dscr("kT_d", [NSEQ, 512, S], BF16, OUTK)
    dqT_d = dscr("dqT_d", [NSEQ, 512, S], BF16, OUTK)
    iqT_d = dscr("iqT_d", [NSEQ, 512, S], BF16, OUTK)
    dkT_d = dscr("dkT_d", [NSEQ, 64, S], BF16, OUTK)
    ikT_d = dscr("ikT_d", [NSEQ, 64, S], BF16, OUTK)
    vf_d = dscr("vf_d", [NSEQ, S, 512], BF16, OUTK)
    vd_d = dscr("vd_d", [NSEQ, 128, 16, 64], BF16, OUTK)
    ff_d = dscr("ff_d", [NSEQ, 8, S], F32, OUTK)
    iw_d = dscr("iw_d", [NSEQ, 128, 16, 8], F32, OUTK)
    gT_d = dscr("gT_d", [NSEQ, 2048, S], BF16, OUTK)
    cum_d = dscr("cum_d", [NSEQ, 8, 6, S], BF16, OUTK)
    OK_O = "ExternalInput" if (dbg and stage == 3) else OUTK
    oaT_d = dscr("oaT_d", [NSEQ, 512, S], BF16, OK_O)
    obT_d = dscr("obT_d", [NSEQ, 512, S], BF16, OK_O)

    uid = [0]

    def sb(stack, name, shape, dt):
        uid[0] += 1
        return stack.enter_context(nc.sbuf_tensor("%s_s%d" % (name, uid[0]), shape, dt))

    cf = sb(es, "cf", [128, NCF], F32)
    cb = sb(es, "cb", [128, NCB], BF16)
    nrb = sb(es, "nrb", [128, 8], F32)
    Rt = sb(es, "Rt", [128, 8, 256], BF16)
    b_Rt = Buf("Rt")
    banks = [es.enter_context(nc.psum_tensor("bank%d" % i, [128, 512], F32)) for i in range(8)]
    bbuf = [Buf("bank%d" % i, excl=True) for i in range(8)]
    bstate = {"i": 0}

    def nbank():
        i = bstate["i"]
        bstate["i"] = (i + 1) % 8
        return banks[i], bbuf[i]

    b_cf, b_cb, b_nrb = Buf("cf"), Buf("cb"), Buf("nrb")
    ident = cb[:, CB_ID:CB_ID + 128]
    blkones = cb[:, CB_BLK:CB_BLK + 128]
    ones64 = cb[:, CB_ONE:CB_ONE + 64]
    ones128 = cb[:, CB_ONE2:CB_ONE2 + 128]

    tr.op("sp", lambda q: q.dma_start(out=cf[:], in_=cf_d[:, :]), w=[b_cf], dk="cf")
    tr.op("sp", lambda q: q.dma_start(out=cb[:], in_=cb_d[:, :]), w=[b_cb], dk="cb")
    def cast_dma(k, srcv, dstv, tag):
        cbuf = Buf("wc_%s_%s" % (k, tag))
        wbuf[k].append(cbuf)
        ci = ncast[0] % NCAST
        ncast[0] += 1
        tr.op("pool", lambda q: q.dma_start(out=dstv, in_=srcv), w=[cbuf, castslot[ci]], dk="cw%d" % ci)

    def cast_slots(kg, ku):
        for s_ in range(NFC // 2):
            for k in (kg, ku):
                cast_dma(k, wsrc[k][s_].rearrange("(a b) c -> a (b c)", b=8), wb16[k][s_].rearrange("(a b) c -> a (b c)", b=8), "s%d" % s_)

    def cast_flat(k):
        src = wsrc[k]
        n = src.shape[0] * src.shape[1]
        rows = n // 2048
        s2 = src.rearrange("a b -> (a b)").rearrange("(r c) -> r c", c=2048)
        d2 = wb16[k].rearrange("a b -> (a b)").rearrange("(r c) -> r c", c=2048)
        step = 704 if rows % 704 == 0 else (rows if rows <= 704 else 602)
        assert rows % step == 0, (k, rows, step)
        for r0 in range(0, rows, step):
            cast_dma(k, s2[r0:r0 + step, :], d2[r0:r0 + step, :], "r%d" % r0)

    cast_slots("f1g", "f1u")
    for k in ("f1d", "win", "wa", "wb", "wo"):
        cast_flat(k)
    cast_slots("f2g", "f2u")
    cast_flat("f2d")
    tr.op("dve", lambda v: v.tensor_scalar(out=nrb[:], in0=cf[:, CF_RB31:CF_RB31 + 8], scalar1=-1.0, scalar2=None, op0=ALU.mult), r=[b_cf], w=[b_nrb])
    for h in range(8):
        tr.op("act", lambda a, h=h: a.activation(out=Rt[:, h, :], in_=cf[:, CF_GT + h * 256:CF_GT + (h + 1) * 256], func=AF.Exp, bias=nrb[:, h:h + 1]), r=[b_cf, b_nrb], w=[b_Rt])

    class Dense:
        pass

    def alloc_dense(stack):
        d = Dense()
        d.xt = sb(stack, "xt", [128, 8, D], F32)
        d.xnT = sb(stack, "xnT", [128, 8, TB], BF16)
        d.aT = sb(stack, "aT", [128, NFC, TB], BF16)
        d.ringA = [sb(stack, "rA%d" % i, [128, 8, 512], BF16) for i in range(3)]
        d.ringB = [sb(stack, "rB%d" % i, [128, NFC, 512], BF16) for i in range(2)]
        d.sg = [sb(stack, "sg%d" % i, [128, 512], F32) for i in range(3)]
        d.xs = [sb(stack, "xs%d" % i, [128, D], BF16) for i in range(2)]
        d.junk = sb(stack, "junk", [128, D], BF16)
        d.ssq = sb(stack, "ssq", [128, 8], F32)
        d.rstd = sb(stack, "rstd", [128, 8], F32)
        d.b_xtl, d.b_xnT = [Buf("xt%d" % i) for i in range(8)], Buf("xnT")
        d.b_aT = [Buf("aT%d" % i) for i in range(NFC)]
        d.b_rA = [Buf("rA%d" % i) for i in range(3)]
        d.b_rB = [Buf("rB%d" % i) for i in range(2)]
        d.b_sg = [Buf("sg%d" % i) for i in range(3)]
        d.b_xs = [Buf("xs%d" % i) for i in range(2)]
        d.b_junk, d.b_ssq, d.b_rstd = Buf("junk"), Buf("ssq"), Buf("rstd")
        d.iA = 0
        d.iB = 0
        d.isg = 0
        return d

    def ringA_load(d, wkey, parts):
        i = d.iA % 3
        d.iA += 1
        t, b = d.ringA[i], d.b_rA[i]
        for (dc, sc, ncol, kch) in parts:
            srcv = wb16[wkey][0:kch * 128, sc:sc + ncol].rearrange("(k p) c -> p k c", p=128)
            tr.op("sp", lambda q, t=t, dc=dc, ncol=ncol, kch=kch, srcv=srcv: q.dma_start(out=t[:, 0:kch, dc:dc + ncol], in_=srcv),
                  r=wbuf[wkey], w=[b], dk="rA%d" % i)
        return t, b

    def ringB_load(d, wkey, c0):
        i = d.iB % 2
        d.iB += 1
        t, b = d.ringB[i], d.b_rB[i]
        srcv = wb16[wkey][:, c0:c0 + 512].rearrange("(k p) c -> p k c", p=128)
        tr.op("sp", lambda q: q.dma_start(out=t[:, :, :], in_=srcv), r=wbuf[wkey], w=[b], dk="rB%d" % i)
        return t, b

    def mm(outap, pairs, r, w):
        def fn(pe):
            ins = None
            n = len(pairs)
            for i, (l, rh) in enumerate(pairs):
                ins = pe.matmul(outap, l, rh, start=(i == 0), stop=(i == n - 1))
            return ins
        return tr.op("pe", fn, r=r, w=w)

    def pipeline(units, stages, skew=1):
        n = len(units)
        for s_ in range(n + (len(stages) - 1) * skew):
            for i, fn in enumerate(stages):
                u = s_ - i * skew
                if 0 <= u < n:
                    fn(units[u])

    def norm_transpose(d, gj):
        for i in range(8):
            tr.op("act", lambda a, i=i: a.activation(out=d.junk[:], in_=d.xt[:, i, :], func=AF.Square, accum_out=d.ssq[:, i:i + 1]),
                  r=[d.b_xtl[i]], w=[d.b_junk, d.b_ssq])
        tr.op("act", lambda a: a.activation(out=d.rstd[:], in_=d.ssq[:], func=AF.Sqrt, bias=EPS, scale=1.0 / D), r=[d.b_ssq], w=[d.b_rstd])
        tr.op("dve", lambda v: v.reciprocal(out=d.rstd[:], in_=d.rstd[:]), r=[d.b_rstd], w=[d.b_rstd])
        for i in range(8):
            xs, bxs = d.xs[i % 2], d.b_xs[i % 2]
            tr.op("act", lambda a, i=i, xs=xs: a.activation(out=xs[:], in_=d.xt[:, i, :], func=AF.Copy, scale=d.rstd[:, i:i + 1]),
                  r=[d.b_xtl[i], d.b_rstd], w=[bxs])
            bk, bb = nbank()
            pt = bk[:].bitcast(BF16)

            def fn(pe, xs=xs, pt=pt):
                ins = None
                for kc in range(8):
                    ins = pe.transpose(out=pt[:, kc * 128:(kc + 1) * 128], in_=xs[:, kc * 128:(kc + 1) * 128], identity=ident)
                return ins
            tr.op("pe", fn, r=[bxs, b_cb], w=[bb])
            def fev(v, i=i, pt=pt):
                ins = None
                for kc in range(8):
                    ins = v.tensor_scalar(out=d.xnT[:, kc, i * 128:(i + 1) * 128], in0=pt[:, kc * 128:(kc + 1) * 128], scalar1=cf[:, gj * 8 + kc:gj * 8 + kc + 1], scalar2=None, op0=ALU.mult)
                return ins
            tr.op("dve", fev, r=[bb, b_cf], w=[d.b_xnT])

    def ffn(d, wg, wu, wd):
        for s in range(NFC // 2):
            i = d.iA % 3
            d.iA += 1
            t, b = d.ringA[i], d.b_rA[i]
            for (wk, c0_) in ((wg, 0), (wu, 256)):
                srcv = wb16[wk][s].rearrange("(k p) c -> p k c", p=128)
                tr.op("sp", lambda q, t=t, srcv=srcv, c0_=c0_: q.dma_start(out=t[:, :, c0_:c0_ + 256], in_=srcv), r=[wbuf[wk][s]], w=[b], dk="rA%d" % i)
            for f in range(2):
                fc = s * 2 + f
                for sub in range(2):
                    tok = slice(sub * 512, (sub + 1) * 512)
                    bg, bbg = nbank()
                    bu, bbu = nbank()
                    mm(bg[:, :], [(t[:, kc, f * 128:(f + 1) * 128], d.xnT[:, kc, tok]) for kc in range(8)], r=[b, d.b_xnT], w=[bbg])
                    mm(bu[:, :], [(t[:, kc, 256 + f * 128:256 + (f + 1) * 128], d.xnT[:, kc, tok]) for kc in range(8)], r=[b, d.b_xnT], w=[bbu])
                    j = d.isg % 3
                    d.isg += 1
                    sg, bsg = d.sg[j], d.b_sg[j]
                    tr.op("act", lambda a, sg=sg, bg=bg: a.activation(out=sg[:], in_=bg[:, :], func=AF.Silu), r=[bbg], w=[bsg])
                    tr.op("dve", lambda v, sg=sg, bu=bu, fc=fc, tok=tok: v.tensor_tensor(out=d.aT[:, fc, tok], in0=sg[:], in1=bu[:, :], op=ALU.mult),
                          r=[bsg, bbu], w=[d.b_aT[fc]])
        for half in range(2):
            t, b = ringB_load(d, wd, half * 512)
            for i in range(8):
                bk, bb = nbank()
                mm(bk[:, :], [(d.aT[:, kc, i * 128:(i + 1) * 128], t[:, kc, :]) for kc in range(NFC)], r=[b] + d.b_aT, w=[bb])
                tr.op("dve", lambda v, i=i, bk=bk, half=half: v.scalar_tensor_tensor(out=d.xt[:, i, half * 512:(half + 1) * 512], in0=bk[:, :], scalar=0.5, in1=d.xt[:, i, half * 512:(half + 1) * 512], op0=ALU.mult, op1=ALU.add),
                      r=[bb, d.b_xtl[i]], w=[d.b_xtl[i]])

    def phase_A(d, st, blk):
        seq = blk // 2
        t0 = (blk % 2) * TB
        g0 = blk * TB
        for i in range(8):
            tr.op("sp", lambda q, i=i: q.dma_start(out=d.xt[:, i, :], in_=x[g0 + i * 128:g0 + (i + 1) * 128, :]), w=[d.b_xtl[i]], dk="xt%d" % i)
        norm_transpose(d, 0)
        ffn(d, "f1g", "f1u", "f1d")
        for i in range(8):
            tr.op("sp", lambda q, i=i: q.dma_start(out=x1_d[g0 + i * 128:g0 + (i + 1) * 128, :], in_=d.xt[:, i, :]), r=[d.b_xtl[i]], dk="x1st%d" % i)
        if stage < 2:
            return
        slot_parts = [[(0, C_FQ, 512, 8)], [(0, C_FK, 512, 8)], [(0, C_DQ, 512, 8)], [(0, C_IQ, 512, 8)], [(0, C_FV, 512, 8)],
                      [(0, C_SM, 136, 8), (256, C_SM + 136, 72, 8)]] + [[(0, C_GA + gs * 512, 512, 8)] for gs in range(4)]
        loaded = {}

        def get_slot(i):
            for k in range(i, min(i + 3, len(slot_parts))):
                if k not in loaded:
                    loaded[k] = ringA_load(d, "win", slot_parts[k])
            return loaded[i]
        get_slot(0)
        norm_transpose(d, 1)
        ist = [0]

        def stage_buf():
            j = ist[0] % 3
            ist[0] += 1
            return st.stg[j], st.b_stg[j], j

        groups = (
            (C_FQ, qT_d, CF_GQ, 1.0, 64.0 * EPS, True),
            (C_FK, kT_d, CF_GK, 1.0 / 64, EPS, True),
            (C_DQ, dqT_d, CF_GDQ, 1.0, 64.0 * EPS, True),
            (C_IQ, iqT_d, None, None, None, False))
        units = []
        for gi, grp in enumerate(groups):
            for c in range(4):
                for sub in range(2):
                    units.append(dict(gi=gi, grp=grp, c=c, sub=sub))
        gstate = {}

        def p1(u):
            (c0, dst, gcol, sq_scale, sq_bias, donorm) = u["grp"]
            c, sub = u["c"], u["sub"]
            if c == 0 and sub == 0:
                gstate[u["gi"]] = get_slot(u["gi"])
            t, b = gstate[u["gi"]]
            if sub == 0:
                gstate[(u["gi"], c)] = stage_buf()
            sg_t, sg_b, j = gstate[(u["gi"], c)]
            tok = slice(sub * 512, (sub + 1) * 512)
            bk, bb = nbank()
            u["bk"], u["bb"] = bk, bb
            mm(bk[:, :], [(t[:, kc, c * 128:(c + 1) * 128], d.xnT[:, kc, tok]) for kc in range(8)], r=[b, d.b_xnT], w=[bb])
            if donorm:
                jj = d.isg % 3
                d.isg += 1
                u["jj"] = jj
                tr.op("act", lambda a: a.activation(out=st.sq[jj][:], in_=bk[:, :], func=AF.Square), r=[bb], w=[st.b_sq[jj]])
            else:
                tr.op("act", lambda a: a.activation(out=sg_t[:, tok], in_=bk[:, :], func=AF.Copy), r=[bb], w=[sg_b])
                if sub == 1:
                    tr.op("sp", lambda q: q.dma_start(out=dst[seq, c * 128:(c + 1) * 128, t0:t0 + TB], in_=sg_t[:, :]), r=[sg_b], dk="stg%d" % j)

        def p2(u):
            (c0, dst, gcol, sq_scale, sq_bias, donorm) = u["grp"]
            if not donorm:
                return
            jj = u["jj"]
            b2, bb2 = nbank()
            mm(b2[:, :], [(blkones, st.sq[jj][:])], r=[st.b_sq[jj], b_cb], w=[bb2])
            rs, brs = d.sg[jj], d.b_sg[jj]
            tr.op("act", lambda a: a.activation(out=rs[:], in_=b2[:, :], func=AF.Ln, bias=sq_bias, scale=sq_scale), r=[bb2], w=[brs])
            tr.op("act", lambda a: a.activation(out=rs[:], in_=rs[:], func=AF.Exp, scale=-0.5), w=[brs])

        def p3(u):
            (c0, dst, gcol, sq_scale, sq_bias, donorm) = u["grp"]
            if not donorm:
                return
            c, sub, jj, bk, bb = u["c"], u["sub"], u["jj"], u["bk"], u["bb"]
            sg_t, sg_b, j = gstate[(u["gi"], c)]
            tok = slice(sub * 512, (sub + 1) * 512)
            rs, brs = d.sg[jj], d.b_sg[jj]
            tr.op("dve", lambda v: v.scalar_tensor_tensor(out=sg_t[:, tok], in0=bk[:, :], scalar=cf[:, gcol:gcol + 1], in1=rs[:], op0=ALU.mult, op1=ALU.mult),
                  r=[bb, brs, b_cf], w=[sg_b])
            if sub == 1:
                tr.op("sp", lambda q: q.dma_start(out=dst[seq, c * 128:(c + 1) * 128, t0:t0 + TB], in_=sg_t[:, :]), r=[sg_b], dk="stg%d" % j)

        pipeline(units, [p1, p2, p3], skew=1)
        if LVL < 2:
            return
        t, b = get_slot(4)
        for i in range(8 if 'fv' in PARTS else 0):
            bk, bb = nbank()
            mm(bk[:, :], [(d.xnT[:, kc, i * 128:(i + 1) * 128], t[:, kc, :]) for kc in range(8)], r=[b, d.b_xnT], w=[bb])
            tr.op("act", lambda a, bk=bk, i=i: a.activation(out=st.vst[:, i, :], in_=bk[:, :], func=AF.Copy), r=[bb], w=[st.b_vst])
        if 'fv' in PARTS:
          tr.op("sp", lambda q: q.dma_start(out=vf_d[seq, t0:t0 + TB, :].rearrange("(i p) c -> p i c", p=128), in_=st.vst[:, :, :]), r=[st.b_vst], dk="vst")
        if LVL < 3:
            return
        t, b = get_slot(5)
        sg_dk, b_dk, jdk = stage_buf()
        sg_ik, b_ik, jik = stage_buf()
        for sub in range(2):
            tok = slice(sub * 512, (sub + 1) * 512)
            bk, bb = nbank()
            mm(bk[0:64, :], [(t[:, kc, 0:64], d.xnT[:, kc, tok]) for kc in range(8)], r=[b, d.b_xnT], w=[bb])
            jj = d.isg % 3
            d.isg += 1
            sq, bsq = st.sq[jj], st.b_sq[jj]
            tr.op("act", lambda a, sq=sq, bk=bk: a.activation(out=sq[0:64, :], in_=bk[0:64, :], func=AF.Square), r=[bb], w=[bsq])
            b2, bb2 = nbank()
            mm(b2[0:64, :], [(blkones[0:64, 0:64], sq[0:64, :])], r=[bsq, b_cb], w=[bb2])
            rs, brs = d.sg[jj], d.b_sg[jj]
            tr.op("act", lambda a, rs=rs, b2=b2: a.activation(out=rs[0:64, :], in_=b2[0:64, :], func=AF.Ln, bias=EPS, scale=1.0 / 64), r=[bb2], w=[brs])
            tr.op("act", lambda a, rs=rs: a.activation(out=rs[0:64, :], in_=rs[0:64, :], func=AF.Exp, scale=-0.5), w=[brs])
            tr.op("dve", lambda v, rs=rs, bk=bk, tok=tok: v.scalar_tensor_tensor(out=sg_dk[0:64, tok], in0=bk[0:64, :], scalar=cf[0:64, CF_GDK:CF_GDK + 1], in1=rs[0:64, :], op0=ALU.mult, op1=ALU.mult),
                  r=[bb, brs, b_cf], w=[b_dk])
            bk, bb = nbank()
            mm(bk[0:64, :], [(t[:, kc, 64:128], d.xnT[:, kc, tok]) for kc in range(8)], r=[b, d.b_xnT], w=[bb])
            tr.op("act", lambda a, bk=bk, tok=tok: a.activation(out=sg_ik[0:64, tok], in_=bk[0:64, :], func=AF.Copy), r=[bb], w=[b_ik])
            bk, bb = nbank()
            mm(bk[0:8, :], [(t[:, kc, 128:136], d.xnT[:, kc, tok]) for kc in range(8)], r=[b, d.b_xnT], w=[bb])
            jf = d.isg % 3
            d.isg += 1
            fst, bfst = d.sg[jf], d.b_sg[jf]
            tr.op("act", lambda a, bk=bk, fst=fst: a.activation(out=fst[0:8, :], in_=bk[0:8, :], func=AF.Copy), r=[bb], w=[bfst])
            tr.op("sp", lambda q, fst=fst, sub=sub: q.dma_start(out=ff_d[seq, :, t0 + sub * 512:t0 + (sub + 1) * 512], in_=fst[0:8, :]), r=[bfst], dk="sgst%d" % jf)
        tr.op("sp", lambda q: q.dma_start(out=dkT_d[seq, :, t0:t0 + TB], in_=sg_dk[0:64, :]), r=[b_dk], dk="stg%d" % jdk)
        tr.op("sp", lambda q: q.dma_start(out=ikT_d[seq, :, t0:t0 + TB], in_=sg_ik[0:64, :]), r=[b_ik], dk="stg%d" % jik)
        if LVL < 4:
            return
        for i in range(8):
            bk, bb = nbank()
            mm(bk[:, 0:128], [(d.xnT[:, kc, i * 128:(i + 1) * 128], t[:, kc, 256:384]) for kc in range(8)], r=[b, d.b_xnT], w=[bb])
            tr.op("act", lambda a, bk=bk, i=i: a.activation(out=st.dvst[:, i, :], in_=bk[:, 0:64], func=AF.Copy), r=[bb], w=[st.b_dvst])
            if 'iw' in PARTS:
                tr.op("dve", lambda v, bk=bk, i=i: v.tensor_copy(out=st.iwst[:, i, :], in_=bk[:, 64:72]), r=[bb], w=[st.b_iwst])
        if 'dvd' in PARTS:
          tr.op("sp", lambda q: q.dma_start(out=vd_d[seq, :, t0 // 128:t0 // 128 + 8, :], in_=st.dvst[:, :, :]), r=[st.b_dvst], dk="dvst")
        if 'iwd' in PARTS:
          tr.op("sp", lambda q: q.dma_start(out=iw_d[seq, :, t0 // 128:t0 // 128 + 8, :], in_=st.iwst[:, :, :]), r=[st.b_iwst], dk="iwst")
        if LVL < 5:
            return
        for gs in range(4):
            t, b = get_slot(6 + gs)
            for c in range(4):
                sg_t, sg_b, j = stage_buf()
                for sub in range(2):
                    tok = slice(sub * 512, (sub + 1) * 512)
                    bk, bb = nbank()
                    mm(bk[:, :], [(t[:, kc, c * 128:(c + 1) * 128], d.xnT[:, kc, tok]) for kc in range(8)], r=[b, d.b_xnT], w=[bb])
                    tr.op("act", lambda a, bk=bk, sg_t=sg_t, tok=tok: a.activation(out=sg_t[:, tok], in_=bk[:, :], func=AF.Sigmoid), r=[bb], w=[sg_b])
                row = gs * 512 + c * 128
                tr.op("sp", lambda q, row=row, sg_t=sg_t: q.dma_start(out=gT_d[seq, row:row + 128, t0:t0 + TB], in_=sg_t[:, :]), r=[sg_b], dk="stg%d" % j)

    class Stg:
        pass

    def alloc_stage(stack):
        st = Stg()
        st.stg = [sb(stack, "stg%d" % i, [128, TB], BF16) for i in range(3)]
        st.b_stg = [Buf("stg%d" % i) for i in range(3)]
        st.sq = [sb(stack, "sq%d" % i, [128, 512], BF16) for i in range(3)]
        st.b_sq = [Buf("sq%d" % i) for i in range(3)]
        st.vst = sb(stack, "vst", [128, 8, 512], BF16)
        st.b_vst = Buf("vst")
        st.dvst = sb(stack, "dvst", [128, 8, 64], BF16)
        st.b_dvst = Buf("dvst")
        st.iwst = sb(stack, "iwst", [128, 8, 8], F32)
        st.b_iwst = Buf("iwst")
        return st

    def phase_C(d, st, blk):
        seq = blk // 2
        t0 = (blk % 2) * TB
        g0 = blk * TB
        for i in range(8):
            tr.op("sp", lambda q, i=i: q.dma_start(out=d.xt[:, i, :], in_=x1_d[g0 + i * 128:g0 + (i + 1) * 128, :]), w=[d.b_xtl[i]], dk="xt%d" % i)
        tr.op("sp", lambda q: q.dma_start(out=st.oa[:, :, :], in_=oaT_d[seq, :, t0:t0 + TB].rearrange("(k p) t -> p k t", p=128)), w=[st.b_oa], dk="oa")
        tr.op("sp", lambda q: q.dma_start(out=st.ob[:, :, :], in_=obT_d[seq, :, t0:t0 + TB].rearrange("(k p) t -> p k t", p=128)), w=[st.b_ob], dk="ob")
        for hb in range(2):
            tr.op("sp", lambda q, hb=hb: q.dma_start(out=d.aT[:, hb * 8:(hb + 1) * 8, :], in_=gT_d[seq, hb * 1024:(hb + 1) * 1024, t0:t0 + TB].rearrange("(k p) t -> p k t", p=128)), w=d.b_aT[hb * 8:(hb + 1) * 8], dk="gt%d" % hb)
        units = [dict(half=half, c=c, sub=sub) for half in range(2) for c in range(4) for sub in range(2)]
        wst = {}

        def m1(u):
            half, c, sub = u["half"], u["c"], u["sub"]
            if c == 0 and sub == 0:
                wst[half] = (ringA_load(d, "wa", [(0, half * 512, 512, 4)]), ringA_load(d, "wb", [(0, half * 512, 512, 4)]))
            (ta, ba), (tb_, bb_) = wst[half]
            tok = slice(sub * 512, (sub + 1) * 512)
            bk1, bb1 = nbank()
            bk2, bb2 = nbank()
            u["b"] = (bk1, bb1, bk2, bb2)
            mm(bk1[:, :], [(ta[:, kc, c * 128:(c + 1) * 128], st.oa[:, kc, tok]) for kc in range(4)], r=[ba, st.b_oa], w=[bb1])
            mm(bk2[:, :], [(tb_[:, kc, c * 128:(c + 1) * 128], st.ob[:, kc, tok]) for kc in range(4)], r=[bb_, st.b_ob], w=[bb2])

        def m2(u):
            half, c, sub = u["half"], u["c"], u["sub"]
            ch = half * 4 + c
            tok = slice(sub * 512, (sub + 1) * 512)
            bk1, bb1, bk2, bb2 = u["b"]
            jj = d.isg % 3
            d.isg += 1
            u["jj"] = jj
            tmp, btmp = d.sg[jj], d.b_sg[jj]
            tr.op("dve", lambda v: v.tensor_tensor(out=tmp[:], in0=bk1[:, :], in1=d.aT[:, ch, tok], op=ALU.mult), r=[bb1, d.b_aT[ch]], w=[btmp])
            tr.op("dve", lambda v: v.tensor_tensor(out=bk2[:, :], in0=bk2[:, :], in1=d.aT[:, 8 + ch, tok], op=ALU.mult), r=[d.b_aT[8 + ch]], w=[bb2])

        def m3(u):
            half, c, sub = u["half"], u["c"], u["sub"]
            ch = half * 4 + c
            tok = slice(sub * 512, (sub + 1) * 512)
            bk1, bb1, bk2, bb2 = u["b"]
            tmp, btmp = d.sg[u["jj"]], d.b_sg[u["jj"]]
            tr.op("dve", lambda v: v.tensor_tensor(out=d.xnT[:, ch, tok], in0=bk2[:, :], in1=tmp[:], op=ALU.add), r=[bb2, btmp], w=[d.b_xnT])

        pipeline(units, [m1, m2, m3], skew=1)
        for half in range(2):
            t, b = ringA_load(d, "wo", [(0, half * 512, 512, 8)])
            for i in range(8):
                bk, bb = nbank()
                mm(bk[:, :], [(d.xnT[:, kc, i * 128:(i + 1) * 128], t[:, kc, :]) for kc in range(8)], r=[b, d.b_xnT], w=[bb])
                tr.op("dve", lambda v, i=i, bk=bk, half=half: v.tensor_tensor(out=d.xt[:, i, half * 512:(half + 1) * 512], in0=bk[:, :], in1=d.xt[:, i, half * 512:(half + 1) * 512], op=ALU.add),
                      r=[bb, d.b_xtl[i]], w=[d.b_xtl[i]])
        norm_transpose(d, 2)
        ffn(d, "f2g", "f2u", "f2d")
        for i in range(8):
            tr.op("sp", lambda q, i=i: q.dma_start(out=out[g0 + i * 128:g0 + (i + 1) * 128, :], in_=d.xt[:, i, :]), r=[d.b_xtl[i]], dk="outst%d" % i)

    class StgC:
        pass

    def alloc_stage_C(stack):
        st = StgC()
        st.oa = sb(stack, "oa", [128, 4, TB], BF16)
        st.ob = sb(stack, "ob", [128, 4, TB], BF16)
        st.b_oa, st.b_ob = Buf("oa"), Buf("ob")
        return st

    def run_fn(e, fn, r, w):
        return tr.op(e, fn, r=r, w=w)

    def phase_B_fox(seq):
        with contextlib.ExitStack() as ph:
            with contextlib.ExitStack() as p0:
                ffl = sb(p0, "ffl", [8, S], F32)
                t1 = sb(p0, "fft1", [8, S], F32)
                t2 = sb(p0, "fft2", [8, S], F32)
                onesr = sb(p0, "onesr", [8, S], F32)
                parts = [sb(p0, "cpart%d" % i, [8, S], BF16) for i in range(6)]
                b_ffl, b_t1, b_t2, b_on = Buf("ffl"), Buf("t1"), Buf("t2"), Buf("onesr")
                b_parts = [Buf("cpart%d" % i) for i in range(6)]
                tr.op("sp", lambda q: q.dma_start(out=ffl[:], in_=ff_d[seq, :, :]), w=[b_ffl], dk="ffl")
                tr.op("dve", lambda v: v.memset(onesr[:], 1.0), w=[b_on])
                tr.op("dve", lambda v: v.tensor_scalar(out=ffl[:], in0=ffl[:], scalar1=cf[0:8, CF_BF:CF_BF + 1], scalar2=None, op0=ALU.add), r=[b_cf], w=[b_ffl])
                tr.op("dve", lambda v: v.tensor_scalar(out=t1[:], in0=ffl[:], scalar1=-1.0, scalar2=None, op0=ALU.mult), r=[b_ffl], w=[b_t1])
                tr.op("dve", lambda v: v.tensor_tensor(out=t1[:], in0=ffl[:], in1=t1[:], op=ALU.min), r=[b_ffl], w=[b_t1])
                tr.op("act", lambda a: a.activation(out=t1[:], in_=t1[:], func=AF.Exp), w=[b_t1])
                tr.op("act", lambda a: a.activation(out=t1[:], in_=t1[:], func=AF.Ln, bias=1.0), w=[b_t1])
                tr.op("dve", lambda v: v.scalar_tensor_tensor(out=t2[:], in0=ffl[:], scalar=0.0, in1=t1[:], op0=ALU.min, op1=ALU.subtract), r=[b_ffl, b_t1], w=[b_t2])
                tr.op("dve", lambda v: v.tensor_tensor_scan(out=ffl[:], data0=onesr[:], data1=t2[:], initial=0.0, op0=ALU.mult, op1=ALU.add), r=[b_on, b_t2], w=[b_ffl])
                tr.op("dve", lambda v: v.tensor_copy(out=parts[0][:], in_=ffl[:]), r=[b_ffl], w=[b_parts[0]])
                tr.op("dve", lambda v: v.tensor_tensor(out=t1[:], in0=ffl[:], in1=parts[0][:], op=ALU.subtract), r=[b_ffl, b_parts[0]], w=[b_t1])
                tr.op("dve", lambda v: v.tensor_copy(out=parts[1][:], in_=t1[:]), r=[b_t1], w=[b_parts[1]])
                tr.op("dve", lambda v: v.tensor_tensor(out=t2[:], in0=t1[:], in1=parts[1][:], op=ALU.subtract), r=[b_t1, b_parts[1]], w=[b_t2])
                tr.op("dve", lambda v: v.tensor_copy(out=parts[2][:], in_=t2[:]), r=[b_t2], w=[b_parts[2]])
                for i in range(3):
                    tr.op("dve", lambda v, i=i: v.tensor_scalar(out=parts[3 + i][:], in0=parts[i][:], scalar1=-1.0, scalar2=None, op0=ALU.mult), r=[b_parts[i]], w=[b_parts[3 + i]])
                b_cumd = Buf("cumd")
                for i in range(6):
                    tr.op("sp", lambda q, i=i: q.dma_start(out=cum_d[seq, :, i, :], in_=parts[i][:]), r=[b_parts[i]], w=[b_cumd], dk="cumst")
                tr.barrier()
            qA = sb(ph, "qA", [128, 8, S], BF16)
            kA = sb(ph, "kA", [128, 8, S], BF16)
            Vf = sb(ph, "Vf", [128, 16, 512], BF16)
            NPT = 4
            PT = [sb(ph, "PT%d" % i, [128, 512], BF16) for i in range(NPT)]
            dtmp = [sb(ph, "dtmp%d" % i, [128, 128], F32) for i in range(2)]
            rec = [sb(ph, "rec%d" % i, [128, 512], F32) for i in range(2)]
            ost = [sb(ph, "ost%d" % i, [128, 512], BF16) for i in range(2)]
            b_qA, b_kA, b_Vf = Buf("qA"), Buf("kA"), Buf("Vf")
            b_PT = [Buf("PT%d" % i) for i in range(NPT)]
            b_dtmp = [Buf("dtmp%d" % i) for i in range(2)]
            b_rec = [Buf("rec%d" % i) for i in range(2)]
            b_ost = [Buf("ost%d" % i) for i in range(2)]
            tr.op("pool", lambda g: g.memset(qA[64:70, :, :], 1.0), w=[b_qA])
            tr.op("pool", lambda g: g.memset(kA[64:70, :, :], 1.0), w=[b_kA])
            tr.op("sp", lambda q: q.dma_start(out=qA[0:64, :, :], in_=qT_d[seq, :, :].rearrange("(h d) s -> d h s", d=64)), w=[b_qA], dk="qA")
            tr.op("sp", lambda q: q.dma_start(out=kA[0:64, :, :], in_=kT_d[seq, :, :].rearrange("(h d) s -> d h s", d=64)), w=[b_kA], dk="kA")
            tr.op("sp", lambda q: q.dma_start(out=qA[64:67, :, :], in_=cum_d[seq, :, 0:3, :].rearrange("h j s -> j h s")), r=[b_cumd], w=[b_qA], dk="qA")
            tr.op("sp", lambda q: q.dma_start(out=kA[67:70, :, :], in_=cum_d[seq, :, 3:6, :].rearrange("h j s -> j h s")), r=[b_cumd], w=[b_kA], dk="kA")
            tr.op("sp", lambda q: q.dma_start(out=Vf[:, :, :], in_=vf_d[seq, :, :].rearrange("(i p) c -> p i c", p=128)), w=[b_Vf], dk="Vf")
            units = []
            npair = 0
            for h in range(8):
                for qb in range(4):
                    nkt = 4 * qb + 4
                    for kt in range(nkt):
                        units.append(dict(h=h, qb=qb, kt=kt, nkt=nkt, pair=npair))
                    npair += 1
            cnt = {"u": 0}

            def stA(u):
                h, qb, kt = u["h"], u["qb"], u["kt"]
                j = kt - 4 * qb
                c0 = max(j, 0) * 128
                i = cnt["u"]
                cnt["u"] += 1
                si = 4 + (i % 4)
                Sb, Sbb = banks[si], bbuf[si]
                pt, bpt = PT[i % NPT], b_PT[i % NPT]
                u["pt"], u["bpt"], u["c0"] = pt, bpt, c0
                mm(Sb[:, c0:512], [(kA[0:70, h, kt * 128:(kt + 1) * 128], qA[0:70, h, qb * 512 + c0:(qb + 1) * 512])], r=[b_kA, b_qA], w=[Sbb])
                if j >= 0:
                    dt_, bdt = dtmp[kt % 2], b_dtmp[kt % 2]
                    tr.op("dve", lambda v: v.tensor_tensor(out=dt_[:], in0=Sb[:, c0:c0 + 128], in1=cf[:, CF_MNEG:CF_MNEG + 128], op=ALU.add), r=[Sbb, b_cf], w=[bdt])
                    tr.op("act", lambda a: a.activation(out=pt[:, c0:c0 + 128], in_=dt_[:], func=AF.Exp), r=[bdt], w=[bpt])
                    if c0 + 128 < 512:
                        tr.op("act", lambda a: a.activation(out=pt[:, c0 + 128:512], in_=Sb[:, c0 + 128:512], func=AF.Exp), r=[Sbb], w=[bpt])
                else:
                    tr.op("act", lambda a: a.activation(out=pt[:, :], in_=Sb[:, :], func=AF.Exp), r=[Sbb], w=[bpt])

            def stB(u):
                h, qb, kt, nkt = u["h"], u["qb"], u["kt"], u["nkt"]
                so = (u["pair"] % 2) * 2
                io = u["pair"] % 2
                Ob, Obb, Db, Dbb = banks[so], bbuf[so], banks[so + 1], bbuf[so + 1]
                pt, bpt, c0 = u["pt"], u["bpt"], u["c0"]

                hp = h // 2
                r0 = (h % 2) * 64

                def fpv(pe):
                    pe.matmul(Ob[:, c0:512], Vf[:, kt, hp * 128:(hp + 1) * 128], pt[:, c0:512], start=(kt == 0), stop=(kt == nkt - 1))
                    return pe.matmul(Db[:, c0:512], ones128, pt[:, c0:512], start=(kt == 0), stop=(kt == nkt - 1))
                tr.op("pe", fpv, r=[bpt, b_Vf, b_cb], w=[Obb, Dbb])
                if kt == nkt - 1:
                    rc, brc = rec[io], b_rec[io]
                    os_, bos = ost[io], b_ost[io]
                    tr.op("dve", lambda v: v.reciprocal(out=rc[r0:r0 + 64, :], in_=Db[r0:r0 + 64, :]), r=[Dbb], w=[brc])
                    tr.op("dve", lambda v: v.tensor_tensor(out=os_[r0:r0 + 64, :], in0=Ob[r0:r0 + 64, :], in1=rc[r0:r0 + 64, :], op=ALU.mult), r=[Obb, brc], w=[bos])
                    tr.op("sp", lambda q: q.dma_start(out=oaT_d[seq, h * 64:(h + 1) * 64, qb * 512:(qb + 1) * 512], in_=os_[r0:r0 + 64, :]), r=[bos], dk="ost%d" % io)

            pipeline(units, [stA, stB], skew=2)
            tr.barrier()

    def phase_B_dsa(seq):
        with contextlib.ExitStack() as ph:
            QQ = sb(ph, "QQ", [128, 8, S], BF16)
            Kz = sb(ph, "Kz", [128, S], BF16)
            Kzi = sb(ph, "Kzi", [128, S], BF16)
            dg = [sb(ph, "dg%d" % i, [128, 8, 128], F32R) for i in range(2)]
            b_dg = [Buf("dg%d" % i) for i in range(2)]
            Vd = sb(ph, "Vd", [128, 16, 128], BF16)
            iwt = sb(ph, "iwt", [128, 16, 8], F32)
            NSC = 7
            NRL = 3
            sc = [sb(ph, "sc%d" % i, [128, S], F32) for i in range(NSC)]
            Mtok = [sb(ph, "Mtok%d" % i, [128, S], BF16) for i in range(4)]
            rl = [sb(ph, "rl%d" % i, [128, 512], F32R) for i in range(NRL)]
            MT = [sb(ph, "MT%d" % i, [128, 16, 512], BF16) for i in range(2)]
            NEP = 5
            E = [sb(ph, "E%d" % i, [128, 512], BF16) for i in range(NEP)]
            PT = [sb(ph, "PTd%d" % i, [128, 512], BF16) for i in range(NEP)]
            MR = [sb(ph, "MR%d" % i, [128, 128], BF16) for i in range(4)]
            osb = [sb(ph, "osb%d" % i, [64, 512], F32) for i in range(2)]
            rec = [sb(ph, "recd%d" % i, [64, 512], F32) for i in range(2)]
            ost = [sb(ph, "ostd%d" % i, [64, 512], BF16) for i in range(2)]
            sm = [sb(ph, "sm%d" % i, [128, 8], F32) for i in range(4)]
            steps = [sb(ph, "steps%d" % i, [128, NITER + 1], F32) for i in range(4)]
            b_dqT, b_iqT, b_dkT, b_ikT, b_Vd, b_iwt = Buf("dqT"), Buf("iqT"), Buf("dkT"), Buf("ikT"), Buf("Vd"), Buf("iwt")
            b_sc = [[Buf("sc%d_%d" % (i, c)) for c in range(4)] for i in range(NSC)]
            b_Mtok = [Buf("Mtok%d" % i) for i in range(4)]
            b_rl = [Buf("rl%d" % i) for i in range(NRL)]
            b_MT = [[Buf("MT%d_%d" % (i, k)) for k in range(4)] for i in range(2)]
            b_E = [Buf("E%d" % i) for i in range(NEP)]
            b_PT = [Buf("PTd%d" % i) for i in range(NEP)]
            b_MR = [Buf("MR%d" % i) for i in range(4)]
            b_osb = [Buf("osb%d" % i) for i in range(2)]
            b_rec = [Buf("recd%d" % i) for i in range(2)]
            b_ost = [Buf("ostd%d" % i) for i in range(2)]
            b_sm = [Buf("sm%d" % i) for i in range(4)]
            b_steps = [Buf("steps%d" % i) for i in range(4)]
            tr.op("sp", lambda q: q.dma_start(out=QQ[0:64, :, :], in_=dqT_d[seq, :, :].rearrange("(h d) s -> d h s", d=64)), w=[b_dqT], dk="qA")
            tr.op("sp", lambda q: q.dma_start(out=QQ[64:128, :, :], in_=iqT_d[seq, :, :].rearrange("(h d) s -> d h s", d=64)), w=[b_iqT], dk="kA")
            tr.op("pool", lambda g: g.memset(Kz[:, :], 0.0), w=[b_dkT])
            tr.op("pool", lambda g: g.memset(Kzi[:, :], 0.0), w=[b_ikT])
            tr.op("pool", lambda g: g.memset(Vd[:, :, :], 0.0), w=[b_Vd])
            tr.op("sp", lambda q: q.dma_start(out=Kz[0:64, :], in_=dkT_d[seq, :, :]), w=[b_dkT], dk="dkT")
            tr.op("sp", lambda q: q.dma_start(out=Kzi[64:128, :], in_=ikT_d[seq, :, :]), w=[b_ikT], dk="ikT")
            tr.op("sp", lambda q: q.dma_start(out=Vd[:, :, 0:64], in_=vd_d[seq, :, :, :]), w=[b_Vd], dk="Vf")
            tr.op("sp", lambda q: q.dma_start(out=iwt[:, :, :], in_=iw_d[seq, :, :, :]), w=[b_iwt], dk="iwt")
            cau = cb[:, CB_CAU:CB_CAU + 128]
            state = {"sb": 0, "db": 0, "acc": 0, "dg": 0, "rl": 0, "pt": 0, "mr": 0, "pair": 0}

            def sbank():
                i = 2 + (state["sb"] % 2)
                state["sb"] += 1
                return banks[i], bbuf[i]

            def dbank():
                i = 4 + (state["db"] % 2)
                state["db"] += 1
                return banks[i], bbuf[i]

            tiles_of = {}

            def scores(qb):
                mt, bmt = MT[qb % 2], b_MT[qb % 2]
                tiles = []
                tiles_of[qb] = tiles
                for qq in range(4):
                    qt = 4 * qb + qq
                    cs = slice(qq * 128, (qq + 1) * 128)
                    if qt < 2:
                        if qt == 1:
                            tr.op("pool", lambda g, cs=cs: g.memset(mt[:, 0, cs], 1.0), w=[bmt[qq]])
                        tr.op("pool", lambda g, cs=cs, qt=qt: g.tensor_copy(out=mt[:, qt, cs], in_=cau), r=[b_cb], w=[bmt[qq]])
                        continue
                    n = (qt + 1) * 128
                    nch = (n + 511) // 512
                    tiles.append(dict(qt=qt, qq=qq, cs=cs, n=n, nch=nch, s_=sc[qt % NSC], bsl=b_sc[qt % NSC][0:nch],
                                      smt=sm[qq], bsm=b_sm[qq], stp=steps[qq], bstp=b_steps[qq], mk=Mtok[qq], bmk=b_Mtok[qq]))
                units = []
                for T in tiles:
                    for c4 in range(T["nch"]):
                        for h in range(8):
                            units.append(dict(T=T, h=h, c4=c4, k0=c4 * 512, wdt=min(512, T["n"] - c4 * 512)))

                def s1(u):
                    T, h, k0, wdt = u["T"], u["h"], u["k0"], u["wdt"]
                    qt = T["qt"]
                    if h == 0 and u["c4"] == 0:
                        di = state["dg"] % 2
                        state["dg"] += 1
                        T["di"] = di

                        def fdg(g):
                            ins = None
                            for hh in range(8):
                                ins = g.tensor_scalar(out=dg[di][:, hh, :], in0=cf[:, CF_IDF:CF_IDF + 128], scalar1=iwt[:, qt, hh:hh + 1], scalar2=0.0, op0=ALU.mult, op1=ALU.add)
                            return ins
                        tr.op("pool", fdg, r=[b_cf, b_iwt], w=[b_dg[di]])
                    bk, bb = dbank()
                    mm(bk[:, 0:wdt], [(QQ[:, h, qt * 128:(qt + 1) * 128], Kzi[:, k0:k0 + wdt])], r=[b_iqT, b_dqT, b_ikT], w=[bb])
                    j = state["rl"] % NRL
                    state["rl"] += 1
                    u["j"] = j
                    tr.op("act", lambda a: a.activation(out=rl[j][:, 0:wdt], in_=bk[:, 0:wdt], func=AF.Relu), r=[bb], w=[b_rl[j]])

                def s2(u):
                    T, h, k0, wdt, j = u["T"], u["h"], u["k0"], u["wdt"], u["j"]
                    if h == 0:
                        ai = 6 + (state["acc"] % 2)
                        state["acc"] += 1
                        T["ai"] = ai
                    ai = T["ai"]
                    di = T["di"]
                    tr.op("pe", lambda pe: pe.matmul(banks[ai][:, 0:wdt], dg[di][:, h, :], rl[j][:, 0:wdt], start=(h == 0), stop=(h == 7)), r=[b_rl[j], b_dg[di]], w=[bbuf[ai]])
                    if h == 7:
                        s_ = T["s_"]
                        tr.op("act", lambda a: a.activation(out=s_[:, k0:k0 + wdt], in_=banks[ai][:, 0:wdt], func=AF.Copy), r=[bbuf[ai]], w=[T["bsl"][u["c4"]]])

                pipeline(units, [s1, s2], skew=1)

            def bisect(qb):
                tiles = tiles_of[qb]
                for T in tiles:
                    tr.op("dve", lambda v, T=T: v.tensor_reduce(out=T["smt"][:, 0:1], in_=T["s_"][:, 0:T["n"]], axis=AX.X, op=ALU.min), r=T["bsl"], w=[T["bsm"]])
                for T in tiles:
                    tr.op("dve", lambda v, T=T: v.tensor_reduce(out=T["smt"][:, 1:2], in_=T["s_"][:, 0:T["n"]], axis=AX.X, op=ALU.max), r=T["bsl"], w=[T["bsm"]])
                for T in tiles:
                    tr.op("dve", lambda v, T=T: v.tensor_tensor(out=T["s_"][:, T["n"] - 128:T["n"]], in0=T["s_"][:, T["n"] - 128:T["n"]], in1=cf[:, CF_MNEGT:CF_MNEGT + 128], op=ALU.add), r=[b_cf], w=[T["bsl"][-1]])
                for T in tiles:
                    tr.op("dve", lambda v, T=T: v.tensor_scalar(out=T["smt"][:, 2:3], in0=T["smt"][:, 1:2], scalar1=T["smt"][:, 0:1], scalar2=1.00390625, op0=ALU.subtract, op1=ALU.mult), w=[T["bsm"]])
                for T in tiles:
                    tr.op("dve", lambda v, T=T: v.tensor_scalar(out=T["stp"][:, :], in0=cf[:, CF_POW:CF_POW + NITER + 1], scalar1=T["smt"][:, 2:3], scalar2=0.5, op0=ALU.mult, op1=ALU.mult), r=[T["bsm"], b_cf], w=[T["bstp"]])
                for T in tiles:
                    tr.op("dve", lambda v, T=T: v.tensor_tensor(out=T["smt"][:, 3:4], in0=T["smt"][:, 0:1], in1=T["stp"][:, 0:1], op=ALU.add), r=[T["bstp"]], w=[T["bsm"]])
                for k in range(NITER):
                    for T in tiles:
                        tr.op("dve", lambda v, T=T: v.tensor_scalar(out=T["mk"][:, 0:T["n"]], in0=T["s_"][:, 0:T["n"]], scalar1=T["smt"][:, 3:4], scalar2=None, op0=ALU.is_ge, op1=ALU.add, accum_out=T["smt"][:, 4:5]),
                              r=T["bsl"], w=[T["bsm"], T["bmk"]])
                    for T in tiles:
                        tr.op("dve", lambda v, T=T: v.tensor_scalar(out=T["smt"][:, 5:6], in0=T["smt"][:, 4:5], scalar1=TOPK - 0.5, scalar2=0.5, op0=ALU.is_ge, op1=ALU.subtract), w=[T["bsm"]])
                    for T in tiles:
                        tr.op("dve", lambda v, T=T, k=k: v.scalar_tensor_tensor(out=T["smt"][:, 3:4], in0=T["smt"][:, 5:6], scalar=T["stp"][:, k:k + 1], in1=T["smt"][:, 3:4], op0=ALU.mult, op1=ALU.add), r=[T["bstp"]], w=[T["bsm"]])
                for T in tiles:
                    tr.op("dve", lambda v, T=T: v.tensor_tensor(out=T["smt"][:, 6:7], in0=T["smt"][:, 3:4], in1=T["stp"][:, NITER:NITER + 1], op=ALU.subtract), r=[T["bstp"]], w=[T["bsm"]])
                for T in tiles:
                    tr.op("dve", lambda v, T=T: v.tensor_scalar(out=T["mk"][:, 0:T["n"]], in0=T["s_"][:, 0:T["n"]], scalar1=T["smt"][:, 6:7], scalar2=None, op0=ALU.is_ge), r=T["bsl"] + [T["bsm"]], w=[T["bmk"]])

            def transp(qb):
                mt, bmt = MT[qb % 2], b_MT[qb % 2]
                for T in tiles_of[qb]:
                    qt, mk, cs, qq = T["qt"], T["mk"], T["cs"], T["qq"]
                    for g0 in range(0, qt + 1, 8):
                        g1 = min(qt + 1, g0 + 8)
                        bk, bb = dbank()
                        ptb = bk[:].bitcast(BF16)

                        def ftr(pe, g0=g0, g1=g1, ptb=ptb, mk=mk):
                            ins = None
                            for kt in range(g0, g1):
                                ins = pe.transpose(out=ptb[:, (kt - g0) * 128:(kt - g0 + 1) * 128], in_=mk[:, kt * 128:(kt + 1) * 128], identity=ident)
                            return ins
                        tr.op("pe", ftr, r=[T["bmk"], b_cb], w=[bb])
                        tr.op("act", lambda a, g0=g0, g1=g1, ptb=ptb, cs=cs: a.activation(out=mt[:, g0:g1, cs], in_=ptb[:, 0:(g1 - g0) * 128].rearrange("p (g t) -> p g t", t=128), func=AF.Copy),
                              r=[bb], w=[bmt[qq]])

            def heads(qb):
                mt, bmt = MT[qb % 2], b_MT[qb % 2]
                nkt = 4 * qb + 4
                units = []
                for h in range(8):
                    for kt in range(nkt):
                        units.append(dict(h=h, kt=kt, pair=state["pair"]))
                    state["pair"] += 1

                def hA(u):
                    h, kt = u["h"], u["kt"]
                    j = kt - 4 * qb
                    c0 = max(j, 0) * 128
                    si_ = (2, 3, 6, 7)[state["sb"] % 4]
                    state["sb"] += 1
                    Sb, Sbb = banks[si_], bbuf[si_]
                    ip = state["pt"] % NEP
                    state["pt"] += 1
                    u["ip"], u["c0"], u["j"] = ip, c0, j
                    e_, be = E[ip], b_E[ip]
                    mm(Sb[:, c0:512], [(Kz[:, kt * 128:(kt + 1) * 128], QQ[:, h, qb * 512 + c0:(qb + 1) * 512])], r=[b_dkT, b_dqT, b_iqT], w=[Sbb])
                    tr.op("act", lambda a: a.activation(out=e_[:, c0:512], in_=Sb[:, c0:512], func=AF.Exp, bias=cf[:, CF_RB31 + h:CF_RB31 + h + 1]), r=[Sbb, b_cf], w=[be])

                def hB(u):
                    h, kt, ip, c0, j = u["h"], u["kt"], u["ip"], u["c0"], u["j"]
                    e_, be = E[ip], b_E[ip]
                    pt, bpt = PT[ip], b_PT[ip]
                    near = []
                    if j >= 0:
                        near.append((c0, 0))
                        if c0 + 128 < 512:
                            near.append((c0 + 128, 128))
                    elif j == -1:
                        near.append((0, 128))
                    cfar = (near[-1][0] + 128) if near else 0
                    rds = [bmt[c // 128] for c in range(c0, 512, 128)]
                    for (cn, ro) in near:
                        im = state["mr"] % 4
                        state["mr"] += 1
                        tr.op("pool", lambda g, im=im, cn=cn, ro=ro: g.tensor_tensor(out=MR[im][:], in0=mt[:, kt, cn:cn + 128], in1=Rt[:, h, ro:ro + 128], op=ALU.mult), r=[bmt[cn // 128], b_Rt], w=[b_MR[im]])
                        tr.op("pool", lambda g, im=im, cn=cn: g.tensor_tensor(out=pt[:, cn:cn + 128], in0=e_[:, cn:cn + 128], in1=MR[im][:], op=ALU.mult), r=[be, b_MR[im]], w=[bpt])
                    if cfar < 512:
                        tr.op("dve" if qb == 3 else "pool", lambda g: g.tensor_tensor(out=pt[:, cfar:512], in0=e_[:, cfar:512], in1=mt[:, kt, cfar:512], op=ALU.mult), r=[be] + rds, w=[bpt])

                def hC(u):
                    h, kt, ip, c0 = u["h"], u["kt"], u["ip"], u["c0"]
                    pt, bpt = PT[ip], b_PT[ip]
                    so = (u["pair"] % 2) * 4
                    io = u["pair"] % 2
                    Ob, Obb, Db, Dbb = banks[so], bbuf[so], banks[so + 1], bbuf[so + 1]

                    def fpv(pe):
                        pe.matmul(Ob[:, c0:512], Vd[:, kt, :], pt[:, c0:512], start=(kt == 0), stop=(kt == nkt - 1))
                        return pe.matmul(Db[:, c0:512], ones128, pt[:, c0:512], start=(kt == 0), stop=(kt == nkt - 1))
                    tr.op("pe", fpv, r=[bpt, b_Vd, b_cb], w=[Obb, Dbb])
                    if kt == nkt - 1:
                        tr.op("act", lambda a: a.activation(out=rec[io][:], in_=Db[0:64, :], func=AF.Ln), r=[Dbb], w=[b_rec[io]])
                        tr.op("act", lambda a: a.activation(out=osb[io][:], in_=Ob[0:64, :], func=AF.Copy), r=[Obb], w=[b_osb[io]])
                        tr.op("act", lambda a: a.activation(out=rec[io][:], in_=rec[io][:], func=AF.Exp, scale=-1.0), w=[b_rec[io]])
                        tr.op("pool", lambda g: g.tensor_tensor(out=ost[io][:], in0=osb[io][:], in1=rec[io][:], op=ALU.mult), r=[b_osb[io], b_rec[io]], w=[b_ost[io]])
                        tr.op("sp", lambda q: q.dma_start(out=obT_d[seq, h * 64:(h + 1) * 64, qb * 512:(qb + 1) * 512], in_=ost[io][:]), r=[b_ost[io]], dk="ostd%d" % io)

                pipeline(units, [hA, hB, hC], skew=2)

            scores(0)
            bisect(0)
            scores(1)
            transp(0)
            bisect(1)
            heads(0)
            scores(2)
            transp(1)
            bisect(2)
            heads(1)
            scores(3)
            transp(2)
            bisect(3)
            heads(2)
            transp(3)
            heads(3)
            tr.barrier()

    for seq in range(NSEQ):
        with contextlib.ExitStack() as ph:
            d = alloc_dense(ph)
            st = alloc_stage(ph)
            for blk in (2 * seq, 2 * seq + 1):
                phase_A(d, st, blk)
            tr.barrier()
        if stage < 3:
            continue
        if stage >= 4:
            phase_B_fox(seq)
        if stage >= 5:
            phase_B_dsa(seq)
        if stage in (4, 5):
            continue
        with contextlib.ExitStack() as ph:
            d = alloc_dense(ph)
            st = alloc_stage_C(ph)
            for blk in (2 * seq, 2 * seq + 1):
                phase_C(d, st, blk)
            tr.barrier()
    tr.barrier()
    es.close()
    return nc


def _t5_bucket_table(n):
    rel = np.arange(n, dtype=np.int32)
    relf = np.maximum(rel, 1).astype(np.float32)
    large = 16 + (np.log(relf / np.float32(16)) / np.float32(np.log(128 / 16)) * np.float32(16)).astype(np.int32)
    large = np.minimum(large, 31)
    return np.where(rel < 16, rel, large)


def _consts(inp):
    cf = np.zeros((128, NCF), np.float32)
    for c0, k in ((CF_G1, "ffn1_norm"), (CF_GM, "mix_norm"), (CF_G2, "ffn2_norm")):
        cf[:, c0:c0 + 8] = np.asarray(inp[k], np.float32).reshape(8, 128).T
    cf[:, CF_GQ] = np.tile(np.asarray(inp["fox_q_norm"], np.float32), 2)
    cf[:, CF_GK] = np.tile(np.asarray(inp["fox_k_norm"], np.float32), 2)
    cf[:, CF_GDQ] = np.tile(np.asarray(inp["dsa_q_norm"], np.float32), 2)
    cf[:, CF_GDK] = np.tile(np.asarray(inp["dsa_k_norm"], np.float32), 2)
    cf[0:8, CF_BF] = np.asarray(inp["b_forget"], np.float32)
    rb = np.asarray(inp["rel_bias"], np.float32)
    cf[:, CF_RB31:CF_RB31 + 8] = rb[31][None, :]
    cf[:, CF_POW:CF_POW + NITER + 1] = (0.5 ** np.arange(NITER + 1, dtype=np.float64))[None, :].astype(np.float32)
    s = np.arange(128)
    cf[:, CF_MNEG:CF_MNEG + 128] = np.where(s[:, None] <= s[None, :], 0.0, NEG)
    cf[:, CF_MNEGT:CF_MNEGT + 128] = np.where(s[None, :] <= s[:, None], 0.0, NEG)
    tp = np.arange(256)
    delta = np.maximum(tp[None, :] - s[:, None], 0)
    bt = _t5_bucket_table(512)[delta]
    g = rb[bt]
    cf[:, CF_GT:CF_GT + 2048] = np.transpose(g, (0, 2, 1)).reshape(128, 2048)
    cf[:, CF_IDF:CF_IDF + 128] = np.eye(128)
    cb = np.zeros((128, NCB), np.float32)
    cb[:, CB_ID:CB_ID + 128] = np.eye(128)
    cb[:, CB_BLK:CB_BLK + 128] = (s[:, None] // 64 == s[None, :] // 64)
    cb[:, CB_ONE:CB_ONE + 64] = 1.0
    cb[:, CB_ONE2:CB_ONE2 + 128] = 1.0
    cb[:, CB_CAU:CB_CAU + 128] = (s[:, None] <= s[None, :])
    return cf, cb.astype(ml_dtypes.bfloat16)


def _perm_win(w_in):
    w = np.asarray(w_in, np.float32)
    o = {}
    off = 0
    for name, wd in (("fq", 512), ("fk", 512), ("fv", 512), ("ff", 8), ("dq", 512), ("dk", 64), ("dv", 64),
                     ("iq", 512), ("ik", 64), ("iw", 8), ("ga", 1024), ("gb", 1024)):
        o[name] = w[:, off:off + wd]
        off += wd
    return np.ascontiguousarray(np.concatenate([o[k] for k in ("fq", "fk", "dq", "iq", "fv", "dk", "ik", "ff", "dv", "iw", "ga", "gb")], axis=1))


def make_in_maps(inp, n_cores=8):
    f = lambda k: np.ascontiguousarray(np.asarray(inp[k], np.float32))
    cf, cb = _consts(inp)
    slotmajor = lambda k: np.ascontiguousarray(np.asarray(inp[k], np.float32).reshape(D, NFC // 2, 256).transpose(1, 0, 2))
    shared = {
        "f1g": slotmajor("ffn1_w_gate"), "f1u": slotmajor("ffn1_w_up"), "f1d": f("ffn1_w_down"),
        "win": _perm_win(inp["w_in"]), "wa": f("w_branch_a"), "wb": f("w_branch_b"), "wo": f("w_out"),
        "f2g": slotmajor("ffn2_w_gate"), "f2u": slotmajor("ffn2_w_up"), "f2d": f("ffn2_w_down"),
        "cf": cf, "cb": cb,
    }
    x = np.asarray(inp["x"], np.float32)
    maps = []
    for c in range(n_cores):
        m = dict(shared)
        m["x"] = np.ascontiguousarray(x[NSEQ * c:NSEQ * (c + 1)].reshape(NSEQ * S, D))
        maps.append(m)
    return maps


def kernel(**inputs):
    nc = bass.Bass("TRN2", target_bir_lowering=False)
    build(nc)
    maps = make_in_maps(inputs, 8)
    res = run_bass_kernel_spmd(nc, maps, core_ids=list(range(8)))
    outs = [np.asarray(r["out"], np.float32).reshape(NSEQ, S, D) for r in res.results]
    return np.concatenate(outs, axis=0)
```

```python
import contextlib
import os
PARTS = os.environ.get('DBG_PARTS', 'qk,fv,small,small2,gates,iw,iwd,dvd').split(',')
LVL = int(os.environ.get('DBG_LVL', '9'))
LVLB = int(os.environ.get('DBG_LVLB', '9'))
import numpy as np
import ml_dtypes
import concourse.bass as bass
import concourse.mybir as mybir
from concourse.bass_utils import run_bass_kernel_spmd

F32 = mybir.dt.float32
BF16 = mybir.dt.bfloat16
F32R = mybir.dt.float32r
AF = mybir.ActivationFunctionType
ALU = mybir.AluOpType
AX = mybir.AxisListType

D = 1024
S = 2048
NSEQ = 2
DFF = 2816
NFC = DFF // 128
TB = 1024
NITER = 16
TOPK = 256
EPS = 1e-6
NEG = -1e30

C_FQ, C_FK, C_DQ, C_IQ, C_FV = 0, 512, 1024, 1536, 2048
C_SM = 2560
C_GA = C_SM + 208
C_GB = C_GA + 1024
NIN = C_GB + 1024

CF_G1, CF_GM, CF_G2 = 0, 8, 16
CF_GQ, CF_GK, CF_GDQ, CF_GDK, CF_BF = 24, 25, 26, 27, 28
CF_RB31 = 29
CF_POW = 37
CF_MNEG = 64
CF_MNEGT = 192
CF_GT = 320
CF_IDF = CF_GT + 8 * 256
NCF = CF_IDF + 128
CB_ID, CB_BLK, CB_ONE, CB_CAU, CB_ONE2 = 0, 128, 256, 320, 448
NCB = 576


class Buf:
    __slots__ = ("name", "w", "rs", "excl")

    def __init__(self, name, excl=False):
        self.name = name
        self.excl = excl
        self.w = None
        self.rs = {}


class Trk:
    def __init__(self, nc, es):
        self.nc = nc
        self.es = es
        self.eng = {"pe": nc.tensor, "act": nc.scalar, "dve": nc.vector, "pool": nc.gpsimd, "sp": nc.sync}
        self.sem = {k: es.enter_context(nc.semaphore("s_" + k)) for k in ("pe", "act", "dve", "pool")}
        self.cnt = {k: 0 for k in self.sem}
        self.waited = {k: {} for k in self.eng}
        self.dsem = {}
        self.dcnt = {}
        self.nwaits = 0

    def _wait(self, e, tok):
        sem, val, key, src, isdma = tok
        if src == e and e == "pe" and not isdma:
            return
        w = self.waited[e]
        if w.get(key, 0) >= val:
            return
        w[key] = val
        self.nwaits += 1
        self.eng[e].wait_ge(sem, val)

    def op(self, e, fn, r=(), w=(), dk=None):
        w = list(w) + [b for b in r if b.excl]
        r = [b for b in r if not b.excl]
        for b in r:
            if b.w is not None:
                self._wait(e, b.w)
        for b in w:
            if b.w is not None:
                self._wait(e, b.w)
            for t in b.rs.values():
                self._wait(e, t)
        ins = fn(self.eng[e])
        if dk is not None:
            if dk not in self.dsem:
                self.dsem[dk] = self.es.enter_context(self.nc.semaphore("d_" + dk))
                self.dcnt[dk] = 0
            self.dcnt[dk] += 16
            ins.then_inc(self.dsem[dk], 16)
            tok = (self.dsem[dk], self.dcnt[dk], "d_" + dk, e, True)
        else:
            self.cnt[e] += 1
            ins.then_inc(self.sem[e], 1)
            tok = (self.sem[e], self.cnt[e], "s_" + e, e, False)
        for b in r:
            b.rs[tok[2]] = tok
        for b in w:
            b.w = tok
            b.rs = {}
        return tok

    def barrier(self):
        toks = [(self.sem[k], self.cnt[k], "s_" + k, k, False) for k in self.sem if self.cnt[k] > 0]
        toks += [(self.dsem[k], self.dcnt[k], "d_" + k, None, True) for k in self.dsem]
        for e in self.eng:
            for t in toks:
                self._wait(e, t)


def build(nc, dbg=False, stage=99):
    es = contextlib.ExitStack()
    tr = Trk(nc, es)
    OUTK = "ExternalOutput" if dbg else "Internal"

    def din(name, shape, dt=F32):
        return nc.dram_tensor(name, shape, dt, kind="ExternalInput").ap()

    def dscr(name, shape, dt, kind=None):
        return nc.dram_tensor(name, shape, dt, kind=kind or "Internal").ap()

    x = din("x", [NSEQ * S, D])
    wsrc = {
        "f1g": din("f1g", [NFC // 2, D, 256]), "f1u": din("f1u", [NFC // 2, D, 256]), "f1d": din("f1d", [DFF, D]),
        "win": din("win", [D, NIN]), "wa": din("wa", [512, D]), "wb": din("wb", [512, D]),
        "wo": din("wo", [D, D]),
        "f2g": din("f2g", [NFC // 2, D, 256]), "f2u": din("f2u", [NFC // 2, D, 256]), "f2d": din("f2d", [DFF, D]),
    }
    cf_d = din("cf", [128, NCF])
    cb_d = din("cb", [128, NCB], BF16)
    out = nc.dram_tensor("out", [NSEQ * S, D], F32, kind="ExternalOutput").ap()

    wb16 = {k: dscr(k + "_b", list(v.shape), BF16) for k, v in wsrc.items()}
    wbuf = {k: [] for k in wsrc}
    NCAST = 2
    castslot = [Buf("cast%d" % i) for i in range(NCAST)]
    ncast = [0]
    x1_d = dscr("x1_d", [NSEQ * S, D], F32, OUTK)
    qT_d = dscr("qT_d", [NSEQ, 512, S], BF16, OUTK)
    kT_d = dscr("kT_d", [NSEQ, 512, S], BF16, OUTK)
    dqT_d = dscr("dqT_d", [NSEQ, 512, S], BF16, OUTK)
    iqT_d = dscr("iqT_d", [NSEQ, 512, S], BF16, OUTK)
    dkT_d = dscr("dkT_d", [NSEQ, 64, S], BF16, OUTK)
    ikT_d = dscr("ikT_d", [NSEQ, 64, S], BF16, OUTK)
    vf_d = dscr("vf_d", [NSEQ, S, 512], BF16, OUTK)
    vd_d = dscr("vd_d", [NSEQ, 128, 16, 64], BF16, OUTK)
    ff_d = dscr("ff_d", [NSEQ, 8, S], F32, OUTK)
    iw_d = dscr("iw_d", [NSEQ, 128, 16, 8], F32, OUTK)
    gT_d = dscr("gT_d", [NSEQ, 2048, S], BF16, OUTK)
    cum_d = dscr("cum_d", [NSEQ, 8, 6, S], BF16, OUTK)
    OK_O = "ExternalInput" if (dbg and stage == 3) else OUTK
    oaT_d = dscr("oaT_d", [NSEQ, 512, S], BF16, OK_O)
    obT_d = dscr("obT_d", [NSEQ, 512, S], BF16, OK_O)

    uid = [0]

    def sb(stack, name, shape, dt):
        uid[0] += 1
        return stack.enter_context(nc.sbuf_tensor("%s_s%d" % (name, uid[0]), shape, dt))

    cf = sb(es, "cf", [128, NCF], F32)
    cb = sb(es, "cb", [128, NCB], BF16)
    nrb = sb(es, "nrb", [128, 8], F32)
    Rt = sb(es, "Rt", [128, 8, 256], BF16)
    b_Rt = Buf("Rt")
    banks = [es.enter_context(nc.psum_tensor("bank%d" % i, [128, 512], F32)) for i in range(8)]
    bbuf = [Buf("bank%d" % i, excl=True) for i in range(8)]
    bstate = {"i": 0}

    def nbank():
        i = bstate["i"]
        bstate["i"] = (i + 1) % 8
        return banks[i], bbuf[i]

    b_cf, b_cb, b_nrb = Buf("cf"), Buf("cb"), Buf("nrb")
    ident = cb[:, CB_ID:CB_ID + 128]
    blkones = cb[:, CB_BLK:CB_BLK + 128]
    ones64 = cb[:, CB_ONE:CB_ONE + 64]
    ones128 = cb[:, CB_ONE2:CB_ONE2 + 128]

    tr.op("sp", lambda q: q.dma_start(out=cf[:], in_=cf_d[:, :]), w=[b_cf], dk="cf")
    tr.op("sp", lambda q: q.dma_start(out=cb[:], in_=cb_d[:, :]), w=[b_cb], dk="cb")
    def cast_dma(k, srcv, dstv, tag):
        cbuf = Buf("wc_%s_%s" % (k, tag))
        wbuf[k].append(cbuf)
        ci = ncast[0] % NCAST
        ncast[0] += 1
        tr.op("pool", lambda q: q.dma_start(out=dstv, in_=srcv), w=[cbuf, castslot[ci]], dk="cw%d" % ci)

    def cast_slots(kg, ku):
        for s_ in range(NFC // 2):
            for k in (kg, ku):
                cast_dma(k, wsrc[k][s_].rearrange("(a b) c -> a (b c)", b=8), wb16[k][s_].rearrange("(a b) c -> a (b c)", b=8), "s%d" % s_)

    def cast_flat(k):
        src = wsrc[k]
        n = src.shape[0] * src.shape[1]
        rows = n // 2048
        s2 = src.rearrange("a b -> (a b)").rearrange("(r c) -> r c", c=2048)
        d2 = wb16[k].rearrange("a b -> (a b)").rearrange("(r c) -> r c", c=2048)
        step = 704 if rows % 704 == 0 else (rows if rows <= 704 else 602)
        assert rows % step == 0, (k, rows, step)
        for r0 in range(0, rows, step):
            cast_dma(k, s2[r0:r0 + step, :], d2[r0:r0 + step, :], "r%d" % r0)

    cast_slots("f1g", "f1u")
    for k in ("f1d", "win", "wa", "wb", "wo"):
        cast_flat(k)
    cast_slots("f2g", "f2u")
    cast_flat("f2d")
    tr.op("dve", lambda v: v.tensor_scalar(out=nrb[:], in0=cf[:, CF_RB31:CF_RB31 + 8], scalar1=-1.0, scalar2=None, op0=ALU.mult), r=[b_cf], w=[b_nrb])
    for h in range(8):
        tr.op("act", lambda a, h=h: a.activation(out=Rt[:, h, :], in_=cf[:, CF_GT + h * 256:CF_GT + (h + 1) * 256], func=AF.Exp, bias=nrb[:, h:h + 1]), r=[b_cf, b_nrb], w=[b_Rt])

    class Dense:
        pass

    def alloc_dense(stack):
        d = Dense()
        d.xt = sb(stack, "xt", [128, 8, D], F32)
        d.xnT = sb(stack, "xnT", [128, 8, TB], BF16)
        d.aT = sb(stack, "aT", [128, NFC, TB], BF16)
        d.ringA = [sb(stack, "rA%d" % i, [128, 8, 512], BF16) for i in range(3)]
        d.ringB = [sb(stack, "rB%d" % i, [128, NFC, 512], BF16) for i in range(2)]
        d.sg = [sb(stack, "sg%d" % i, [128, 512], F32) for i in range(3)]
        d.xs = [sb(stack, "xs%d" % i, [128, D], BF16) for i in range(2)]
        d.junk = sb(stack, "junk", [128, D], BF16)
        d.ssq = sb(stack, "ssq", [128, 8], F32)
        d.rstd = sb(stack, "rstd", [128, 8], F32)
        d.b_xtl, d.b_xnT = [Buf("xt%d" % i) for i in range(8)], Buf("xnT")
        d.b_aT = [Buf("aT%d" % i) for i in range(NFC)]
        d.b_rA = [Buf("rA%d" % i) for i in range(3)]
        d.b_rB = [Buf("rB%d" % i) for i in range(2)]
        d.b_sg = [Buf("sg%d" % i) for i in range(3)]
        d.b_xs = [Buf("xs%d" % i) for i in range(2)]
        d.b_junk, d.b_ssq, d.b_rstd = Buf("junk"), Buf("ssq"), Buf("rstd")
        d.iA = 0
        d.iB = 0
        d.isg = 0
        return d

    def ringA_load(d, wkey, parts):
        i = d.iA % 3
        d.iA += 1
        t, b = d.ringA[i], d.b_rA[i]
        for (dc, sc, ncol, kch) in parts:
            srcv = wb16[wkey][0:kch * 128, sc:sc + ncol].rearrange("(k p) c -> p k c", p=128)
            tr.op("sp", lambda q, t=t, dc=dc, ncol=ncol, kch=kch, srcv=srcv: q.dma_start(out=t[:, 0:kch, dc:dc + ncol], in_=srcv),
                  r=wbuf[wkey], w=[b], dk="rA%d" % i)
        return t, b

    def ringB_load(d, wkey, c0):
        i = d.iB % 2
        d.iB += 1
        t, b = d.ringB[i], d.b_rB[i]
        srcv = wb16[wkey][:, c0:c0 + 512].rearrange("(k p) c -> p k c", p=128)
        tr.op("sp", lambda q: q.dma_start(out=t[:, :, :], in_=srcv), r=wbuf[wkey], w=[b], dk="rB%d" % i)
        return t, b

    def mm(outap, pairs, r, w):
        def fn(pe):
            ins = None
            n = len(pairs)
            for i, (l, rh) in enumerate(pairs):
                ins = pe.matmul(outap, l, rh, start=(i == 0), stop=(i == n - 1))
            return ins
        return tr.op("pe", fn, r=r, w=w)

    def pipeline(units, stages, skew=1):
        n = len(units)
        for s_ in range(n + (len(stages) - 1) * skew):
            for i, fn in enumerate(stages):
                u = s_ - i * skew
                if 0 <= u < n:
                    fn(units[u])

    def norm_transpose(d, gj):
        for i in range(8):
            tr.op("act", lambda a, i=i: a.activation(out=d.junk[:], in_=d.xt[:, i, :], func=AF.Square, accum_out=d.ssq[:, i:i + 1]),
                  r=[d.b_xtl[i]], w=[d.b_junk, d.b_ssq])
        tr.op("act", lambda a: a.activation(out=d.rstd[:], in_=d.ssq[:], func=AF.Sqrt, bias=EPS, scale=1.0 / D), r=[d.b_ssq], w=[d.b_rstd])
        tr.op("dve", lambda v: v.reciprocal(out=d.rstd[:], in_=d.rstd[:]), r=[d.b_rstd], w=[d.b_rstd])
        for i in range(8):
            xs, bxs = d.xs[i % 2], d.b_xs[i % 2]
            tr.op("act", lambda a, i=i, xs=xs: a.activation(out=xs[:], in_=d.xt[:, i, :], func=AF.Copy, scale=d.rstd[:, i:i + 1]),
                  r=[d.b_xtl[i], d.b_rstd], w=[bxs])
            bk, bb = nbank()
            pt = bk[:].bitcast(BF16)

            def fn(pe, xs=xs, pt=pt):
                ins = None
                for kc in range(8):
                    ins = pe.transpose(out=pt[:, kc * 128:(kc + 1) * 128], in_=xs[:, kc * 128:(kc + 1) * 128], identity=ident)
                return ins
            tr.op("pe", fn, r=[bxs, b_cb], w=[bb])
            def fev(v, i=i, pt=pt):
                ins = None
                for kc in range(8):
                    ins = v.tensor_scalar(out=d.xnT[:, kc, i * 128:(i + 1) * 128], in0=pt[:, kc * 128:(kc + 1) * 128], scalar1=cf[:, gj * 8 + kc:gj * 8 + kc + 1], scalar2=None, op0=ALU.mult)
                return ins
            tr.op("dve", fev, r=[bb, b_cf], w=[d.b_xnT])

    def ffn(d, wg, wu, wd):
        for s in range(NFC // 2):
            i = d.iA % 3
            d.iA += 1
            t, b = d.ringA[i], d.b_rA[i]
            for (wk, c0_) in ((wg, 0), (wu, 256)):
                srcv = wb16[wk][s].rearrange("(k p) c -> p k c", p=128)
                tr.op("sp", lambda q, t=t, srcv=srcv, c0_=c0_: q.dma_start(out=t[:, :, c0_:c0_ + 256], in_=srcv), r=[wbuf[wk][s]], w=[b], dk="rA%d" % i)
            for f in range(2):
                fc = s * 2 + f
                for sub in range(2):
                    tok = slice(sub * 512, (sub + 1) * 512)
                    bg, bbg = nbank()
                    bu, bbu = nbank()
                    mm(bg[:, :], [(t[:, kc, f * 128:(f + 1) * 128], d.xnT[:, kc, tok]) for kc in range(8)], r=[b, d.b_xnT], w=[bbg])
                    mm(bu[:, :], [(t[:, kc, 256 + f * 128:256 + (f + 1) * 128], d.xnT[:, kc, tok]) for kc in range(8)], r=[b, d.b_xnT], w=[bbu])
                    j = d.isg % 3
                    d.isg += 1
                    sg, bsg = d.sg[j], d.b_sg[j]
                    tr.op("act", lambda a, sg=sg, bg=bg: a.activation(out=sg[:], in_=bg[:, :], func=AF.Silu), r=[bbg], w=[bsg])
                    tr.op("dve", lambda v, sg=sg, bu=bu, fc=fc, tok=tok: v.tensor_tensor(out=d.aT[:, fc, tok], in0=sg[:], in1=bu[:, :], op=ALU.mult),
                          r=[bsg, bbu], w=[d.b_aT[fc]])
        for half in range(2):
            t, b = ringB_load(d, wd, half * 512)
            for i in range(8):
                bk, bb = nbank()
                mm(bk[:, :], [(d.aT[:, kc, i * 128:(i + 1) * 128], t[:, kc, :]) for kc in range(NFC)], r=[b] + d.b_aT, w=[bb])
                tr.op("dve", lambda v, i=i, bk=bk, half=half: v.scalar_tensor_tensor(out=d.xt[:, i, half * 512:(half + 1) * 512], in0=bk[:, :], scalar=0.5, in1=d.xt[:, i, half * 512:(half + 1) * 512], op0=ALU.mult, op1=ALU.add),
                      r=[bb, d.b_xtl[i]], w=[d.b_xtl[i]])

    def phase_A(d, st, blk):
        seq = blk // 2
        t0 = (blk % 2) * TB
        g0 = blk * TB
        for i in range(8):
            tr.op("sp", lambda q, i=i: q.dma_start(out=d.xt[:, i, :], in_=x[g0 + i * 128:g0 + (i + 1) * 128, :]), w=[d.b_xtl[i]], dk="xt%d" % i)
        norm_transpose(d, 0)
        ffn(d, "f1g", "f1u", "f1d")
        for i in range(8):
            tr.op("sp", lambda q, i=i: q.dma_start(out=x1_d[g0 + i * 128:g0 + (i + 1) * 128, :], in_=d.xt[:, i, :]), r=[d.b_xtl[i]], dk="x1st%d" % i)
        if stage < 2:
            return
        slot_parts = [[(0, C_FQ, 512, 8)], [(0, C_FK, 512, 8)], [(0, C_DQ, 512, 8)], [(0, C_IQ, 512, 8)], [(0, C_FV, 512, 8)],
                      [(0, C_SM, 136, 8), (256, C_SM + 136, 72, 8)]] + [[(0, C_GA + gs * 512, 512, 8)] for gs in range(4)]
        loaded = {}

        def get_slot(i):
            for k in range(i, min(i + 3, len(slot_parts))):
                if k not in loaded:
                    loaded[k] = ringA_load(d, "win", slot_parts[k])
            return loaded[i]
        get_slot(0)
        norm_transpose(d, 1)
        ist = [0]

        def stage_buf():
            j = ist[0] % 3
            ist[0] += 1
            return st.stg[j], st.b_stg[j], j

        groups = (
            (C_FQ, qT_d, CF_GQ, 1.0, 64.0 * EPS, True),
            (C_FK, kT_d, CF_GK, 1.0 / 64, EPS, True),
            (C_DQ, dqT_d, CF_GDQ, 1.0, 64.0 * EPS, True),
            (C_IQ, iqT_d, None, None, None, False))
        units = []
        for gi, grp in enumerate(groups):
            for c in range(4):
                for sub in range(2):
                    units.append(dict(gi=gi, grp=grp, c=c, sub=sub))
        gstate = {}

        def p1(u):
            (c0, dst, gcol, sq_scale, sq_bias, donorm) = u["grp"]
            c, sub = u["c"], u["sub"]
            if c == 0 and sub == 0:
                gstate[u["gi"]] = get_slot(u["gi"])
            t, b = gstate[u["gi"]]
            if sub == 0:
                gstate[(u["gi"], c)] = stage_buf()
            sg_t, sg_b, j = gstate[(u["gi"], c)]
            tok = slice(sub * 512, (sub + 1) * 512)
            bk, bb = nbank()
            u["bk"], u["bb"] = bk, bb
            mm(bk[:, :], [(t[:, kc, c * 128:(c + 1) * 128], d.xnT[:, kc, tok]) for kc in range(8)], r=[b, d.b_xnT], w=[bb])
            if donorm:
                jj = d.isg % 3
                d.isg += 1
                u["jj"] = jj
                tr.op("act", lambda a: a.activation(out=st.sq[jj][:], in_=bk[:, :], func=AF.Square), r=[bb], w=[st.b_sq[jj]])
            else:
                tr.op("act", lambda a: a.activation(out=sg_t[:, tok], in_=bk[:, :], func=AF.Copy), r=[bb], w=[sg_b])
                if sub == 1:
                    tr.op("sp", lambda q: q.dma_start(out=dst[seq, c * 128:(c + 1) * 128, t0:t0 + TB], in_=sg_t[:, :]), r=[sg_b], dk="stg%d" % j)

        def p2(u):
            (c0, dst, gcol, sq_scale, sq_bias, donorm) = u["grp"]
            if not donorm:
                return
            jj = u["jj"]
            b2, bb2 = nbank()
            mm(b2[:, :], [(blkones, st.sq[jj][:])], r=[st.b_sq[jj], b_cb], w=[bb2])
            rs, brs = d.sg[jj], d.b_sg[jj]
            tr.op("act", lambda a: a.activation(out=rs[:], in_=b2[:, :], func=AF.Ln, bias=sq_bias, scale=sq_scale), r=[bb2], w=[brs])
            tr.op("act", lambda a: a.activation(out=rs[:], in_=rs[:], func=AF.Exp, scale=-0.5), w=[brs])

        def p3(u):
            (c0, dst, gcol, sq_scale, sq_bias, donorm) = u["grp"]
            if not donorm:
                return
            c, sub, jj, bk, bb = u["c"], u["sub"], u["jj"], u["bk"], u["bb"]
            sg_t, sg_b, j = gstate[(u["gi"], c)]
            tok = slice(sub * 512, (sub + 1) * 512)
            rs, brs = d.sg[jj], d.b_sg[jj]
            tr.op("dve", lambda v: v.scalar_tensor_tensor(out=sg_t[:, tok], in0=bk[:, :], scalar=cf[:, gcol:gcol + 1], in1=rs[:], op0=ALU.mult, op1=ALU.mult),
                  r=[bb, brs, b_cf], w=[sg_b])
            if sub == 1:
                tr.op("sp", lambda q: q.dma_start(out=dst[seq, c * 128:(c + 1) * 128, t0:t0 + TB], in_=sg_t[:, :]), r=[sg_b], dk="stg%d" % j)

        pipeline(units, [p1, p2, p3], skew=1)
        if LVL < 2:
            return
        t, b = get_slot(4)
        for i in range(8 if 'fv' in PARTS else 0):
            bk, bb = nbank()
            mm(bk[:, :], [(d.xnT[:, kc, i * 128:(i + 1) * 128], t[:, kc, :]) for kc in range(8)], r=[b, d.b_xnT], w=[bb])
            tr.op("act", lambda a, bk=bk, i=i: a.activation(out=st.vst[:, i, :], in_=bk[:, :], func=AF.Copy), r=[bb], w=[st.b_vst])
        if 'fv' in PARTS:
          tr.op("sp", lambda q: q.dma_start(out=vf_d[seq, t0:t0 + TB, :].rearrange("(i p) c -> p i c", p=128), in_=st.vst[:, :, :]), r=[st.b_vst], dk="vst")
        if LVL < 3:
            return
        t, b = get_slot(5)
        sg_dk, b_dk, jdk = stage_buf()
        sg_ik, b_ik, jik = stage_buf()
        for sub in range(2):
            tok = slice(sub * 512, (sub + 1) * 512)
            bk, bb = nbank()
            mm(bk[0:64, :], [(t[:, kc, 0:64], d.xnT[:, kc, tok]) for kc in range(8)], r=[b, d.b_xnT], w=[bb])
            jj = d.isg % 3
            d.isg += 1
            sq, bsq = st.sq[jj], st.b_sq[jj]
            tr.op("act", lambda a, sq=sq, bk=bk: a.activation(out=sq[0:64, :], in_=bk[0:64, :], func=AF.Square), r=[bb], w=[bsq])
            b2, bb2 = nbank()
            mm(b2[0:64, :], [(blkones[0:64, 0:64], sq[0:64, :])], r=[bsq, b_cb], w=[bb2])
            rs, brs = d.sg[jj], d.b_sg[jj]
            tr.op("act", lambda a, rs=rs, b2=b2: a.activation(out=rs[0:64, :], in_=b2[0:64, :], func=AF.Ln, bias=EPS, scale=1.0 / 64), r=[bb2], w=[brs])
            tr.op("act", lambda a, rs=rs: a.activation(out=rs[0:64, :], in_=rs[0:64, :], func=AF.Exp, scale=-0.5), w=[brs])
            tr.op("dve", lambda v, rs=rs, bk=bk, tok=tok: v.scalar_tensor_tensor(out=sg_dk[0:64, tok], in0=bk[0:64, :], scalar=cf[0:64, CF_GDK:CF_GDK + 1], in1=rs[0:64, :], op0=ALU.mult, op1=ALU.mult),
                  r=[bb, brs, b_cf], w=[b_dk])
            bk, bb = nbank()
            mm(bk[0:64, :], [(t[:, kc, 64:128], d.xnT[:, kc, tok]) for kc in range(8)], r=[b, d.b_xnT], w=[bb])
            tr.op("act", lambda a, bk=bk, tok=tok: a.activation(out=sg_ik[0:64, tok], in_=bk[0:64, :], func=AF.Copy), r=[bb], w=[b_ik])
            bk, bb = nbank()
            mm(bk[0:8, :], [(t[:, kc, 128:136], d.xnT[:, kc, tok]) for kc in range(8)], r=[b, d.b_xnT], w=[bb])
            jf = d.isg % 3
            d.isg += 1
            fst, bfst = d.sg[jf], d.b_sg[jf]
            tr.op("act", lambda a, bk=bk, fst=fst: a.activation(out=fst[0:8, :], in_=bk[0:8, :], func=AF.Copy), r=[bb], w=[bfst])
            tr.op("sp", lambda q, fst=fst, sub=sub: q.dma_start(out=ff_d[seq, :, t0 + sub * 512:t0 + (sub + 1) * 512], in_=fst[0:8, :]), r=[bfst], dk="sgst%d" % jf)
        tr.op("sp", lambda q: q.dma_start(out=dkT_d[seq, :, t0:t0 + TB], in_=sg_dk[0:64, :]), r=[b_dk], dk="stg%d" % jdk)
        tr.op("sp", lambda q: q.dma_start(out=ikT_d[seq, :, t0:t0 + TB], in_=sg_ik[0:64, :]), r=[b_ik], dk="stg%d" % jik)
        if LVL < 4:
            return
        for i in range(8):
            bk, bb = nbank()
            mm(bk[:, 0:128], [(d.xnT[:, kc, i * 128:(i + 1) * 128], t[:, kc, 256:384]) for kc in range(8)], r=[b, d.b_xnT], w=[bb])
            tr.op("act", lambda a, bk=bk, i=i: a.activation(out=st.dvst[:, i, :], in_=bk[:, 0:64], func=AF.Copy), r=[bb], w=[st.b_dvst])
            if 'iw' in PARTS:
                tr.op("dve", lambda v, bk=bk, i=i: v.tensor_copy(out=st.iwst[:, i, :], in_=bk[:, 64:72]), r=[bb], w=[st.b_iwst])
        if 'dvd' in PARTS:
          tr.op("sp", lambda q: q.dma_start(out=vd_d[seq, :, t0 // 128:t0 // 128 + 8, :], in_=st.dvst[:, :, :]), r=[st.b_dvst], dk="dvst")
        if 'iwd' in PARTS:
          tr.op("sp", lambda q: q.dma_start(out=iw_d[seq, :, t0 // 128:t0 // 128 + 8, :], in_=st.iwst[:, :, :]), r=[st.b_iwst], dk="iwst")
        if LVL < 5:
            return
        for gs in range(4):
            t, b = get_slot(6 + gs)
            for c in range(4):
                sg_t, sg_b, j = stage_buf()
                for sub in range(2):
                    tok = slice(sub * 512, (sub + 1) * 512)
                    bk, bb = nbank()
                    mm(bk[:, :], [(t[:, kc, c * 128:(c + 1) * 128], d.xnT[:, kc, tok]) for kc in range(8)], r=[b, d.b_xnT], w=[bb])
                    tr.op("act", lambda a, bk=bk, sg_t=sg_t, tok=tok: a.activation(out=sg_t[:, tok], in_=bk[:, :], func=AF.Sigmoid), r=[bb], w=[sg_b])
                row = gs * 512 + c * 128
                tr.op("sp", lambda q, row=row, sg_t=sg_t: q.dma_start(out=gT_d[seq, row:row + 128, t0:t0 + TB], in_=sg_t[:, :]), r=[sg_b], dk="stg%d" % j)

    class Stg:
        pass

    def alloc_stage(stack):
        st = Stg()
        st.stg = [sb(stack, "stg%d" % i, [128, TB], BF16) for i in range(3)]
        st.b_stg = [Buf("stg%d" % i) for i in range(3)]
        st.sq = [sb(stack, "sq%d" % i, [128, 512], BF16) for i in range(3)]
        st.b_sq = [Buf("sq%d" % i) for i in range(3)]
        st.vst = sb(stack, "vst", [128, 8, 512], BF16)
        st.b_vst = Buf("vst")
        st.dvst = sb(stack, "dvst", [128, 8, 64], BF16)
        st.b_dvst = Buf("dvst")
        st.iwst = sb(stack, "iwst", [128, 8, 8], F32)
        st.b_iwst = Buf("iwst")
        return st

    def phase_C(d, st, blk):
        seq = blk // 2
        t0 = (blk % 2) * TB
        g0 = blk * TB
        tr.op("sp", lambda q: q.dma_start(out=st.oa[:, :, :], in_=oaT_d[seq, :, t0:t0 + TB].rearrange("(k p) t -> p k t", p=128)), w=[st.b_oa], dk="oa")
        tr.op("sp", lambda q: q.dma_start(out=st.ob[:, :, :], in_=obT_d[seq, :, t0:t0 + TB].rearrange("(k p) t -> p k t", p=128)), w=[st.b_ob], dk="ob")
        for ch in range(8):
            for hb in range(2):
                cc = hb * 8 + ch
                tr.op("sp", lambda q, cc=cc: q.dma_start(out=d.aT[:, cc, :], in_=gT_d[seq, cc * 128:(cc + 1) * 128, t0:t0 + TB]), w=[d.b_aT[cc]], dk="gt%d" % cc)
        for i in range(8):
            tr.op("sp", lambda q, i=i: q.dma_start(out=d.xt[:, i, :], in_=x1_d[g0 + i * 128:g0 + (i + 1) * 128, :]), w=[d.b_xtl[i]], dk="xt%d" % i)
        units = [dict(half=half, c=c, sub=sub) for half in range(2) for c in range(4) for sub in range(2)]
        wst = {}

        def m1(u):
            half, c, sub = u["half"], u["c"], u["sub"]
            if c == 0 and sub == 0:
                wst[half] = (ringA_load(d, "wa", [(0, half * 512, 512, 4)]), ringA_load(d, "wb", [(0, half * 512, 512, 4)]))
            (ta, ba), (tb_, bb_) = wst[half]
            tok = slice(sub * 512, (sub + 1) * 512)
            bk1, bb1 = nbank()
            bk2, bb2 = nbank()
            u["b"] = (bk1, bb1, bk2, bb2)
            mm(bk1[:, :], [(ta[:, kc, c * 128:(c + 1) * 128], st.oa[:, kc, tok]) for kc in range(4)], r=[ba, st.b_oa], w=[bb1])
            mm(bk2[:, :], [(tb_[:, kc, c * 128:(c + 1) * 128], st.ob[:, kc, tok]) for kc in range(4)], r=[bb_, st.b_ob], w=[bb2])

        def m2(u):
            half, c, sub = u["half"], u["c"], u["sub"]
            ch = half * 4 + c
            tok = slice(sub * 512, (sub + 1) * 512)
            bk1, bb1, bk2, bb2 = u["b"]
            jj = d.isg % 3
            d.isg += 1
            u["jj"] = jj
            tmp, btmp = d.sg[jj], d.b_sg[jj]
            tr.op("dve", lambda v: v.tensor_tensor(out=tmp[:], in0=bk1[:, :], in1=d.aT[:, ch, tok], op=ALU.mult), r=[bb1, d.b_aT[ch]], w=[btmp])
            tr.op("dve", lambda v: v.tensor_tensor(out=bk2[:, :], in0=bk2[:, :], in1=d.aT[:, 8 + ch, tok], op=ALU.mult), r=[d.b_aT[8 + ch]], w=[bb2])

        def m3(u):
            half, c, sub = u["half"], u["c"], u["sub"]
            ch = half * 4 + c
            tok = slice(sub * 512, (sub + 1) * 512)
            bk1, bb1, bk2, bb2 = u["b"]
            tmp, btmp = d.sg[u["jj"]], d.b_sg[u["jj"]]
            tr.op("dve", lambda v: v.tensor_tensor(out=d.xnT[:, ch, tok], in0=bk2[:, :], in1=tmp[:], op=ALU.add), r=[bb2, btmp], w=[d.b_xnT])

        pipeline(units, [m1, m2, m3], skew=1)
        for half in range(2):
            t, b = ringA_load(d, "wo", [(0, half * 512, 512, 8)])
            for i in range(8):
                bk, bb = nbank()
                mm(bk[:, :], [(d.xnT[:, kc, i * 128:(i + 1) * 128], t[:, kc, :]) for kc in range(8)], r=[b, d.b_xnT], w=[bb])
                tr.op("dve", lambda v, i=i, bk=bk, half=half: v.tensor_tensor(out=d.xt[:, i, half * 512:(half + 1) * 512], in0=bk[:, :], in1=d.xt[:, i, half * 512:(half + 1) * 512], op=ALU.add),
                      r=[bb, d.b_xtl[i]], w=[d.b_xtl[i]])
        norm_transpose(d, 2)
        ffn(d, "f2g", "f2u", "f2d")
        for i in range(8):
            tr.op("sp", lambda q, i=i: q.dma_start(out=out[g0 + i * 128:g0 + (i + 1) * 128, :], in_=d.xt[:, i, :]), r=[d.b_xtl[i]], dk="outst%d" % i)

    class StgC:
        pass

    def alloc_stage_C(stack):
        st = StgC()
        st.oa = sb(stack, "oa", [128, 4, TB], BF16)
        st.ob = sb(stack, "ob", [128, 4, TB], BF16)
        st.b_oa, st.b_ob = Buf("oa"), Buf("ob")
        return st

    def run_fn(e, fn, r, w):
        return tr.op(e, fn, r=r, w=w)

    def phase_B_fox(seq):
        with contextlib.ExitStack() as ph:
            b_cumd = Buf("cumd")
            ffl = sb(ph, "ffl", [8, S], F32)
            b_ffl = Buf("ffl")
            tr.op("sp", lambda q: q.dma_start(out=ffl[:], in_=ff_d[seq, :, :]), w=[b_ffl], dk="ffl")
            qA = sb(ph, "qA", [128, 8, S], BF16)
            kA = sb(ph, "kA", [128, 8, S], BF16)
            Vf = sb(ph, "Vf", [128, 16, 512], BF16)
            NPT = 6
            PT = [sb(ph, "PT%d" % i, [128, 512], BF16) for i in range(NPT)]
            dtmp = [sb(ph, "dtmp%d" % i, [128, 128], F32) for i in range(2)]
            rec = [sb(ph, "rec%d" % i, [128, 512], F32) for i in range(2)]
            ost = [sb(ph, "ost%d" % i, [128, 512], BF16) for i in range(2)]
            b_qA, b_kA, b_Vf = Buf("qA"), Buf("kA"), Buf("Vf")
            b_PT = [Buf("PT%d" % i) for i in range(NPT)]
            b_dtmp = [Buf("dtmp%d" % i) for i in range(2)]
            b_rec = [Buf("rec%d" % i) for i in range(2)]
            b_ost = [Buf("ost%d" % i) for i in range(2)]
            tr.op("pool", lambda g: g.memset(qA[64:70, :, :], 1.0), w=[b_qA])
            tr.op("pool", lambda g: g.memset(kA[64:70, :, :], 1.0), w=[b_kA])
            tr.op("sp", lambda q: q.dma_start(out=qA[0:64, :, :], in_=qT_d[seq, :, :].rearrange("(h d) s -> d h s", d=64)), w=[b_qA], dk="qA")
            tr.op("sp", lambda q: q.dma_start(out=kA[0:64, :, :], in_=kT_d[seq, :, :].rearrange("(h d) s -> d h s", d=64)), w=[b_kA], dk="kA")
            tr.op("sp", lambda q: q.dma_start(out=Vf[:, :, :], in_=vf_d[seq, :, :].rearrange("(i p) c -> p i c", p=128)), w=[b_Vf], dk="Vf")
            with contextlib.ExitStack() as p0:
                t1 = sb(p0, "fft1", [8, S], F32)
                t2 = sb(p0, "fft2", [8, S], F32)
                onesr = sb(p0, "onesr", [8, S], F32)
                parts = [sb(p0, "cpart%d" % i, [8, S], BF16) for i in range(6)]
                b_t1, b_t2, b_on = Buf("t1"), Buf("t2"), Buf("onesr")
                b_parts = [Buf("cpart%d" % i) for i in range(6)]
                tr.op("dve", lambda v: v.memset(onesr[:], 1.0), w=[b_on])
                tr.op("dve", lambda v: v.tensor_scalar(out=ffl[:], in0=ffl[:], scalar1=cf[0:8, CF_BF:CF_BF + 1], scalar2=None, op0=ALU.add), r=[b_cf], w=[b_ffl])
                tr.op("dve", lambda v: v.tensor_scalar(out=t1[:], in0=ffl[:], scalar1=-1.0, scalar2=None, op0=ALU.mult), r=[b_ffl], w=[b_t1])
                tr.op("dve", lambda v: v.tensor_tensor(out=t1[:], in0=ffl[:], in1=t1[:], op=ALU.min), r=[b_ffl], w=[b_t1])
                tr.op("act", lambda a: a.activation(out=t1[:], in_=t1[:], func=AF.Exp), w=[b_t1])
                tr.op("act", lambda a: a.activation(out=t1[:], in_=t1[:], func=AF.Ln, bias=1.0), w=[b_t1])
                tr.op("dve", lambda v: v.scalar_tensor_tensor(out=t2[:], in0=ffl[:], scalar=0.0, in1=t1[:], op0=ALU.min, op1=ALU.subtract), r=[b_ffl, b_t1], w=[b_t2])
                tr.op("dve", lambda v: v.tensor_tensor_scan(out=ffl[:], data0=onesr[:], data1=t2[:], initial=0.0, op0=ALU.mult, op1=ALU.add), r=[b_on, b_t2], w=[b_ffl])
                tr.op("dve", lambda v: v.tensor_copy(out=parts[0][:], in_=ffl[:]), r=[b_ffl], w=[b_parts[0]])
                tr.op("dve", lambda v: v.tensor_tensor(out=t1[:], in0=ffl[:], in1=parts[0][:], op=ALU.subtract), r=[b_ffl, b_parts[0]], w=[b_t1])
                tr.op("dve", lambda v: v.tensor_copy(out=parts[1][:], in_=t1[:]), r=[b_t1], w=[b_parts[1]])
                tr.op("dve", lambda v: v.tensor_tensor(out=t2[:], in0=t1[:], in1=parts[1][:], op=ALU.subtract), r=[b_t1, b_parts[1]], w=[b_t2])
                tr.op("dve", lambda v: v.tensor_copy(out=parts[2][:], in_=t2[:]), r=[b_t2], w=[b_parts[2]])
                for i in range(3):
                    tr.op("dve", lambda v, i=i: v.tensor_scalar(out=parts[3 + i][:], in0=parts[i][:], scalar1=-1.0, scalar2=None, op0=ALU.mult), r=[b_parts[i]], w=[b_parts[3 + i]])
                for i in range(6):
                    tr.op("sp", lambda q, i=i: q.dma_start(out=cum_d[seq, :, i, :], in_=parts[i][:]), r=[b_parts[i]], w=[b_cumd], dk="cumst")
            tr.op("sp", lambda q: q.dma_start(out=qA[64:67, :, :], in_=cum_d[seq, :, 0:3, :].rearrange("h j s -> j h s")), r=[b_cumd], w=[b_qA], dk="qA")
            tr.op("sp", lambda q: q.dma_start(out=kA[67:70, :, :], in_=cum_d[seq, :, 3:6, :].rearrange("h j s -> j h s")), r=[b_cumd], w=[b_kA], dk="kA")
            units = []
            npair = 0
            for h in range(8):
                for qb in range(4):
                    nkt = 4 * qb + 4
                    for kt in range(nkt):
                        units.append(dict(h=h, qb=qb, kt=kt, nkt=nkt, pair=npair))
                    npair += 1
            cnt = {"u": 0}

            def stA(u):
                h, qb, kt = u["h"], u["qb"], u["kt"]
                j = kt - 4 * qb
                c0 = max(j, 0) * 128
                i = cnt["u"]
                cnt["u"] += 1
                si = 4 + (i % 4)
                Sb, Sbb = banks[si], bbuf[si]
                pt, bpt = PT[i % NPT], b_PT[i % NPT]
                u["pt"], u["bpt"], u["c0"] = pt, bpt, c0
                mm(Sb[:, c0:512], [(kA[0:70, h, kt * 128:(kt + 1) * 128], qA[0:70, h, qb * 512 + c0:(qb + 1) * 512])], r=[b_kA, b_qA], w=[Sbb])
                if j >= 0:
                    dt_, bdt = dtmp[kt % 2], b_dtmp[kt % 2]
                    tr.op("dve", lambda v: v.tensor_tensor(out=dt_[:], in0=Sb[:, c0:c0 + 128], in1=cf[:, CF_MNEG:CF_MNEG + 128], op=ALU.add), r=[Sbb, b_cf], w=[bdt])
                    tr.op("act", lambda a: a.activation(out=pt[:, c0:c0 + 128], in_=dt_[:], func=AF.Exp), r=[bdt], w=[bpt])
                    if c0 + 128 < 512:
                        tr.op("act", lambda a: a.activation(out=pt[:, c0 + 128:512], in_=Sb[:, c0 + 128:512], func=AF.Exp), r=[Sbb], w=[bpt])
                else:
                    tr.op("act", lambda a: a.activation(out=pt[:, :], in_=Sb[:, :], func=AF.Exp), r=[Sbb], w=[bpt])

            def stB(u):
                h, qb, kt, nkt = u["h"], u["qb"], u["kt"], u["nkt"]
                so = (u["pair"] % 2) * 2
                io = u["pair"] % 2
                Ob, Obb, Db, Dbb = banks[so], bbuf[so], banks[so + 1], bbuf[so + 1]
                pt, bpt, c0 = u["pt"], u["bpt"], u["c0"]

                hp = h // 2
                r0 = (h % 2) * 64

                def fpv(pe):
                    pe.matmul(Ob[:, c0:512], Vf[:, kt, hp * 128:(hp + 1) * 128], pt[:, c0:512], start=(kt == 0), stop=(kt == nkt - 1))
                    return pe.matmul(Db[:, c0:512], ones128, pt[:, c0:512], start=(kt == 0), stop=(kt == nkt - 1))
                tr.op("pe", fpv, r=[bpt, b_Vf, b_cb], w=[Obb, Dbb])
                if kt == nkt - 1:
                    rc, brc = rec[io], b_rec[io]
                    os_, bos = ost[io], b_ost[io]
                    tr.op("dve", lambda v: v.reciprocal(out=rc[r0:r0 + 64, :], in_=Db[r0:r0 + 64, :]), r=[Dbb], w=[brc])
                    tr.op("dve", lambda v: v.tensor_tensor(out=os_[r0:r0 + 64, :], in0=Ob[r0:r0 + 64, :], in1=rc[r0:r0 + 64, :], op=ALU.mult), r=[Obb, brc], w=[bos])
                    tr.op("sp", lambda q: q.dma_start(out=oaT_d[seq, h * 64:(h + 1) * 64, qb * 512:(qb + 1) * 512], in_=os_[r0:r0 + 64, :]), r=[bos], dk="ost%d" % io)

            pipeline(units, [stA, stB], skew=3)
            tr.barrier()

    def phase_B_dsa(seq):
        with contextlib.ExitStack() as ph:
            QQ = sb(ph, "QQ", [128, 8, S], BF16)
            Kz = sb(ph, "Kz", [128, S], BF16)
            Kzi = sb(ph, "Kzi", [128, S], BF16)
            dg = [sb(ph, "dg%d" % i, [128, 8, 128], F32R) for i in range(2)]
            b_dg = [Buf("dg%d" % i) for i in range(2)]
            Vd = sb(ph, "Vd", [128, 16, 128], BF16)
            iwt = sb(ph, "iwt", [128, 16, 8], F32)
            NSC = 7
            NRL = 3
            sc = [sb(ph, "sc%d" % i, [128, S], F32) for i in range(NSC)]
            Mtok = [sb(ph, "Mtok%d" % i, [128, S], BF16) for i in range(4)]
            rl = [sb(ph, "rl%d" % i, [128, 512], F32R) for i in range(NRL)]
            MT = [sb(ph, "MT%d" % i, [128, 16, 512], BF16) for i in range(2)]
            NEP = 5
            E = [sb(ph, "E%d" % i, [128, 512], BF16) for i in range(NEP)]
            PT = [sb(ph, "PTd%d" % i, [128, 512], BF16) for i in range(NEP)]
            MR = [sb(ph, "MR%d" % i, [128, 128], BF16) for i in range(4)]
            osb = [sb(ph, "osb%d" % i, [64, 512], F32) for i in range(2)]
            rec = [sb(ph, "recd%d" % i, [64, 512], F32) for i in range(2)]
            ost = [sb(ph, "ostd%d" % i, [64, 512], BF16) for i in range(2)]
            sm = [sb(ph, "sm%d" % i, [128, 8], F32) for i in range(4)]
            steps = [sb(ph, "steps%d" % i, [128, NITER + 1], F32) for i in range(4)]
            b_dqT, b_iqT, b_dkT, b_ikT, b_Vd, b_iwt = Buf("dqT"), Buf("iqT"), Buf("dkT"), Buf("ikT"), Buf("Vd"), Buf("iwt")
            b_sc = [[Buf("sc%d_%d" % (i, c)) for c in range(4)] for i in range(NSC)]
            b_Mtok = [Buf("Mtok%d" % i) for i in range(4)]
            b_rl = [Buf("rl%d" % i) for i in range(NRL)]
            b_MT = [[Buf("MT%d_%d" % (i, k)) for k in range(4)] for i in range(2)]
            b_E = [Buf("E%d" % i) for i in range(NEP)]
            b_PT = [Buf("PTd%d" % i) for i in range(NEP)]
            b_MR = [Buf("MR%d" % i) for i in range(4)]
            b_osb = [Buf("osb%d" % i) for i in range(2)]
            b_rec = [Buf("recd%d" % i) for i in range(2)]
            b_ost = [Buf("ostd%d" % i) for i in range(2)]
            b_sm = [Buf("sm%d" % i) for i in range(4)]
            b_steps = [Buf("steps%d" % i) for i in range(4)]
            tr.op("sp", lambda q: q.dma_start(out=QQ[0:64, :, :], in_=dqT_d[seq, :, :].rearrange("(h d) s -> d h s", d=64)), w=[b_dqT], dk="qA")
            tr.op("sp", lambda q: q.dma_start(out=QQ[64:128, :, :], in_=iqT_d[seq, :, :].rearrange("(h d) s -> d h s", d=64)), w=[b_iqT], dk="kA")
            tr.op("pool", lambda g: g.memset(Kz[:, :], 0.0), w=[b_dkT])
            tr.op("pool", lambda g: g.memset(Kzi[:, :], 0.0), w=[b_ikT])
            tr.op("pool", lambda g: g.memset(Vd[:, :, :], 0.0), w=[b_Vd])
            tr.op("sp", lambda q: q.dma_start(out=Kz[0:64, :], in_=dkT_d[seq, :, :]), w=[b_dkT], dk="dkT")
            tr.op("sp", lambda q: q.dma_start(out=Kzi[64:128, :], in_=ikT_d[seq, :, :]), w=[b_ikT], dk="ikT")
            tr.op("sp", lambda q: q.dma_start(out=Vd[:, :, 0:64], in_=vd_d[seq, :, :, :]), w=[b_Vd], dk="Vf")
            tr.op("sp", lambda q: q.dma_start(out=iwt[:, :, :], in_=iw_d[seq, :, :, :]), w=[b_iwt], dk="iwt")
            cau = cb[:, CB_CAU:CB_CAU + 128]
            state = {"sb": 0, "db": 0, "acc": 0, "dg": 0, "rl": 0, "pt": 0, "mr": 0, "pair": 0}

            def sbank():
                i = 2 + (state["sb"] % 2)
                state["sb"] += 1
                return banks[i], bbuf[i]

            def dbank():
                i = 4 + (state["db"] % 2)
                state["db"] += 1
                return banks[i], bbuf[i]

            tiles_of = {}

            def scores(qb):
                mt, bmt = MT[qb % 2], b_MT[qb % 2]
                tiles = []
                tiles_of[qb] = tiles
                for qq in range(4):
                    qt = 4 * qb + qq
                    cs = slice(qq * 128, (qq + 1) * 128)
                    if qt < 2:
                        if qt == 1:
                            tr.op("pool", lambda g, cs=cs: g.memset(mt[:, 0, cs], 1.0), w=[bmt[qq]])
                        tr.op("pool", lambda g, cs=cs, qt=qt: g.tensor_copy(out=mt[:, qt, cs], in_=cau), r=[b_cb], w=[bmt[qq]])
                        continue
                    n = (qt + 1) * 128
                    nch = (n + 511) // 512
                    tiles.append(dict(qt=qt, qq=qq, cs=cs, n=n, nch=nch, s_=sc[qt % NSC], bsl=b_sc[qt % NSC][0:nch],
                                      smt=sm[qq], bsm=b_sm[qq], stp=steps[qq], bstp=b_steps[qq], mk=Mtok[qq], bmk=b_Mtok[qq]))
                units = []
                for T in tiles:
                    for c4 in range(T["nch"]):
                        for h in range(8):
                            units.append(dict(T=T, h=h, c4=c4, k0=c4 * 512, wdt=min(512, T["n"] - c4 * 512)))

                def s1(u):
                    T, h, k0, wdt = u["T"], u["h"], u["k0"], u["wdt"]
                    qt = T["qt"]
                    if h == 0 and u["c4"] == 0:
                        di = state["dg"] % 2
                        state["dg"] += 1
                        T["di"] = di

                        def fdg(g):
                            ins = None
                            for hh in range(8):
                                ins = g.tensor_scalar(out=dg[di][:, hh, :], in0=cf[:, CF_IDF:CF_IDF + 128], scalar1=iwt[:, qt, hh:hh + 1], scalar2=0.0, op0=ALU.mult, op1=ALU.add)
                            return ins
                        tr.op("pool", fdg, r=[b_cf, b_iwt], w=[b_dg[di]])
                    bk, bb = dbank()
                    mm(bk[:, 0:wdt], [(QQ[:, h, qt * 128:(qt + 1) * 128], Kzi[:, k0:k0 + wdt])], r=[b_iqT, b_dqT, b_ikT], w=[bb])
                    j = state["rl"] % NRL
                    state["rl"] += 1
                    u["j"] = j
                    tr.op("act", lambda a: a.activation(out=rl[j][:, 0:wdt], in_=bk[:, 0:wdt], func=AF.Relu), r=[bb], w=[b_rl[j]])

                def s2(u):
                    T, h, k0, wdt, j = u["T"], u["h"], u["k0"], u["wdt"], u["j"]
                    if h == 0:
                        ai = 6 + (state["acc"] % 2)
                        state["acc"] += 1
                        T["ai"] = ai
                    ai = T["ai"]
                    di = T["di"]
                    tr.op("pe", lambda pe: pe.matmul(banks[ai][:, 0:wdt], dg[di][:, h, :], rl[j][:, 0:wdt], start=(h == 0), stop=(h == 7)), r=[b_rl[j], b_dg[di]], w=[bbuf[ai]])
                    if h == 7:
                        s_ = T["s_"]
                        tr.op("act", lambda a: a.activation(out=s_[:, k0:k0 + wdt], in_=banks[ai][:, 0:wdt], func=AF.Copy), r=[bbuf[ai]], w=[T["bsl"][u["c4"]]])

                pipeline(units, [s1, s2], skew=1)

            def bisect(qb):
                tiles = tiles_of[qb]
                for T in tiles:
                    tr.op("dve", lambda v, T=T: v.tensor_reduce(out=T["smt"][:, 0:1], in_=T["s_"][:, 0:T["n"]], axis=AX.X, op=ALU.min), r=T["bsl"], w=[T["bsm"]])
                for T in tiles:
                    tr.op("dve", lambda v, T=T: v.tensor_reduce(out=T["smt"][:, 1:2], in_=T["s_"][:, 0:T["n"]], axis=AX.X, op=ALU.max), r=T["bsl"], w=[T["bsm"]])
                for T in tiles:
                    tr.op("dve", lambda v, T=T: v.tensor_tensor(out=T["s_"][:, T["n"] - 128:T["n"]], in0=T["s_"][:, T["n"] - 128:T["n"]], in1=cf[:, CF_MNEGT:CF_MNEGT + 128], op=ALU.add), r=[b_cf], w=[T["bsl"][-1]])
                for T in tiles:
                    tr.op("dve", lambda v, T=T: v.tensor_scalar(out=T["smt"][:, 2:3], in0=T["smt"][:, 1:2], scalar1=T["smt"][:, 0:1], scalar2=1.00390625, op0=ALU.subtract, op1=ALU.mult), w=[T["bsm"]])
                for T in tiles:
                    tr.op("dve", lambda v, T=T: v.tensor_scalar(out=T["stp"][:, :], in0=cf[:, CF_POW:CF_POW + NITER + 1], scalar1=T["smt"][:, 2:3], scalar2=0.5, op0=ALU.mult, op1=ALU.mult), r=[T["bsm"], b_cf], w=[T["bstp"]])
                for T in tiles:
                    tr.op("dve", lambda v, T=T: v.tensor_tensor(out=T["smt"][:, 3:4], in0=T["smt"][:, 0:1], in1=T["stp"][:, 0:1], op=ALU.add), r=[T["bstp"]], w=[T["bsm"]])
                for k in range(NITER):
                    for T in tiles:
                        tr.op("dve", lambda v, T=T: v.tensor_scalar(out=T["mk"][:, 0:T["n"]], in0=T["s_"][:, 0:T["n"]], scalar1=T["smt"][:, 3:4], scalar2=None, op0=ALU.is_ge, op1=ALU.add, accum_out=T["smt"][:, 4:5]),
                              r=T["bsl"], w=[T["bsm"], T["bmk"]])
                    for T in tiles:
                        tr.op("dve", lambda v, T=T: v.tensor_scalar(out=T["smt"][:, 5:6], in0=T["smt"][:, 4:5], scalar1=TOPK - 0.5, scalar2=0.5, op0=ALU.is_ge, op1=ALU.subtract), w=[T["bsm"]])
                    for T in tiles:
                        tr.op("dve", lambda v, T=T, k=k: v.scalar_tensor_tensor(out=T["smt"][:, 3:4], in0=T["smt"][:, 5:6], scalar=T["stp"][:, k:k + 1], in1=T["smt"][:, 3:4], op0=ALU.mult, op1=ALU.add), r=[T["bstp"]], w=[T["bsm"]])
                for T in tiles:
                    tr.op("dve", lambda v, T=T: v.tensor_tensor(out=T["smt"][:, 6:7], in0=T["smt"][:, 3:4], in1=T["stp"][:, NITER:NITER + 1], op=ALU.subtract), r=[T["bstp"]], w=[T["bsm"]])
                for T in tiles:
                    tr.op("dve", lambda v, T=T: v.tensor_scalar(out=T["mk"][:, 0:T["n"]], in0=T["s_"][:, 0:T["n"]], scalar1=T["smt"][:, 6:7], scalar2=None, op0=ALU.is_ge), r=T["bsl"] + [T["bsm"]], w=[T["bmk"]])

            def transp(qb):
                mt, bmt = MT[qb % 2], b_MT[qb % 2]
                for T in tiles_of[qb]:
                    qt, mk, cs, qq = T["qt"], T["mk"], T["cs"], T["qq"]
                    for g0 in range(0, qt + 1, 8):
                        g1 = min(qt + 1, g0 + 8)
                        bk, bb = dbank()
                        ptb = bk[:].bitcast(BF16)

                        def ftr(pe, g0=g0, g1=g1, ptb=ptb, mk=mk):
                            ins = None
                            for kt in range(g0, g1):
                                ins = pe.transpose(out=ptb[:, (kt - g0) * 128:(kt - g0 + 1) * 128], in_=mk[:, kt * 128:(kt + 1) * 128], identity=ident)
                            return ins
                        tr.op("pe", ftr, r=[T["bmk"], b_cb], w=[bb])
                        tr.op("act", lambda a, g0=g0, g1=g1, ptb=ptb, cs=cs: a.activation(out=mt[:, g0:g1, cs], in_=ptb[:, 0:(g1 - g0) * 128].rearrange("p (g t) -> p g t", t=128), func=AF.Copy),
                              r=[bb], w=[bmt[qq]])

            def heads(qb):
                mt, bmt = MT[qb % 2], b_MT[qb % 2]
                nkt = 4 * qb + 4
                units = []
                for h in range(8):
                    for kt in range(nkt):
                        units.append(dict(h=h, kt=kt, pair=state["pair"]))
                    state["pair"] += 1

                def hA(u):
                    h, kt = u["h"], u["kt"]
                    j = kt - 4 * qb
                    c0 = max(j, 0) * 128
                    si_ = (2, 3, 6, 7)[state["sb"] % 4]
                    state["sb"] += 1
                    Sb, Sbb = banks[si_], bbuf[si_]
                    ip = state["pt"] % NEP
                    state["pt"] += 1
                    u["ip"], u["c0"], u["j"] = ip, c0, j
                    e_, be = E[ip], b_E[ip]
                    mm(Sb[:, c0:512], [(Kz[:, kt * 128:(kt + 1) * 128], QQ[:, h, qb * 512 + c0:(qb + 1) * 512])], r=[b_dkT, b_dqT, b_iqT], w=[Sbb])
                    tr.op("act", lambda a: a.activation(out=e_[:, c0:512], in_=Sb[:, c0:512], func=AF.Exp, bias=cf[:, CF_RB31 + h:CF_RB31 + h + 1]), r=[Sbb, b_cf], w=[be])

                def hB(u):
                    h, kt, ip, c0, j = u["h"], u["kt"], u["ip"], u["c0"], u["j"]
                    e_, be = E[ip], b_E[ip]
                    pt, bpt = PT[ip], b_PT[ip]
                    near = []
                    if j >= 0:
                        near.append((c0, 0))
                        if c0 + 128 < 512:
                            near.append((c0 + 128, 128))
                    elif j == -1:
                        near.append((0, 128))
                    cfar = (near[-1][0] + 128) if near else 0
                    rds = [bmt[c // 128] for c in range(c0, 512, 128)]
                    for (cn, ro) in near:
                        im = state["mr"] % 4
                        state["mr"] += 1
                        tr.op("pool", lambda g, im=im, cn=cn, ro=ro: g.tensor_tensor(out=MR[im][:], in0=mt[:, kt, cn:cn + 128], in1=Rt[:, h, ro:ro + 128], op=ALU.mult), r=[bmt[cn // 128], b_Rt], w=[b_MR[im]])
                        tr.op("pool", lambda g, im=im, cn=cn: g.tensor_tensor(out=pt[:, cn:cn + 128], in0=e_[:, cn:cn + 128], in1=MR[im][:], op=ALU.mult), r=[be, b_MR[im]], w=[bpt])
                    if cfar < 512:
                        tr.op("dve" if qb == 3 else "pool", lambda g: g.tensor_tensor(out=pt[:, cfar:512], in0=e_[:, cfar:512], in1=mt[:, kt, cfar:512], op=ALU.mult), r=[be] + rds, w=[bpt])

                def hC(u):
                    h, kt, ip, c0 = u["h"], u["kt"], u["ip"], u["c0"]
                    pt, bpt = PT[ip], b_PT[ip]
                    so = (u["pair"] % 2) * 4
                    io = u["pair"] % 2
                    Ob, Obb, Db, Dbb = banks[so], bbuf[so], banks[so + 1], bbuf[so + 1]

                    def fpv(pe):
                        pe.matmul(Ob[:, c0:512], Vd[:, kt, :], pt[:, c0:512], start=(kt == 0), stop=(kt == nkt - 1))
                        return pe.matmul(Db[:, c0:512], ones128, pt[:, c0:512], start=(kt == 0), stop=(kt == nkt - 1))
                    tr.op("pe", fpv, r=[bpt, b_Vd, b_cb], w=[Obb, Dbb])
                    if kt == nkt - 1:
                        tr.op("act", lambda a: a.activation(out=rec[io][:], in_=Db[0:64, :], func=AF.Ln), r=[Dbb], w=[b_rec[io]])
                        tr.op("act", lambda a: a.activation(out=osb[io][:], in_=Ob[0:64, :], func=AF.Copy), r=[Obb], w=[b_osb[io]])
                        tr.op("act", lambda a: a.activation(out=rec[io][:], in_=rec[io][:], func=AF.Exp, scale=-1.0), w=[b_rec[io]])
                        tr.op("pool", lambda g: g.tensor_tensor(out=ost[io][:], in0=osb[io][:], in1=rec[io][:], op=ALU.mult), r=[b_osb[io], b_rec[io]], w=[b_ost[io]])
                        tr.op("sp", lambda q: q.dma_start(out=obT_d[seq, h * 64:(h + 1) * 64, qb * 512:(qb + 1) * 512], in_=ost[io][:]), r=[b_ost[io]], dk="ostd%d" % io)

                pipeline(units, [hA, hB, hC], skew=2)

            scores(0)
            bisect(0)
            scores(1)
            transp(0)
            bisect(1)
            heads(0)
            scores(2)
            transp(1)
            bisect(2)
            heads(1)
            scores(3)
            transp(2)
            bisect(3)
            heads(2)
            transp(3)
            heads(3)
            tr.barrier()

    for seq in range(NSEQ):
        with contextlib.ExitStack() as ph:
            d = alloc_dense(ph)
            st = alloc_stage(ph)
            for blk in (2 * seq, 2 * seq + 1):
                phase_A(d, st, blk)
            tr.barrier()
        if stage < 3:
            continue
        if stage >= 4:
            phase_B_fox(seq)
        if stage >= 5:
            phase_B_dsa(seq)
        if stage in (4, 5):
            continue
        with contextlib.ExitStack() as ph:
            d = alloc_dense(ph)
            st = alloc_stage_C(ph)
            for blk in (2 * seq, 2 * seq + 1):
                phase_C(d, st, blk)
            tr.barrier()
    tr.barrier()
    es.close()
    return nc


def _t5_bucket_table(n):
    rel = np.arange(n, dtype=np.int32)
    relf = np.maximum(rel, 1).astype(np.float32)
    large = 16 + (np.log(relf / np.float32(16)) / np.float32(np.log(128 / 16)) * np.float32(16)).astype(np.int32)
    large = np.minimum(large, 31)
    return np.where(rel < 16, rel, large)


def _consts(inp):
    cf = np.zeros((128, NCF), np.float32)
    for c0, k in ((CF_G1, "ffn1_norm"), (CF_GM, "mix_norm"), (CF_G2, "ffn2_norm")):
        cf[:, c0:c0 + 8] = np.asarray(inp[k], np.float32).reshape(8, 128).T
    cf[:, CF_GQ] = np.tile(np.asarray(inp["fox_q_norm"], np.float32), 2)
    cf[:, CF_GK] = np.tile(np.asarray(inp["fox_k_norm"], np.float32), 2)
    cf[:, CF_GDQ] = np.tile(np.asarray(inp["dsa_q_norm"], np.float32), 2)
    cf[:, CF_GDK] = np.tile(np.asarray(inp["dsa_k_norm"], np.float32), 2)
    cf[0:8, CF_BF] = np.asarray(inp["b_forget"], np.float32)
    rb = np.asarray(inp["rel_bias"], np.float32)
    cf[:, CF_RB31:CF_RB31 + 8] = rb[31][None, :]
    cf[:, CF_POW:CF_POW + NITER + 1] = (0.5 ** np.arange(NITER + 1, dtype=np.float64))[None, :].astype(np.float32)
    s = np.arange(128)
    cf[:, CF_MNEG:CF_MNEG + 128] = np.where(s[:, None] <= s[None, :], 0.0, NEG)
    cf[:, CF_MNEGT:CF_MNEGT + 128] = np.where(s[None, :] <= s[:, None], 0.0, NEG)
    tp = np.arange(256)
    delta = np.maximum(tp[None, :] - s[:, None], 0)
    bt = _t5_bucket_table(512)[delta]
    g = rb[bt]
    cf[:, CF_GT:CF_GT + 2048] = np.transpose(g, (0, 2, 1)).reshape(128, 2048)
    cf[:, CF_IDF:CF_IDF + 128] = np.eye(128)
    cb = np.zeros((128, NCB), np.float32)
    cb[:, CB_ID:CB_ID + 128] = np.eye(128)
    cb[:, CB_BLK:CB_BLK + 128] = (s[:, None] // 64 == s[None, :] // 64)
    cb[:, CB_ONE:CB_ONE + 64] = 1.0
    cb[:, CB_ONE2:CB_ONE2 + 128] = 1.0
    cb[:, CB_CAU:CB_CAU + 128] = (s[:, None] <= s[None, :])
    return cf, cb.astype(ml_dtypes.bfloat16)


def _perm_win(w_in):
    w = np.asarray(w_in, np.float32)
    o = {}
    off = 0
    for name, wd in (("fq", 512), ("fk", 512), ("fv", 512), ("ff", 8), ("dq", 512), ("dk", 64), ("dv", 64),
                     ("iq", 512), ("ik", 64), ("iw", 8), ("ga", 1024), ("gb", 1024)):
        o[name] = w[:, off:off + wd]
        off += wd
    return np.ascontiguousarray(np.concatenate([o[k] for k in ("fq", "fk", "dq", "iq", "fv", "dk", "ik", "ff", "dv", "iw", "ga", "gb")], axis=1))


def make_in_maps(inp, n_cores=8):
    f = lambda k: np.ascontiguousarray(np.asarray(inp[k], np.float32))
    cf, cb = _consts(inp)
    slotmajor = lambda k: np.ascontiguousarray(np.asarray(inp[k], np.float32).reshape(D, NFC // 2, 256).transpose(1, 0, 2))
    shared = {
        "f1g": slotmajor("ffn1_w_gate"), "f1u": slotmajor("ffn1_w_up"), "f1d": f("ffn1_w_down"),
        "win": _perm_win(inp["w_in"]), "wa": f("w_branch_a"), "wb": f("w_branch_b"), "wo": f("w_out"),
        "f2g": slotmajor("ffn2_w_gate"), "f2u": slotmajor("ffn2_w_up"), "f2d": f("ffn2_w_down"),
        "cf": cf, "cb": cb,
    }
    x = np.asarray(inp["x"], np.float32)
    maps = []
    for c in range(n_cores):
        m = dict(shared)
        m["x"] = np.ascontiguousarray(x[NSEQ * c:NSEQ * (c + 1)].reshape(NSEQ * S, D))
        maps.append(m)
    return maps


def kernel(**inputs):
    nc = bass.Bass("TRN2", target_bir_lowering=False)
    build(nc)
    maps = make_in_maps(inputs, 8)
    res = run_bass_kernel_spmd(nc, maps, core_ids=list(range(8)))
    outs = [np.asarray(r["out"], np.float32).reshape(NSEQ, S, D) for r in res.results]
    return np.concatenate(outs, axis=0)
```
